# Optimizing a Trainium2 kernel written in Bass

```python
import math
import jax, jax.numpy as jnp
from jax import lax
import numpy as np

D_MODEL = 1024
BATCH = 8
SEQ = 4096
DEPTH = 4

MIX_WIDTH = D_MODEL
A_HEADS = 8
A_KV_HEADS = 2
A_GROUP = A_HEADS // A_KV_HEADS
B_HEADS = 8
HEAD_DIM = MIX_WIDTH // (A_HEADS + B_HEADS)
A_WIDTH = A_HEADS * HEAD_DIM
A_KV_WIDTH = A_KV_HEADS * HEAD_DIM
B_WIDTH = B_HEADS * HEAD_DIM
PROJ_WIDTH = A_WIDTH + 2 * A_KV_WIDTH + 3 * B_WIDTH
WINDOW = 128
BLOCK = 128
GRID_W = 64
NA_ROWS_MAX = 8
NA_COLS = 16
T5_BUCKETS = 32
T5_MAX_DIST = 128
PEER_HEADS = 8
PEER_N_KEYS = 128
PEER_N_EXPERTS = PEER_N_KEYS * PEER_N_KEYS
PEER_TOPK = 16
PEER_QDIM = 256
PEER_CHUNK = 128
ALPHA = (2 * DEPTH) ** 0.25
BETA = (8 * DEPTH) ** -0.25
LN_EPS = 1e-5
NEG = -1e30

kernel_name = 'hymba_window_natten_peer_deepnorm'


def layer_norm(x, g, b):
    xf = x.astype(jnp.float32)
    mu = xf.mean(-1, keepdims=True)
    var = jnp.square(xf - mu).mean(-1, keepdims=True)
    return ((xf - mu) * lax.rsqrt(var + LN_EPS) * g + b).astype(x.dtype)


def rms_norm(x, g):
    xf = x.astype(jnp.float32)
    return (xf * lax.rsqrt(jnp.square(xf).mean(-1, keepdims=True) + LN_EPS) * g).astype(x.dtype)


def t5_bucket(rel):
    nb = T5_BUCKETS // 2
    max_exact = nb // 2
    ret = (rel > 0).astype(jnp.int32) * nb
    n = jnp.abs(rel).astype(jnp.int32)
    nf = jnp.maximum(n, 1).astype(jnp.float32)
    large = max_exact + (jnp.log(nf / max_exact) / math.log(T5_MAX_DIST / max_exact)
                         * (nb - max_exact)).astype(jnp.int32)
    large = jnp.minimum(large, nb - 1)
    return ret + jnp.where(n < max_exact, n, large)


def window_attention(q, k, v, sink, t5_table):
    b, t = q.shape[0], q.shape[1]
    nb = t // BLOCK
    pad = ((0, 0), (BLOCK, BLOCK), (0, 0), (0, 0))
    kp = jnp.pad(k, pad).reshape(b, nb + 2, BLOCK, A_KV_HEADS, HEAD_DIM)
    vp = jnp.pad(v, pad).reshape(b, nb + 2, BLOCK, A_KV_HEADS, HEAD_DIM)
    kw = jnp.concatenate([kp[:, :-2], kp[:, 1:-1], kp[:, 2:]], axis=2)
    vw = jnp.concatenate([vp[:, :-2], vp[:, 1:-1], vp[:, 2:]], axis=2)
    qb = q.reshape(b, nb, BLOCK, A_KV_HEADS, A_GROUP, HEAD_DIM)
    s = jnp.einsum('bnqkgd,bnskd->bnkgqs', qb, kw).astype(jnp.float32) * (HEAD_DIM ** -0.5)
    rel = (jnp.arange(3 * BLOCK)[None, :] - BLOCK) - jnp.arange(BLOCK)[:, None]
    bias = t5_table[t5_bucket(rel)].astype(jnp.float32)
    bias = bias.transpose(2, 0, 1).reshape(A_KV_HEADS, A_GROUP, BLOCK, 3 * BLOCK)
    key_pos = jnp.arange(nb)[:, None] * BLOCK + jnp.arange(3 * BLOCK)[None, :] - BLOCK
    in_range = (key_pos >= 0) & (key_pos < t)
    valid = (jnp.abs(rel) <= WINDOW)[None] & in_range[:, None, :]
    s = jnp.where(valid[None, :, None, None], s + bias, NEG)
    sk = sink.astype(jnp.float32).reshape(A_KV_HEADS, A_GROUP)[None, None, :, :, None, None]
    m = jnp.maximum(s.max(-1, keepdims=True), sk)
    e = jnp.exp(s - m)
    p = e / (e.sum(-1, keepdims=True) + jnp.exp(sk - m))
    o = jnp.einsum('bnkgqs,bnskd->bnqkgd', p.astype(v.dtype), vw)
    return o.reshape(b, t, A_WIDTH)


def neighbourhood_attention(q, k, v, rpb):
    b, t = q.shape[0], q.shape[1]
    rows = t // GRID_W
    kh = min(NA_ROWS_MAX, rows)
    qg = q.reshape(b, rows, GRID_W, B_HEADS, HEAD_DIM)
    kg = k.reshape(b, rows, GRID_W, B_HEADS, HEAD_DIM)
    vg = v.reshape(b, rows, GRID_W, B_HEADS, HEAD_DIM)
    r = jnp.arange(rows)
    row_start = jnp.clip(r - kh // 2, 0, rows - kh)
    key_rows = row_start[:, None] + jnp.arange(kh)[None, :]
    kr = kg[:, key_rows]
    vr = vg[:, key_rows]
    s = jnp.einsum('brqhd,brikhd->brhqik', qg, kr).astype(jnp.float32) * (HEAD_DIM ** -0.5)
    c = jnp.arange(GRID_W)
    col_start = jnp.clip(c - NA_COLS // 2, 0, GRID_W - NA_COLS)
    col_mask = (c[None, :] >= col_start[:, None]) & (c[None, :] < col_start[:, None] + NA_COLS)
    dr = key_rows - r[:, None] + (NA_ROWS_MAX - 1)
    dc = jnp.clip(c[None, :] - c[:, None], -(NA_COLS - 1), NA_COLS - 1) + (NA_COLS - 1)
    bias = rpb[:, dr[:, None, :, None], dc[None, :, None, :]].astype(jnp.float32)
    bias = bias.transpose(1, 0, 2, 3, 4)
    s = jnp.where(col_mask[:, None, :], s + bias[None], NEG)
    p = jax.nn.softmax(s.reshape(b, rows, B_HEADS, GRID_W, kh * GRID_W), axis=-1)
    p = p.reshape(b, rows, B_HEADS, GRID_W, kh, GRID_W).astype(v.dtype)
    o = jnp.einsum('brhqik,brikhd->brqhd', p, vr)
    return o.reshape(b, t, B_WIDTH)


def peer(x, w_q, sub_keys, u, v):
    b, t, d = x.shape
    n = b * t
    xf = x.reshape(n, d)
    q = (xf @ w_q).reshape(n, PEER_HEADS, 2, PEER_QDIM // 2)
    sc = jnp.einsum('nhpc,hpkc->nhpk', q, sub_keys).astype(jnp.float32)
    s_top, i_top = lax.top_k(sc, PEER_TOPK)
    cand = (s_top[:, :, 0, :, None] + s_top[:, :, 1, None, :]).reshape(n, PEER_HEADS, PEER_TOPK * PEER_TOPK)
    cand_idx = (i_top[:, :, 0, :, None] * PEER_N_KEYS + i_top[:, :, 1, None, :]).reshape(
        n, PEER_HEADS, PEER_TOPK * PEER_TOPK)
    fs, fpos = lax.top_k(cand, PEER_TOPK)
    eidx = jnp.take_along_axis(cand_idx, fpos, axis=-1)
    g = jax.nn.softmax(fs, axis=-1)
    nc = n // PEER_CHUNK

    def chunk_fn(args):
        xc, ec, gc = args
        h = jnp.einsum('cd,chkd->chk', xc, u[ec])
        a = (jax.nn.gelu(h.astype(jnp.float32), approximate=False) * gc).astype(xc.dtype)
        return jnp.einsum('chk,chkd->cd', a, v[ec])

    out = lax.map(chunk_fn, (xf.reshape(nc, PEER_CHUNK, d),
                             eidx.reshape(nc, PEER_CHUNK, PEER_HEADS, PEER_TOPK),
                             g.reshape(nc, PEER_CHUNK, PEER_HEADS, PEER_TOPK)))
    return out.reshape(b, t, d)


def setup_inputs(seed: int = 0) -> dict:
    key = jax.random.key(seed)
    ks = jax.random.split(key, 16)
    f32 = jnp.float32
    x = jax.random.normal(ks[0], (BATCH, SEQ, D_MODEL), f32)
    va0 = A_WIDTH + A_KV_WIDTH
    vb0 = A_WIDTH + 2 * A_KV_WIDTH + 2 * B_WIDTH
    col_scale = jnp.ones((PROJ_WIDTH,), f32).at[va0:va0 + A_KV_WIDTH].set(BETA).at[vb0:].set(BETA)
    w_in = jax.random.normal(ks[1], (DEPTH, D_MODEL, PROJ_WIDTH), f32) * (D_MODEL ** -0.5) * col_scale
    w_o = jax.random.normal(ks[2], (DEPTH, MIX_WIDTH, D_MODEL), f32) * (MIX_WIDTH ** -0.5) * BETA
    attn_sink = jax.random.normal(ks[3], (DEPTH, A_HEADS), f32) * 0.5
    na_rpb = jax.random.normal(ks[4], (DEPTH, B_HEADS, 2 * NA_ROWS_MAX - 1, 2 * NA_COLS - 1), f32) * 0.1
    t5_table = jax.random.normal(ks[5], (T5_BUCKETS, A_HEADS), f32) * 0.1
    gnorm_a = 1.0 + 0.02 * jax.random.normal(ks[6], (DEPTH, A_WIDTH), f32)
    gnorm_b = 1.0 + 0.02 * jax.random.normal(ks[7], (DEPTH, B_WIDTH), f32)
    ln1_g = 1.0 + 0.02 * jax.random.normal(ks[8], (DEPTH, D_MODEL), f32)
    ln1_b = 0.02 * jax.random.normal(ks[9], (DEPTH, D_MODEL), f32)
    ln2_g = 1.0 + 0.02 * jax.random.normal(ks[10], (DEPTH, D_MODEL), f32)
    ln2_b = 0.02 * jax.random.normal(ks[11], (DEPTH, D_MODEL), f32)
    peer_wq = jax.random.normal(ks[12], (DEPTH, D_MODEL, PEER_HEADS * PEER_QDIM), f32) * (D_MODEL ** -0.5)
    peer_keys = jax.random.normal(ks[13], (DEPTH, PEER_HEADS, 2, PEER_N_KEYS, PEER_QDIM // 2), f32) * (
        (PEER_QDIM // 2) ** -0.5)
    peer_u = jax.random.normal(ks[14], (DEPTH, PEER_N_EXPERTS, D_MODEL), f32) * (D_MODEL ** -0.5)
    peer_v = jax.random.normal(ks[15], (DEPTH, PEER_N_EXPERTS, D_MODEL), f32) * BETA * (PEER_HEADS ** -0.5)
    return {'x': x, 'w_in': w_in, 'w_o': w_o, 'attn_sink': attn_sink, 'na_rpb': na_rpb,
            't5_table': t5_table, 'gnorm_a': gnorm_a, 'gnorm_b': gnorm_b,
            'ln1_g': ln1_g, 'ln1_b': ln1_b, 'ln2_g': ln2_g, 'ln2_b': ln2_b,
            'peer_wq': peer_wq, 'peer_keys': peer_keys, 'peer_u': peer_u, 'peer_v': peer_v}


def reference(x, w_in, w_o, attn_sink, na_rpb, t5_table, gnorm_a, gnorm_b,
              ln1_g, ln1_b, ln2_g, ln2_b, peer_wq, peer_keys, peer_u, peer_v):
    b, t, _ = x.shape
    o1 = A_WIDTH
    o2 = o1 + A_KV_WIDTH
    o3 = o2 + A_KV_WIDTH
    o4 = o3 + B_WIDTH
    o5 = o4 + B_WIDTH
    for l in range(DEPTH):
        h = x @ w_in[l]
        qa = h[..., :o1].reshape(b, t, A_HEADS, HEAD_DIM)
        ka = h[..., o1:o2].reshape(b, t, A_KV_HEADS, HEAD_DIM)
        va = h[..., o2:o3].reshape(b, t, A_KV_HEADS, HEAD_DIM)
        qb = h[..., o3:o4].reshape(b, t, B_HEADS, HEAD_DIM)
        kb = h[..., o4:o5].reshape(b, t, B_HEADS, HEAD_DIM)
        vb = h[..., o5:].reshape(b, t, B_HEADS, HEAD_DIM)
        ya = window_attention(qa, ka, va, attn_sink[l], t5_table)
        yb = neighbourhood_attention(qb, kb, vb, na_rpb[l])
        mixed = jnp.concatenate([rms_norm(ya, gnorm_a[l]), rms_norm(yb, gnorm_b[l])], axis=-1) @ w_o[l]
        x = layer_norm(ALPHA * x + mixed, ln1_g[l], ln1_b[l])
        x = layer_norm(ALPHA * x + peer(x, peer_wq[l], peer_keys[l], peer_u[l], peer_v[l]), ln2_g[l], ln2_b[l])
    return x
```

```python
import math
from contextlib import ExitStack

import numpy as np
import concourse.bass as bass
import concourse.mybir as mybir
from concourse.bass_utils import run_bass_kernel_spmd

F32 = mybir.dt.float32
BF16 = mybir.dt.bfloat16
U32 = mybir.dt.uint32
I32 = mybir.dt.int32
ALU = mybir.AluOpType
AF = mybir.ActivationFunctionType
AX = mybir.AxisListType

D = 1024
DEPTH = 4
SEQ = 4096
NCORES = 8
ALPHA = (2 * DEPTH) ** 0.25
LN_EPS = 1e-5
PROJX = 2432
NEGM = -30000.0
RING = 8
ENGS = ['sync', 'scalar', 'vector', 'gpsimd', 'tensor']


class SemPool:
    def __init__(self, nc, stack, n_dma=40):
        self.eng_sem = {e: stack.enter_context(nc.semaphore("e_" + e)) for e in ENGS if e != 'sync'}
        self.eng_cnt = {e: 0 for e in self.eng_sem}
        self.dma_sems = [stack.enter_context(nc.semaphore("d%d" % i)) for i in range(n_dma)]
        self.dma_cnt = [0] * n_dma


class _Op:
    __slots__ = ('eng', 'fn', 'deps', 'dma', 'ev', 'waits', 'clock')


class Prog:
    def __init__(self, nc, pool):
        self.nc = nc
        self.pool = pool
        self.ops = []
        self.last_write = {}
        self.readers = {}
        self.slot_idx = {}

    def add(self, eng, fn, reads=(), writes=(), dma=None):
        op = _Op()
        op.eng, op.fn, op.dma = eng, fn, dma
        deps = set()
        for r in reads:
            w = self.last_write.get(r)
            if w is not None:
                deps.add(w)
        for r in writes:
            w = self.last_write.get(r)
            if w is not None:
                deps.add(w)
            for x in self.readers.get(r, ()):
                deps.add(x)
        idx = len(self.ops)
        op.deps = deps
        for r in reads:
            self.readers.setdefault(r, []).append(idx)
        for r in writes:
            self.last_write[r] = idx
            self.readers[r] = []
        self.ops.append(op)
        return idx

    def emit(self):
        nc, pool, ops = self.nc, self.pool, self.ops
        for op in ops:
            if op.dma is None:
                pool.eng_cnt[op.eng] += 1
                op.ev = (('E', op.eng), pool.eng_cnt[op.eng])
            else:
                if op.dma not in self.slot_idx:
                    self.slot_idx[op.dma] = len(self.slot_idx)
                    assert len(self.slot_idx) <= len(pool.dma_sems), "too many dma slots"
                si = self.slot_idx[op.dma]
                pool.dma_cnt[si] += 16
                op.ev = (('D', si), pool.dma_cnt[si])
        clock = {e: {} for e in ENGS}
        for op in ops:
            clk = clock[op.eng]
            waits = {}
            for d in sorted(op.deps, reverse=True):
                A = ops[d]
                if A.eng == 'tensor' and op.eng == 'tensor' and A.dma is None and op.dma is None:
                    continue
                k, v = A.ev
                if clk.get(k, 0) >= v:
                    continue
                waits[k] = max(waits.get(k, 0), v)
                for kk, vv in A.clock.items():
                    if clk.get(kk, 0) < vv:
                        clk[kk] = vv
            op.waits = list(waits.items())
            c = dict(clk)
            c[op.ev[0]] = op.ev[1]
            op.clock = c
        final = {}
        for op in ops:
            final[op.ev[0]] = max(final.get(op.ev[0], 0), op.ev[1])

        def sem_of(k):
            return pool.eng_sem[k[1]] if k[0] == 'E' else pool.dma_sems[k[1]]

        by_eng = {e: [op for op in ops if op.eng == e] for e in ENGS}

        def body(ename):
            def f(eng):
                for op in by_eng[ename]:
                    for k, v in op.waits:
                        eng.wait_ge(sem_of(k), v)
                    ins = op.fn(eng)
                    ins.then_inc(sem_of(op.ev[0]), 16 if op.dma is not None else 1)
                clk = clock[ename]
                for k, v in final.items():
                    if clk.get(k, 0) < v:
                        eng.wait_ge(sem_of(k), v)
            return f

        with nc.Block() as block:
            block.sync(body('sync'))
            block.scalar(body('scalar'))
            block.vector(body('vector'))
            block.gpsimd(body('gpsimd'))
            block.tensor(body('tensor'))
        n = len(ops)
        nw = sum(len(op.waits) for op in ops)
        for op in ops:
            op.clock = None
        return n, nw


def mkap(t, off, dims):
    pst = t[:].ap[0][0]
    return bass.AP(t, off, [[pst, 128]] + [list(d) for d in dims])


def _t5_bucket(rel):
    nb = 16
    max_exact = 8
    ret = (rel > 0).astype(np.int32) * nb
    n = np.abs(rel).astype(np.int32)
    nf = np.maximum(n, 1).astype(np.float32)
    large = max_exact + (np.log(nf / np.float32(max_exact)) / np.float32(math.log(128 / max_exact))
                         * np.float32(nb - max_exact)).astype(np.int32)
    large = np.minimum(large, nb - 1)
    return ret + np.where(n < max_exact, n, large)


def _biasA_table(t5_table):
    kp = np.arange(128)[:, None, None]
    c = np.arange(3)[None, :, None]
    q = np.arange(128)[None, None, :]
    rel = (c - 1) * 128 + kp - q
    valid = np.abs(rel) <= 128
    b = t5_table[_t5_bucket(rel)]
    b = np.where(valid[..., None], b, np.float32(NEGM)).astype(np.float32)
    return np.ascontiguousarray(b.transpose(3, 0, 1, 2)).reshape(8, 128, 384)


def _na_geometry(T):
    rows = T // 64
    kh = min(8, rows)
    nt = T // 128

    def rs(r):
        return min(max(r - kh // 2, 0), rows - kh)

    def cs(c):
        return min(max(c - 8, 0), 64 - 16)

    kl = np.arange(128)
    krl, kc = kl // 64, kl % 64
    masks = []
    mask_ids = {}
    geo = []
    csq = np.array([cs(c) for c in range(64)])
    for t in range(nt):
        lo = min(rs(2 * t), rs(2 * t + 1))
        hi = max(rs(2 * t), rs(2 * t + 1)) + kh - 1
        lst = []
        for kt in range(lo // 2, hi // 2 + 1):
            krow = 2 * kt + krl[:, None]
            qrow = 2 * t + (kl // 64)[None, :]
            rsq = np.array([rs(2 * t), rs(2 * t + 1)])[(kl // 64)][None, :]
            qc = (kl % 64)[None, :]
            valid = (krow >= rsq) & (krow < rsq + kh) & (kc[:, None] >= csq[qc]) & (kc[:, None] < csq[qc] + 16)
            m = np.where(valid, np.float32(0.0), np.float32(NEGM)).astype(np.float32)
            key = m.tobytes()
            if key not in mask_ids:
                mask_ids[key] = len(masks)
                masks.append(m)
            lst.append((kt, kt - t, mask_ids[key]))
        geo.append(lst)
    return geo, np.stack(masks)


def _biasB_table(rpb_l):
    kl = np.arange(128)
    krl, kc = kl // 64, kl % 64
    out = np.zeros((8, 7, 128, 128), np.float32)
    for di, delta in enumerate(range(-3, 4)):
        dr = 2 * delta + krl[:, None] - krl[None, :] + 7
        dc = np.clip(kc[:, None] - kc[None, :], -15, 15) + 15
        ok = (dr >= 0) & (dr <= 14)
        g = rpb_l[:, np.clip(dr, 0, 14), dc]
        out[:, di] = np.where(ok[None], g, np.float32(0.0))
    return out


def build_attn(T, dbg_t=None):
    NT = T // 128
    geo, masks = _na_geometry(T)
    NM = masks.shape[0]
    nc = bass.Bass("TRN2", target_bir_lowering=False)
    x_d = nc.dram_tensor("x", [T, D], F32, kind="ExternalInput").ap()
    win_d = nc.dram_tensor("win", [D, PROJX], F32, kind="ExternalInput").ap()
    wo_d = nc.dram_tensor("wo", [D, D], F32, kind="ExternalInput").ap()
    bA_d = nc.dram_tensor("biasA", [8, 128, 384], F32, kind="ExternalInput").ap()
    bB_d = nc.dram_tensor("biasB", [8, 7, 128, 128], F32, kind="ExternalInput").ap()
    mB_d = nc.dram_tensor("maskB", [NM, 128, 128], F32, kind="ExternalInput").ap()
    pbc_d = nc.dram_tensor("pbc", [128, 3 * D + 8], F32, kind="ExternalInput").ap()
    id_d = nc.dram_tensor("ident", [128, 128], F32, kind="ExternalInput").ap()
    y_d = nc.dram_tensor("y", [T, D], F32, kind="ExternalOutput").ap()

    with ExitStack() as st:
        pool = SemPool(nc, st, n_dma=16)

        def sb(name, shape, dt):
            return st.enter_context(nc.sbuf_tensor(name, shape, dt))

        def ps(name, shape, dt):
            return st.enter_context(nc.psum_tensor(name, shape, dt))

        identf = sb("identf", [128, 128], F32)
        ident = sb("ident_bf", [128, 128], BF16)
        pbc = sb("pbc_s", [128, 3 * D + 8], F32)
        sinkexp = sb("sinkexp", [128, 8], F32)
        biasA = sb("biasA_bf", [128, 8, 384], BF16)
        biasB = sb("biasB_bf", [128, 8, 7, 128], BF16)
        maskB = sb("maskB_bf", [128, NM, 128], BF16)
        win = sb("win_bf", [128, 8, PROJX], BF16)
        wo = sb("wo_bf", [128, 8, D], BF16)
        stg = [sb("stg%d" % i, [128, PROJX], F32) for i in range(2)]
        kring = sb("kring", [128, 6, RING * 128], BF16)
        vring = sb("vring", [128, RING, 10, 66], BF16)
        qring = sb("qring", [128, 4, 1024], BF16)
        xf = [sb("xf%d" % i, [128, D], F32) for i in range(2)]
        xb = sb("xb", [128, D], BF16)
        xT = sb("xT", [128, 8, 128], BF16)
        eA = [sb("eA%d" % i, [128, 384], BF16) for i in range(2)]
        eB = [sb("eB%d" % i, [128, 640], BF16) for i in range(2)]
        ya = sb("ya", [128, D], F32)
        sq = sb("sq", [128, D], F32)
        y16 = sb("y16", [128, D], BF16)
        yT = sb("yT", [128, 8, 128], BF16)
        xres = sb("xres", [128, D], F32)
        res = sb("res", [128, D], F32)
        outt = [sb("outt%d" % i, [128, D], F32) for i in range(2)]
        dn = sb("dn", [128, 16], F32)
        rd = sb("rd", [128, 16], F32)
        sm = sb("sm", [128, 16], F32)

        pT = ps("pT", [128, 1024], BF16)
        pg = [ps("pg%d" % i, [128, 512], F32) for i in range(2)]
        pSA = ps("pSA", [128, 512], F32)
        pSB = [ps("pSB%d" % i, [128, 512], F32) for i in range(2)]
        pPA = ps("pPA", [128, 512], F32)
        pPB = ps("pPB", [128, 512], F32)

        P = Prog(nc, pool)
        A = P.add
        cvt_rr = [0]
        if dbg_t is not None:
            dq_d = nc.dram_tensor("dbg_q", [128, 1024], BF16, kind="ExternalOutput").ap()
            dk_d = nc.dram_tensor("dbg_k", [128, 6, RING * 128], BF16, kind="ExternalOutput").ap()
            dv_d = nc.dram_tensor("dbg_v", [128, RING, 10, 66], BF16, kind="ExternalOutput").ap()
            dya_d = nc.dram_tensor("dbg_ya", [128, 1024], F32, kind="ExternalOutput").ap()
            dy16_d = nc.dram_tensor("dbg_y16", [128, 1024], BF16, kind="ExternalOutput").ap()
            dres_d = nc.dram_tensor("dbg_res", [128, 1024], F32, kind="ExternalOutput").ap()
            dxT_d = nc.dram_tensor("dbg_xT", [128, 8, 128], BF16, kind="ExternalOutput").ap()

        def convert(out_ap, in_ap, reads, writes):
            e = ['vector', 'gpsimd', 'scalar'][cvt_rr[0] % 3]
            cvt_rr[0] += 1
            if e == 'scalar':
                A(e, lambda g: g.copy(out=out_ap, in_=in_ap), reads, writes)
            else:
                A(e, lambda g: g.tensor_copy(out=out_ap, in_=in_ap), reads, writes)

        A('sync', lambda e: e.dma_start(out=identf[:], in_=id_d), writes=['identf'], dma='identf')
        A('vector', lambda e: e.tensor_copy(out=ident[:], in_=identf[:]), ['identf'], ['ident'])
        A('sync', lambda e: e.dma_start(out=pbc[:], in_=pbc_d), writes=['pbc'], dma='pbc')
        A('scalar', lambda e: e.activation(out=sinkexp[:], in_=pbc[:, 3 * D:3 * D + 8], func=AF.Exp), ['pbc'], ['sinkexp'])
        A('vector', lambda e: e.memset(vring[:], 1.0), [], [('v', i) for i in range(RING)])
        si = [0]

        def stage_load(dst_view, src_ap):
            s = si[0] % 2
            si[0] += 1
            A('sync', lambda e: e.dma_start(out=dst_view(stg[s]), in_=src_ap), writes=[('stg', s)], dma=('stg', s))
            return s

        for h in range(8):
            s = stage_load(lambda t: t[:, 0:384], bA_d[h])
            convert(biasA[:, h, :], stg[s][:, 0:384], [('stg', s)], ['biasA'])
        m0 = 0
        while m0 < NM:
            m1 = min(NM, m0 + 19)
            n = m1 - m0
            s = stage_load(lambda t, n=n: t[:, 0:n * 128].rearrange("k (m q) -> k m q", q=128),
                           mB_d[m0:m1].rearrange("m k q -> k m q"))
            convert(maskB[:, m0:m1, :], stg[s][:, 0:n * 128].rearrange("k (m q) -> k m q", q=128), [('stg', s)], ['maskB'])
            m0 = m1
        for h in range(8):
            s = stage_load(lambda t: t[:, 0:896].rearrange("k (m q) -> k m q", q=128),
                           bB_d[h].rearrange("m k q -> k m q"))
            convert(biasB[:, h, :, :], stg[s][:, 0:896].rearrange("k (m q) -> k m q", q=128), [('stg', s)], ['biasB'])
        for k in range(8):
            s = stage_load(lambda t: t[:], win_d[k * 128:(k + 1) * 128, :])
            convert(win[:, k, :], stg[s][:], [('stg', s)], ['win'])
        for k in range(8):
            s = stage_load(lambda t: t[:, 0:D], wo_d[k * 128:(k + 1) * 128, :])
            convert(wo[:, k, :], stg[s][:, 0:D], [('stg', s)], ['wo'])

        pgi = [0]

        def next_pg():
            i = pgi[0] % 2
            pgi[0] += 1
            return i

        projected = set()

        def project(t):
            projected.add(t)
            s = t % 2
            ks = t % RING
            qs = t % 4
            A('sync', lambda e: e.dma_start(out=xf[s][:], in_=x_d[t * 128:(t + 1) * 128, :]), writes=[('xf', s)], dma=('xf', s))
            A('vector', lambda e: e.tensor_copy(out=xb[:], in_=xf[s][:]), [('xf', s)], ['xb'])
            for k in range(8):
                A('tensor', lambda e, k=k: e.transpose(out=pT[:, k * 128:(k + 1) * 128], in_=xb[:, k * 128:(k + 1) * 128], identity=ident[:]),
                  ['xb', 'ident'], ['pT'])
            A('scalar', lambda e: e.copy(out=xT[:].rearrange("p a b -> p (a b)"), in_=pT[:]), ['pT'], ['xT'])
            if dbg_t == t:
                A('sync', lambda e: e.dma_start(out=dxT_d, in_=xT[:]), reads=['xT'], dma='dbg')
            groups = [
                ('q', [0, 128, 256, 384], 0),
                ('q', [768, 896, 1024, 1152], 4),
                ('kb', [1280, 1408, 1536, 1664], 1),
                ('ka', [512, 2304], 0),
            ]
            for gi, (kind, cols, j0) in enumerate(groups):
                b = next_pg()
                for j, col in enumerate(cols):
                    for k in range(8):
                        A('tensor', lambda e, b=b, j=j, col=col, k=k: e.matmul(
                            pg[b][:, j * 128:(j + 1) * 128], win[:, k, col:col + 128], xT[:, k, :],
                            start=(k == 0), stop=(k == 7)), ['xT', 'win'], [('pg', b)])
                if kind == 'q':
                    if gi == 0:
                        A('scalar', lambda e, b=b, j0=j0: e.mul(out=qring[:, qs, j0 * 128:(j0 + 4) * 128], in_=pg[b][:, 0:512], mul=0.125),
                          [('pg', b)], [('q', qs)])
                    else:
                        A('vector', lambda e, b=b, j0=j0: e.tensor_scalar(out=qring[:, qs, j0 * 128:(j0 + 4) * 128], in0=pg[b][:, 0:512],
                                                                          scalar1=0.125, scalar2=None, op0=ALU.mult),
                          [('pg', b)], [('q', qs)])
                elif kind == 'kb':
                    A('scalar', lambda e, b=b: e.copy(out=kring[:, 1:5, ks * 128:(ks + 1) * 128],
                                                      in_=pg[b][:, 0:512].rearrange("p (a b) -> p a b", b=128)),
                      [('pg', b)], [('k', ks)])
                else:
                    A('vector', lambda e, b=b: e.tensor_copy(out=kring[:, 0, ks * 128:(ks + 1) * 128], in_=pg[b][:, 0:128]),
                      [('pg', b)], [('k', ks)])
                    A('vector', lambda e, b=b: e.tensor_copy(out=kring[:, 5, ks * 128:(ks + 1) * 128], in_=pg[b][:, 128:256]),
                      [('pg', b)], [('k', ks)])
            for (col, n, h0, nh) in [(640, 128, 0, 2), (1792, 512, 2, 8)]:
                b = next_pg()
                for k in range(8):
                    A('tensor', lambda e, b=b, col=col, n=n, k=k: e.matmul(
                        pg[b][:, 0:n], xT[:, k, :], win[:, k, col:col + n], start=(k == 0), stop=(k == 7)),
                      ['xT', 'win'], [('pg', b)])
                eng = 'scalar' if nh == 8 else 'vector'
                if eng == 'scalar':
                    A('scalar', lambda e, b=b, n=n, h0=h0, nh=nh: e.copy(
                        out=vring[:, ks, h0:h0 + nh, 0:64], in_=pg[b][:, 0:n].rearrange("p (a b) -> p a b", b=64)),
                      [('pg', b)], [('v', ks)])
                else:
                    A('vector', lambda e, b=b, n=n, h0=h0, nh=nh: e.tensor_copy(
                        out=vring[:, ks, h0:h0 + nh, 0:64], in_=pg[b][:, 0:n].rearrange("p (a b) -> p a b", b=64)),
                      [('pg', b)], [('v', ks)])

        ecnt = [0]

        def attend(t):
            qs = t % 4
            def head_a(h):
                g = h // 4
                base = (h % 2) * 64
                kch = 0 if (h % 2) == g else 5
                blocks = [c for c in (0, 1, 2) if 0 <= t + c - 1 < NT]
                for c in blocks:
                    ks = (t + c - 1) % RING
                    A('tensor', lambda e, c=c, ks=ks: e.matmul(
                        pSA[:, c * 128:(c + 1) * 128], kring[base:base + 64, kch, ks * 128:(ks + 1) * 128],
                        qring[base:base + 64, qs, (h // 2) * 128:(h // 2 + 1) * 128], start=True, stop=False),
                      [('k', ks), ('q', qs)], ['pSA'])
                    A('tensor', lambda e, c=c: e.matmul(
                        pSA[:, c * 128:(c + 1) * 128], ident[:], biasA[:, h, c * 128:(c + 1) * 128], start=False, stop=True),
                      ['ident', 'biasA'], ['pSA'])
                c0, c1 = blocks[0], blocks[-1]
                es = ecnt[0] % 2
                ecnt[0] += 1
                A('scalar', lambda e, es=es: e.activation(out=eA[es][:, c0 * 128:(c1 + 1) * 128], in_=pSA[:, c0 * 128:(c1 + 1) * 128], func=AF.Exp),
                  ['pSA'], [('eA', es)])
                r = h % 4
                for c in blocks:
                    ks = (t + c - 1) % RING
                    A('tensor', lambda e, c=c, ks=ks, es=es: e.matmul(
                        pPA[:, r * 128:r * 128 + 65], eA[es][:, c * 128:(c + 1) * 128], vring[:, ks, g, 0:65],
                        start=(c == c0), stop=(c == c1)), [('eA', es), ('v', ks)], [('pPA', r)])
                A('vector', lambda e: e.tensor_scalar(out=dn[:, h:h + 1], in0=pPA[:, r * 128 + 64:r * 128 + 65], scalar1=sinkexp[:, h:h + 1],
                                                      scalar2=None, op0=ALU.add), [('pPA', r), 'sinkexp'], [('dn', h)])
                A('vector', lambda e: e.reciprocal(out=rd[:, h:h + 1], in_=dn[:, h:h + 1]), [('dn', h)], [('rd', h)])
                A('vector', lambda e: e.tensor_scalar(out=ya[:, h * 64:(h + 1) * 64], in0=pPA[:, r * 128:r * 128 + 64], scalar1=rd[:, h:h + 1],
                                                      scalar2=None, op0=ALU.mult), [('pPA', r), ('rd', h)], [('ya', 0)])
            for h in range(8):
                head_a(h)
            kts = geo[t]
            nb = len(kts)
            assert all(kt in projected for kt, _, _ in kts) and nb <= 5

            def head_b(h):
                base = (h % 2) * 64
                qch = 4 + h // 2
                kch = 1 + h // 2

                def region(i):
                    return pSB[0][:, i * 128:(i + 1) * 128] if i < 4 else pSB[1][:, (i - 4) * 128:(i - 3) * 128]

                for i, (kt, delta, mid) in enumerate(kts):
                    ks = kt % RING
                    A('tensor', lambda e, i=i, ks=ks: e.matmul(
                        region(i), kring[base:base + 64, kch, ks * 128:(ks + 1) * 128],
                        qring[base:base + 64, qs, qch * 128:(qch + 1) * 128], start=True, stop=False),
                      [('k', ks), ('q', qs)], ['pSB'])
                    A('tensor', lambda e, i=i, delta=delta: e.matmul(
                        region(i), ident[:], biasB[:, h, delta + 3, :], start=False, stop=False), ['ident', 'biasB'], ['pSB'])
                    A('tensor', lambda e, i=i, mid=mid: e.matmul(
                        region(i), ident[:], maskB[:, mid, :], start=False, stop=True), ['ident', 'maskB'], ['pSB'])
                es = ecnt[0] % 2
                ecnt[0] += 1
                n4 = min(nb, 4)
                A('scalar', lambda e, es=es: e.activation(out=eB[es][:, 0:n4 * 128], in_=pSB[0][:, 0:n4 * 128], func=AF.Exp),
                  ['pSB'], [('eB', es)])
                if nb > 4:
                    A('scalar', lambda e, es=es: e.activation(out=eB[es][:, 512:512 + (nb - 4) * 128], in_=pSB[1][:, 0:(nb - 4) * 128], func=AF.Exp),
                      ['pSB'], [('eB', es)])
                r = h % 4
                for i, (kt, delta, mid) in enumerate(kts):
                    ks = kt % RING
                    A('tensor', lambda e, i=i, ks=ks, es=es: e.matmul(
                        pPB[:, r * 128:r * 128 + 65], eB[es][:, i * 128:(i + 1) * 128], vring[:, ks, 2 + h, 0:65],
                        start=(i == 0), stop=(i == nb - 1)), [('eB', es), ('v', ks)], [('pPB', r)])
                A('vector', lambda e: e.reciprocal(out=rd[:, 8 + h:9 + h], in_=pPB[:, r * 128 + 64:r * 128 + 65]), [('pPB', r)], [('rd', 8 + h)])
                A('vector', lambda e: e.tensor_scalar(out=ya[:, 512 + h * 64:512 + (h + 1) * 64], in0=pPB[:, r * 128:r * 128 + 64],
                                                      scalar1=rd[:, 8 + h:9 + h], scalar2=None, op0=ALU.mult),
                  [('pPB', r), ('rd', 8 + h)], [('ya', 1)])
            for h in range(8):
                head_b(h)
            if dbg_t == t:
                A('sync', lambda e: e.dma_start(out=dq_d, in_=qring[:, qs, :]), reads=[('q', qs)], dma='dbg')
                A('sync', lambda e: e.dma_start(out=dk_d, in_=kring[:]), reads=[('k', i) for i in range(RING)], dma='dbg')
                A('sync', lambda e: e.dma_start(out=dv_d, in_=vring[:]), reads=[('v', i) for i in range(RING)], dma='dbg')
                A('sync', lambda e: e.dma_start(out=dya_d, in_=ya[:]), reads=[('ya', 0), ('ya', 1)], dma='dbg')
            A('vector', lambda e: e.tensor_tensor(out=sq[:], in0=ya[:], in1=ya[:], op=ALU.mult), [('ya', 0), ('ya', 1)], ['sq'])
            A('vector', lambda e: e.tensor_reduce(out=sm[:, 0:2], in_=sq[:].rearrange("p (a b) -> p a b", b=512), axis=AX.X, op=ALU.add),
              ['sq'], ['sm01'])
            A('vector', lambda e: e.tensor_scalar(out=sm[:, 2:4], in0=sm[:, 0:2], scalar1=1.0 / 512, scalar2=LN_EPS, op0=ALU.mult, op1=ALU.add),
              ['sm01'], ['sm23'])
            A('scalar', lambda e: e.activation(out=sm[:, 4:6], in_=sm[:, 2:4], func=AF.Sqrt), ['sm23'], ['sm45'])
            A('vector', lambda e: e.reciprocal(out=sm[:, 6:8], in_=sm[:, 4:6]), ['sm45'], ['sm67'])
            for j in range(2):
                A('vector', lambda e, j=j: e.scalar_tensor_tensor(out=y16[:, j * 512:(j + 1) * 512], in0=ya[:, j * 512:(j + 1) * 512],
                                                                  scalar=sm[:, 6 + j:7 + j], in1=pbc[:, j * 512:(j + 1) * 512],
                                                                  op0=ALU.mult, op1=ALU.mult),
                  [('ya', j), 'sm67', 'pbc'], ['y16'])
            for k in range(8):
                A('tensor', lambda e, k=k: e.transpose(out=pT[:, k * 128:(k + 1) * 128], in_=y16[:, k * 128:(k + 1) * 128], identity=ident[:]),
                  ['y16', 'ident'], ['pT'])
            A('scalar', lambda e: e.copy(out=yT[:].rearrange("p a b -> p (a b)"), in_=pT[:]), ['pT'], ['yT'])
            for n in range(2):
                for k in range(8):
                    A('tensor', lambda e, n=n, k=k: e.matmul(pg[n][:, 0:512], yT[:, k, :], wo[:, k, n * 512:(n + 1) * 512],
                                                             start=(k == 0), stop=(k == 7)), ['yT', 'wo'], [('pg', n)])
            A('sync', lambda e: e.dma_start(out=xres[:], in_=x_d[t * 128:(t + 1) * 128, :]), writes=['xres'], dma='xres')
            for n in range(2):
                A('vector', lambda e, n=n: e.scalar_tensor_tensor(out=res[:, n * 512:(n + 1) * 512], in0=xres[:, n * 512:(n + 1) * 512],
                                                                  scalar=float(ALPHA), in1=pg[n][:, 0:512], op0=ALU.mult, op1=ALU.add),
                  ['xres', ('pg', n)], ['res'])
            if dbg_t == t:
                A('sync', lambda e: e.dma_start(out=dy16_d, in_=y16[:]), reads=['y16'], dma='dbg')
                A('sync', lambda e: e.dma_start(out=dres_d, in_=res[:]), reads=['res'], dma='dbg')
            layer_norm(A, res, sq, sm, pbc, D, 2 * D, outt[t % 2], ('outt', t % 2))
            A('sync', lambda e: e.dma_start(out=y_d[t * 128:(t + 1) * 128, :], in_=outt[t % 2][:]), reads=[('outt', t % 2)], dma=('outt', t % 2))

        LAG = 3
        for t in range(NT):
            project(t)
            if t >= LAG:
                attend(t - LAG)
        for t in range(max(NT - LAG, 0), NT):
            attend(t)
        stats = P.emit()
    return nc, stats, masks


def layer_norm(A, res, sq, sm, pbc, goff, boff, out_t, out_key):
    A('vector', lambda e: e.tensor_reduce(out=sm[:, 8:9], in_=res[:], axis=AX.X, op=ALU.add), ['res'], ['ln_s'])
    A('vector', lambda e: e.tensor_scalar(out=sm[:, 9:10], in0=sm[:, 8:9], scalar1=1.0 / D, scalar2=None, op0=ALU.mult), ['ln_s'], ['ln_m'])
    A('vector', lambda e: e.tensor_scalar(out=res[:], in0=res[:], scalar1=sm[:, 9:10], scalar2=None, op0=ALU.subtract), ['res', 'ln_m'], ['res'])
    A('vector', lambda e: e.tensor_tensor(out=sq[:], in0=res[:], in1=res[:], op=ALU.mult), ['res'], ['sq'])
    A('vector', lambda e: e.tensor_reduce(out=sm[:, 10:11], in_=sq[:], axis=AX.X, op=ALU.add), ['sq'], ['ln_v'])
    A('vector', lambda e: e.tensor_scalar(out=sm[:, 11:12], in0=sm[:, 10:11], scalar1=1.0 / D, scalar2=LN_EPS, op0=ALU.mult, op1=ALU.add),
      ['ln_v'], ['ln_ve'])
    A('scalar', lambda e: e.activation(out=sm[:, 12:13], in_=sm[:, 11:12], func=AF.Sqrt), ['ln_ve'], ['ln_sd'])
    A('vector', lambda e: e.reciprocal(out=sm[:, 13:14], in_=sm[:, 12:13]), ['ln_sd'], ['ln_r'])
    A('vector', lambda e: e.scalar_tensor_tensor(out=res[:], in0=res[:], scalar=sm[:, 13:14], in1=pbc[:, goff:goff + D], op0=ALU.mult, op1=ALU.mult),
      ['res', 'ln_r', 'pbc'], ['res'])
    A('gpsimd', lambda e: e.tensor_tensor(out=out_t[:], in0=res[:], in1=pbc[:, boff:boff + D], op=ALU.add), ['res', 'pbc'], [out_key])


NS = 8


def build_peer(T):
    NT = T // 128
    nc = bass.Bass("TRN2", target_bir_lowering=False)
    x_d = nc.dram_tensor("x", [T, D], F32, kind="ExternalInput").ap()
    wq_d = nc.dram_tensor("wq", [D, 2048], F32, kind="ExternalInput").ap()
    keys_d = nc.dram_tensor("keys", [16, 128, 128], F32, kind="ExternalInput").ap()
    uv_d = nc.dram_tensor("uv", [16384, 2048], F32, kind="ExternalInput").ap()
    pbc_d = nc.dram_tensor("pbc", [128, 2 * D], F32, kind="ExternalInput").ap()
    id_d = nc.dram_tensor("ident", [128, 128], F32, kind="ExternalInput").ap()
    io_d = nc.dram_tensor("iota16", [128, 16], F32, kind="ExternalInput").ap()
    y_d = nc.dram_tensor("y", [T, D], F32, kind="ExternalOutput").ap()

    with ExitStack() as st:
        pool = SemPool(nc, st, n_dma=24)

        def sb(name, shape, dt):
            return st.enter_context(nc.sbuf_tensor(name, shape, dt))

        def ps(name, shape, dt):
            return st.enter_context(nc.psum_tensor(name, shape, dt))

        identf = sb("identf", [128, 128], F32)
        ident = sb("ident_bf", [128, 128], BF16)
        iota = sb("iota", [128, 16], F32)
        pbc = sb("pbc_s", [128, 2 * D], F32)
        wq = sb("wq_bf", [128, 8, 2048], BF16)
        keysb = sb("keys_bf", [128, 16, 128], BF16)
        keysT = sb("keysT", [128, 16, 128], BF16)
        stg = [sb("stg%d" % i, [128, 2048], F32) for i in range(2)]
        xf = [sb("xf%d" % i, [128, D], F32) for i in range(2)]
        xb = sb("xb", [128, D], BF16)
        xT = sb("xT", [128, 8, 128], BF16)
        qT = sb("qT", [128, 16, 128], BF16)
        sc = sb("sc", [128, 2048], F32)
        tmp = sb("tmp", [128, 256], F32)
        stop_ = sb("s_top", [128, 256], F32)
        itop = sb("i_top", [128, 256], U32)
        itopf = sb("i_topf", [128, 256], F32)
        cand = sb("cand", [128, 2048], F32)
        oh = sb("oh", [128, 2048], F32)
        fs = sb("fs", [128, 128], F32)
        fpos = sb("fpos", [128, 128], U32)
        abu = sb("abu", [128, 256], U32)
        abf = sb("abf", [128, 256], F32)
        isel = sb("isel", [128, 256], F32)
        ef = sb("ef", [128, 128], F32)
        eidx = [sb("eidx%d" % i, [128, 128], I32) for i in range(2)]
        gate = [sb("gate%d" % i, [128, 128], F32) for i in range(2)]
        ge = sb("ge", [128, 128], F32)
        gs = sb("gs", [128, 16], F32)
        hraw = sb("hraw", [128, 128], F32)
        aa = sb("aa", [128, 128], F32)
        uv = sb("uvring", [128, NS, 2048], F32)
        acc = sb("acc", [128, D], F32)
        res = sb("res", [128, D], F32)
        sq = sb("sq", [128, D], F32)
        sm = sb("sm", [128, 16], F32)
        outt = [sb("outt%d" % i, [128, D], F32) for i in range(2)]

        pT = ps("pT", [128, 1024], BF16)
        pg = [ps("pg%d" % i, [128, 512], F32) for i in range(2)]
        pS = [ps("pS%d" % i, [128, 512], F32) for i in range(2)]

        P = Prog(nc, pool)
        A = P.add
        cvt_rr = [0]

        def convert(out_ap, in_ap, reads, writes):
            e = ['vector', 'gpsimd', 'scalar'][cvt_rr[0] % 3]
            cvt_rr[0] += 1
            if e == 'scalar':
                A(e, lambda g: g.copy(out=out_ap, in_=in_ap), reads, writes)
            else:
                A(e, lambda g: g.tensor_copy(out=out_ap, in_=in_ap), reads, writes)

        A('sync', lambda e: e.dma_start(out=identf[:], in_=id_d), writes=['identf'], dma='identf')
        A('vector', lambda e: e.tensor_copy(out=ident[:], in_=identf[:]), ['identf'], ['ident'])
        A('sync', lambda e: e.dma_start(out=iota[:], in_=io_d), writes=['iota'], dma='iota')
        A('sync', lambda e: e.dma_start(out=pbc[:], in_=pbc_d), writes=['pbc'], dma='pbc')
        for k in range(8):
            s = k % 2
            A('sync', lambda e, s=s, k=k: e.dma_start(out=stg[s][:], in_=wq_d[k * 128:(k + 1) * 128, :]), writes=[('stg', s)], dma=('stg', s))
            convert(wq[:, k, :], stg[s][:], [('stg', s)], ['wq'])
        A('sync', lambda e: e.dma_start(out=stg[0][:].rearrange("k (m c) -> k m c", c=128), in_=keys_d.rearrange("m k c -> k m c")),
          writes=[('stg', 0)], dma=('stg', 0))
        A('vector', lambda e: e.tensor_copy(out=keysb[:].rearrange("p a b -> p (a b)"), in_=stg[0][:]), [('stg', 0)], ['keysb'])
        for half in range(2):
            for j in range(8):
                m = half * 8 + j
                A('tensor', lambda e, m=m, j=j: e.transpose(out=pT[:, j * 128:(j + 1) * 128], in_=keysb[:, m, :], identity=ident[:]),
                  ['keysb', 'ident'], ['pT'])
            A('scalar', lambda e, half=half: e.copy(out=keysT[:, half * 8:(half + 1) * 8, :].rearrange("p a b -> p (a b)"), in_=pT[:]),
              ['pT'], ['keysT'])

        def top16(seg, n, vals, idxs, o, use_tmp):
            A('vector', lambda e: e.max(out=vals[:, o:o + 8], in_=seg), ['segsrc'], ['tk_v0'])
            A('vector', lambda e: e.match_replace(out=use_tmp[:, 0:n], in_to_replace=vals[:, o:o + 8], in_values=seg, imm_value=-1e30),
              ['segsrc', 'tk_v0'], ['tk_tmp'])
            A('vector', lambda e: e.max(out=vals[:, o + 8:o + 16], in_=use_tmp[:, 0:n]), ['tk_tmp'], ['tk_v1'])
            A('vector', lambda e: e.max_index(out=idxs[:, o:o + 8], in_max=vals[:, o:o + 8], in_values=seg), ['segsrc', 'tk_v0'], ['tk_i'])
            A('vector', lambda e: e.max_index(out=idxs[:, o + 8:o + 16], in_max=vals[:, o + 8:o + 16], in_values=seg), ['segsrc', 'tk_v1'], ['tk_i'])

        hkc = [0]

        def route(t):
            s = t % 2
            A('sync', lambda e: e.dma_start(out=xf[s][:], in_=x_d[t * 128:(t + 1) * 128, :]), writes=[('xf', s)], dma=('xf', s))
            A('gpsimd', lambda e: e.tensor_copy(out=xb[:], in_=xf[s][:]), [('xf', s)], ['xb'])
            for k in range(8):
                A('tensor', lambda e, k=k: e.transpose(out=pT[:, k * 128:(k + 1) * 128], in_=xb[:, k * 128:(k + 1) * 128], identity=ident[:]),
                  ['xb', 'ident'], ['pT'])
            A('scalar', lambda e: e.copy(out=xT[:].rearrange("p a b -> p (a b)"), in_=pT[:]), ['pT'], ['xT'])
            for g4 in range(4):
                b = g4 % 2
                for j in range(4):
                    m = g4 * 4 + j
                    for k in range(8):
                        A('tensor', lambda e, b=b, j=j, m=m, k=k: e.matmul(
                            pg[b][:, j * 128:(j + 1) * 128], wq[:, k, m * 128:(m + 1) * 128], xT[:, k, :], start=(k == 0), stop=(k == 7)),
                          ['xT', 'wq'], [('pg', b)])
                A('scalar', lambda e, b=b, g4=g4: e.copy(out=qT[:, g4 * 4:(g4 + 1) * 4, :].rearrange("p a b -> p (a b)"), in_=pg[b][:, 0:512]),
                  [('pg', b)], [('qT', g4)])
            for g4 in range(4):
                b = g4 % 2
                for j in range(4):
                    m = g4 * 4 + j
                    A('tensor', lambda e, b=b, j=j, m=m: e.matmul(pS[b][:, j * 128:(j + 1) * 128], qT[:, m, :], keysT[:, m, :], start=True, stop=True),
                      [('qT', g4), 'keysT'], [('pS', b)])
                A('vector', lambda e, b=b, g4=g4: e.tensor_copy(out=sc[:, g4 * 512:(g4 + 1) * 512], in_=pS[b][:, 0:512]), [('pS', b)], ['segsrc'])
            for m in range(16):
                top16(sc[:, m * 128:(m + 1) * 128], 128, stop_, itop, m * 16, tmp)
            A('vector', lambda e: e.tensor_copy(out=itopf[:], in_=itop[:]), ['tk_i', 'tk_v0', 'tk_v1'], ['itopf'])
            A('vector', lambda e: e.tensor_tensor(out=mkap(cand, 0, [[256, 8], [16, 16], [1, 16]]),
                                                  in0=mkap(stop_, 0, [[32, 8], [1, 16], [0, 16]]),
                                                  in1=mkap(stop_, 16, [[32, 8], [0, 16], [1, 16]]), op=ALU.add),
              ['tk_v0', 'tk_v1', 'itopf'], ['segsrc', 'cand'])
            for h in range(8):
                top16(cand[:, h * 256:(h + 1) * 256], 256, fs, fpos, h * 16, tmp)
            A('vector', lambda e: e.tensor_scalar(out=abu[:, 0:128], in0=fpos[:], scalar1=4, scalar2=None, op0=ALU.logical_shift_right),
              ['tk_i', 'tk_v0', 'tk_v1'], ['abu0'])
            A('vector', lambda e: e.tensor_scalar(out=abu[:, 128:256], in0=fpos[:], scalar1=15, scalar2=None, op0=ALU.bitwise_and),
              ['tk_i'], ['abu1'])
            A('vector', lambda e: e.tensor_copy(out=abf[:], in_=abu[:]), ['abu0', 'abu1'], ['abf'])
            for p in range(2):
                A('vector', lambda e, p=p: e.tensor_tensor(out=mkap(oh, 0, [[256, 8], [16, 16], [1, 16]]),
                                                           in0=mkap(abf, p * 128, [[16, 8], [1, 16], [0, 16]]),
                                                           in1=mkap(iota, 0, [[0, 8], [0, 16], [1, 16]]), op=ALU.is_equal),
                  ['abf', 'iota'], ['oh'])
                A('vector', lambda e, p=p: e.tensor_tensor(out=mkap(oh, 0, [[256, 8], [16, 16], [1, 16]]),
                                                           in0=mkap(oh, 0, [[256, 8], [16, 16], [1, 16]]),
                                                           in1=mkap(itopf, p * 16, [[32, 8], [0, 16], [1, 16]]), op=ALU.mult),
                  ['oh', 'itopf'], ['oh'])
                A('vector', lambda e, p=p: e.tensor_reduce(out=mkap(isel, p * 128, [[16, 8], [1, 16]]),
                                                           in_=mkap(oh, 0, [[256, 8], [16, 16], [1, 16]]), axis=AX.X, op=ALU.add),
                  ['oh'], [('isel', p)])
            A('vector', lambda e: e.scalar_tensor_tensor(out=ef[:], in0=isel[:, 0:128], scalar=128.0, in1=isel[:, 128:256], op0=ALU.mult, op1=ALU.add),
              [('isel', 0), ('isel', 1)], ['ef'])
            A('vector', lambda e: e.tensor_copy(out=eidx[s][:], in_=ef[:]), ['ef'], [('eidx', s)])
            A('vector', lambda e: e.tensor_tensor(out=mkap(ge, 0, [[16, 8], [1, 16]]), in0=mkap(fs, 0, [[16, 8], [1, 16]]),
                                                  in1=mkap(fs, 0, [[16, 8], [0, 16]]), op=ALU.subtract), ['tk_v0', 'tk_v1', 'abu0'], ['ge'])
            A('scalar', lambda e: e.activation(out=ge[:], in_=ge[:], func=AF.Exp), ['ge'], ['ge'])
            A('vector', lambda e: e.tensor_reduce(out=gs[:, 0:8], in_=mkap(ge, 0, [[16, 8], [1, 16]]), axis=AX.X, op=ALU.add), ['ge'], ['gs0'])
            A('vector', lambda e: e.reciprocal(out=gs[:, 8:16], in_=gs[:, 0:8]), ['gs0'], ['gs1'])
            A('vector', lambda e: e.tensor_tensor(out=mkap(gate[s], 0, [[16, 8], [1, 16]]), in0=mkap(ge, 0, [[16, 8], [1, 16]]),
                                                  in1=mkap(gs, 8, [[1, 8], [0, 16]]), op=ALU.mult), ['ge', 'gs1'], [('gate', s)])

        def experts(t):
            s = t % 2
            A('gpsimd', lambda e: e.memset(acc[:], 0.0), [], ['acc'])
            for g in range(32):
                slots = []
                for j in range(4):
                    hk = g * 4 + j
                    sl = hkc[0] % NS
                    hkc[0] += 1
                    slots.append(sl)
                    A('gpsimd', lambda e, hk=hk, sl=sl: e.indirect_dma_start(
                        out=uv[:, sl, :], out_offset=None, in_=uv_d,
                        in_offset=bass.IndirectOffsetOnAxis(ap=eidx[s][:, hk:hk + 1], axis=0)),
                      reads=[('eidx', s)], writes=[('uv', sl)], dma=('uv', sl))
                s0 = slots[0]
                assert slots == list(range(s0, s0 + 4))
                A('vector', lambda e, s0=s0: e.tensor_tensor(out=uv[:, s0:s0 + 4, 0:D], in0=uv[:, s0:s0 + 4, 0:D],
                                                             in1=mkap(xf[s], 0, [[0, 4], [1, D]]), op=ALU.mult),
                  [('uv', x) for x in slots] + [('xf', s)], [('uvu', x) for x in slots])
                A('vector', lambda e, s0=s0, g=g: e.tensor_reduce(out=hraw[:, g * 4:(g + 1) * 4], in_=uv[:, s0:s0 + 4, 0:D], axis=AX.X, op=ALU.add),
                  [('uvu', x) for x in slots], [('hraw', g)])
                A('scalar', lambda e, g=g: e.activation(out=aa[:, g * 4:(g + 1) * 4], in_=hraw[:, g * 4:(g + 1) * 4], func=AF.Gelu),
                  [('hraw', g)], [('aa', g)])
                A('vector', lambda e, g=g: e.tensor_tensor(out=aa[:, g * 4:(g + 1) * 4], in0=aa[:, g * 4:(g + 1) * 4],
                                                           in1=gate[s][:, g * 4:(g + 1) * 4], op=ALU.mult),
                  [('aa', g), ('gate', s)], [('aa', g)])
                for j in range(4):
                    hk = g * 4 + j
                    sl = slots[j]
                    A('vector', lambda e, hk=hk, sl=sl: e.scalar_tensor_tensor(out=acc[:], in0=uv[:, sl, D:2 * D], scalar=aa[:, hk:hk + 1],
                                                                               in1=acc[:], op0=ALU.mult, op1=ALU.add),
                      [('uv', sl), ('uvu', sl), ('aa', g), 'acc'], ['acc', ('uv', sl), ('uvu', sl)])
            A('vector', lambda e: e.scalar_tensor_tensor(out=res[:], in0=xf[s][:], scalar=float(ALPHA), in1=acc[:], op0=ALU.mult, op1=ALU.add),
              [('xf', s), 'acc'], ['res'])
            layer_norm(A, res, sq, sm, pbc, 0, D, outt[s], ('outt', s))
            A('sync', lambda e: e.dma_start(out=y_d[t * 128:(t + 1) * 128, :], in_=outt[s][:]), reads=[('outt', s)], dma=('outt', s))

        route(0)
        for t in range(NT):
            if t + 1 < NT:
                route(t + 1)
            experts(t)
        stats = P.emit()
    return nc, stats


_CACHE = {}


def _get(kind, T):
    key = (kind, T)
    if key not in _CACHE:
        _CACHE[key] = build_attn(T) if kind == 'attn' else build_peer(T)
    return _CACHE[key]


def kernel(x, w_in, w_o, attn_sink, na_rpb, t5_table, gnorm_a, gnorm_b,
           ln1_g, ln1_b, ln2_g, ln2_b, peer_wq, peer_keys, peer_u, peer_v):
    x = np.asarray(x, np.float32)
    B, T, _ = x.shape
    ident = np.eye(128, dtype=np.float32)
    iota16 = np.ascontiguousarray(np.broadcast_to(np.arange(16, dtype=np.float32), (128, 16)))
    biasA = _biasA_table(np.asarray(t5_table, np.float32))
    cur = [np.ascontiguousarray(x[b]) for b in range(B)]
    cores = list(range(B))
    for l in range(DEPTH):
        nc_a, _, masks = _get('attn', T)
        wl = np.asarray(w_in[l], np.float32)
        win_ext = np.ascontiguousarray(np.concatenate([wl, wl[:, 576:640], wl[:, 512:576]], axis=1))
        pbc = np.concatenate([gnorm_a[l], gnorm_b[l], ln1_g[l], ln1_b[l], attn_sink[l]]).astype(np.float32)
        pbc = np.ascontiguousarray(np.broadcast_to(pbc, (128, pbc.shape[0])))
        shared = {"win": win_ext, "wo": np.ascontiguousarray(w_o[l], np.float32), "biasA": biasA,
                  "biasB": _biasB_table(np.asarray(na_rpb[l], np.float32)), "maskB": masks, "pbc": pbc, "ident": ident}
        r = run_bass_kernel_spmd(nc_a, [dict(shared, x=cur[b]) for b in range(B)], core_ids=cores)
        cur = [np.asarray(r.results[b]["y"], np.float32) for b in range(B)]
        nc_p, _ = _get('peer', T)
        uvt = np.ascontiguousarray(np.concatenate([np.asarray(peer_u[l], np.float32), np.asarray(peer_v[l], np.float32)], axis=1))
        pbc2 = np.concatenate([ln2_g[l], ln2_b[l]]).astype(np.float32)
        pbc2 = np.ascontiguousarray(np.broadcast_to(pbc2, (128, pbc2.shape[0])))
        shared = {"wq": np.ascontiguousarray(peer_wq[l], np.float32),
                  "keys": np.ascontiguousarray(np.asarray(peer_keys[l], np.float32).reshape(16, 128, 128)),
                  "uv": uvt, "pbc": pbc2, "ident": ident, "iota16": iota16}
        r = run_bass_kernel_spmd(nc_p, [dict(shared, x=cur[b]) for b in range(B)], core_ids=cores)
        cur = [np.asarray(r.results[b]["y"], np.float32) for b in range(B)]
    return np.stack(cur, axis=0).astype(np.float32)
```

```python
import math
from contextlib import ExitStack

import numpy as np
import concourse.bass as bass
import concourse.mybir as mybir
from concourse.bass_utils import run_bass_kernel_spmd

F32 = mybir.dt.float32
BF16 = mybir.dt.bfloat16
U32 = mybir.dt.uint32
I32 = mybir.dt.int32
ALU = mybir.AluOpType
AF = mybir.ActivationFunctionType
AX = mybir.AxisListType

D = 1024
DEPTH = 4
SEQ = 4096
NCORES = 8
ALPHA = (2 * DEPTH) ** 0.25
LN_EPS = 1e-5
PROJX = 2432
NEGM = -30000.0
RING = 8
ENGS = ['sync', 'scalar', 'vector', 'gpsimd', 'tensor']


class DmaSems:
    def __init__(self, nc, stack, n, tag):
        self.sems = [stack.enter_context(nc.semaphore("d%s_%d" % (tag, i))) for i in range(n)]
        self.cnt = [0] * n


class SemPool:
    def __init__(self, nc, stack, dma, tag):
        self.eng_sem = {e: stack.enter_context(nc.semaphore("e%s_%s" % (tag, e))) for e in ENGS if e != 'sync'}
        self.eng_cnt = {e: 0 for e in self.eng_sem}
        self.dma_sems = dma.sems
        self.dma_cnt = dma.cnt


class _Op:
    __slots__ = ('eng', 'fn', 'deps', 'dma', 'ev', 'waits', 'clock')


class Prog:
    def __init__(self, nc, pool):
        self.nc = nc
        self.pool = pool
        self.ops = []
        self.last_write = {}
        self.readers = {}
        self.slot_idx = {}

    def add(self, eng, fn, reads=(), writes=(), dma=None):
        op = _Op()
        op.eng, op.fn, op.dma = eng, fn, dma
        deps = set()
        for r in reads:
            w = self.last_write.get(r)
            if w is not None:
                deps.add(w)
        for r in writes:
            w = self.last_write.get(r)
            if w is not None:
                deps.add(w)
            for x in self.readers.get(r, ()):
                deps.add(x)
        idx = len(self.ops)
        op.deps = deps
        for r in reads:
            self.readers.setdefault(r, []).append(idx)
        for r in writes:
            self.last_write[r] = idx
            self.readers[r] = []
        self.ops.append(op)
        return idx

    def emit(self):
        nc, pool, ops = self.nc, self.pool, self.ops
        for op in ops:
            if op.dma is None:
                pool.eng_cnt[op.eng] += 1
                op.ev = (('E', op.eng), pool.eng_cnt[op.eng])
            else:
                if op.dma not in self.slot_idx:
                    self.slot_idx[op.dma] = len(self.slot_idx)
                    assert len(self.slot_idx) <= len(pool.dma_sems), "too many dma slots"
                si = self.slot_idx[op.dma]
                pool.dma_cnt[si] += 16
                op.ev = (('D', si), pool.dma_cnt[si])
        clock = {e: {} for e in ENGS}
        for op in ops:
            clk = clock[op.eng]
            waits = {}
            for d in sorted(op.deps, reverse=True):
                A = ops[d]
                if A.eng == 'tensor' and op.eng == 'tensor' and A.dma is None and op.dma is None:
                    continue
                k, v = A.ev
                if clk.get(k, 0) >= v:
                    continue
                waits[k] = max(waits.get(k, 0), v)
                for kk, vv in A.clock.items():
                    if clk.get(kk, 0) < vv:
                        clk[kk] = vv
            op.waits = list(waits.items())
            c = dict(clk)
            c[op.ev[0]] = op.ev[1]
            op.clock = c
        final = {}
        for op in ops:
            final[op.ev[0]] = max(final.get(op.ev[0], 0), op.ev[1])

        def sem_of(k):
            return pool.eng_sem[k[1]] if k[0] == 'E' else pool.dma_sems[k[1]]

        by_eng = {e: [op for op in ops if op.eng == e] for e in ENGS}

        def body(ename):
            def f(eng):
                for op in by_eng[ename]:
                    for k, v in op.waits:
                        eng.wait_ge(sem_of(k), v)
                    ins = op.fn(eng)
                    ins.then_inc(sem_of(op.ev[0]), 16 if op.dma is not None else 1)
                clk = clock[ename]
                for k, v in final.items():
                    if clk.get(k, 0) < v:
                        eng.wait_ge(sem_of(k), v)
            return f

        with nc.Block() as block:
            block.sync(body('sync'))
            block.scalar(body('scalar'))
            block.vector(body('vector'))
            block.gpsimd(body('gpsimd'))
            block.tensor(body('tensor'))
        n = len(ops)
        nw = sum(len(op.waits) for op in ops)
        for op in ops:
            op.clock = None
        return n, nw


def mkap(t, off, dims):
    pst = t[:].ap[0][0]
    return bass.AP(t, off, [[pst, 128]] + [list(d) for d in dims])


def _t5_bucket(rel):
    nb = 16
    max_exact = 8
    ret = (rel > 0).astype(np.int32) * nb
    n = np.abs(rel).astype(np.int32)
    nf = np.maximum(n, 1).astype(np.float32)
    large = max_exact + (np.log(nf / np.float32(max_exact)) / np.float32(math.log(128 / max_exact))
                         * np.float32(nb - max_exact)).astype(np.int32)
    large = np.minimum(large, nb - 1)
    return ret + np.where(n < max_exact, n, large)


def _biasA_table(t5_table):
    kp = np.arange(128)[:, None, None]
    c = np.arange(3)[None, :, None]
    q = np.arange(128)[None, None, :]
    rel = (c - 1) * 128 + kp - q
    valid = np.abs(rel) <= 128
    b = t5_table[_t5_bucket(rel)]
    b = np.where(valid[..., None], b, np.float32(NEGM)).astype(np.float32)
    return np.ascontiguousarray(b.transpose(3, 0, 1, 2)).reshape(8, 128, 384)


def _na_geometry(T):
    rows = T // 64
    kh = min(8, rows)
    nt = T // 128

    def rs(r):
        return min(max(r - kh // 2, 0), rows - kh)

    def cs(c):
        return min(max(c - 8, 0), 64 - 16)

    kl = np.arange(128)
    krl, kc = kl // 64, kl % 64
    masks = []
    mask_ids = {}
    geo = []
    csq = np.array([cs(c) for c in range(64)])
    for t in range(nt):
        lo = min(rs(2 * t), rs(2 * t + 1))
        hi = max(rs(2 * t), rs(2 * t + 1)) + kh - 1
        lst = []
        for kt in range(lo // 2, hi // 2 + 1):
            krow = 2 * kt + krl[:, None]
            qrow = 2 * t + (kl // 64)[None, :]
            rsq = np.array([rs(2 * t), rs(2 * t + 1)])[(kl // 64)][None, :]
            qc = (kl % 64)[None, :]
            valid = (krow >= rsq) & (krow < rsq + kh) & (kc[:, None] >= csq[qc]) & (kc[:, None] < csq[qc] + 16)
            m = np.where(valid, np.float32(0.0), np.float32(NEGM)).astype(np.float32)
            key = m.tobytes()
            if key not in mask_ids:
                mask_ids[key] = len(masks)
                masks.append(m)
            lst.append((kt, kt - t, mask_ids[key]))
        geo.append(lst)
    return geo, np.stack(masks)


def _biasB_table(rpb_l):
    kl = np.arange(128)
    krl, kc = kl // 64, kl % 64
    out = np.zeros((8, 7, 128, 128), np.float32)
    for di, delta in enumerate(range(-3, 4)):
        dr = 2 * delta + krl[:, None] - krl[None, :] + 7
        dc = np.clip(kc[:, None] - kc[None, :], -15, 15) + 15
        ok = (dr >= 0) & (dr <= 14)
        g = rpb_l[:, np.clip(dr, 0, 14), dc]
        out[:, di] = np.where(ok[None], g, np.float32(0.0))
    return out


def emit_attn(nc, pool, T, tag, x_d, y_d, win_d, wo_d, bA_d, bB_d, mB_d, pbc_d, id_d, geo, NM):
    NT = T // 128
    dbg_t = None
    with ExitStack() as st:
        def sb(name, shape, dt):
            return st.enter_context(nc.sbuf_tensor(name + tag, shape, dt))

        def ps(name, shape, dt):
            return st.enter_context(nc.psum_tensor(name + tag, shape, dt))

        identf = sb("identf", [128, 128], F32)
        ident = sb("ident_bf", [128, 128], BF16)
        pbc = sb("pbc_s", [128, 3 * D + 8], F32)
        sinkexp = sb("sinkexp", [128, 8], F32)
        biasA = sb("biasA_bf", [128, 8, 384], BF16)
        biasB = sb("biasB_bf", [128, 8, 7, 128], BF16)
        maskB = sb("maskB_bf", [128, NM, 128], BF16)
        win = sb("win_bf", [128, 8, PROJX], BF16)
        wo = sb("wo_bf", [128, 8, D], BF16)
        stg = [sb("stg%d" % i, [128, PROJX], F32) for i in range(2)]
        kring = sb("kring", [128, 6, RING * 128], BF16)
        vring = sb("vring", [128, RING, 10, 66], BF16)
        qring = sb("qring", [128, 4, 1024], BF16)
        xf = [sb("xf%d" % i, [128, D], F32) for i in range(2)]
        xb = sb("xb", [128, D], BF16)
        xT = sb("xT", [128, 8, 128], BF16)
        eA = [sb("eA%d" % i, [128, 384], BF16) for i in range(2)]
        eB = [sb("eB%d" % i, [128, 640], BF16) for i in range(2)]
        ya = sb("ya", [128, D], F32)
        sq = sb("sq", [128, D], F32)
        y16 = sb("y16", [128, D], BF16)
        yT = sb("yT", [128, 8, 128], BF16)
        xres = sb("xres", [128, D], F32)
        res = sb("res", [128, D], F32)
        outt = [sb("outt%d" % i, [128, D], F32) for i in range(2)]
        dn = sb("dn", [128, 16], F32)
        rd = sb("rd", [128, 16], F32)
        sm = sb("sm", [128, 16], F32)

        pT = ps("pT", [128, 1024], BF16)
        pg = [ps("pg%d" % i, [128, 512], F32) for i in range(2)]
        pSA = ps("pSA", [128, 512], F32)
        pSB = [ps("pSB%d" % i, [128, 512], F32) for i in range(2)]
        pPA = ps("pPA", [128, 512], F32)
        pPB = ps("pPB", [128, 512], F32)

        P = Prog(nc, pool)
        A = P.add
        cvt_rr = [0]
        if dbg_t is not None:
            dq_d = nc.dram_tensor("dbg_q", [128, 1024], BF16, kind="ExternalOutput").ap()
            dk_d = nc.dram_tensor("dbg_k", [128, 6, RING * 128], BF16, kind="ExternalOutput").ap()
            dv_d = nc.dram_tensor("dbg_v", [128, RING, 10, 66], BF16, kind="ExternalOutput").ap()
            dya_d = nc.dram_tensor("dbg_ya", [128, 1024], F32, kind="ExternalOutput").ap()
            dy16_d = nc.dram_tensor("dbg_y16", [128, 1024], BF16, kind="ExternalOutput").ap()
            dres_d = nc.dram_tensor("dbg_res", [128, 1024], F32, kind="ExternalOutput").ap()
            dxT_d = nc.dram_tensor("dbg_xT", [128, 8, 128], BF16, kind="ExternalOutput").ap()

        def convert(out_ap, in_ap, reads, writes):
            e = ['vector', 'gpsimd', 'scalar'][cvt_rr[0] % 3]
            cvt_rr[0] += 1
            if e == 'scalar':
                A(e, lambda g: g.copy(out=out_ap, in_=in_ap), reads, writes)
            else:
                A(e, lambda g: g.tensor_copy(out=out_ap, in_=in_ap), reads, writes)

        A('sync', lambda e: e.dma_start(out=identf[:], in_=id_d), writes=['identf'], dma='identf')
        A('vector', lambda e: e.tensor_copy(out=ident[:], in_=identf[:]), ['identf'], ['ident'])
        A('sync', lambda e: e.dma_start(out=pbc[:], in_=pbc_d), writes=['pbc'], dma='pbc')
        A('scalar', lambda e: e.activation(out=sinkexp[:], in_=pbc[:, 3 * D:3 * D + 8], func=AF.Exp), ['pbc'], ['sinkexp'])
        A('vector', lambda e: e.memset(vring[:], 1.0), [], [('v', i) for i in range(RING)])
        si = [0]

        def stage_load(dst_view, src_ap):
            s = si[0] % 2
            si[0] += 1
            A('sync', lambda e: e.dma_start(out=dst_view(stg[s]), in_=src_ap), writes=[('stg', s)], dma=('stg', s))
            return s

        for h in range(8):
            s = stage_load(lambda t: t[:, 0:384], bA_d[h])
            convert(biasA[:, h, :], stg[s][:, 0:384], [('stg', s)], ['biasA'])
        m0 = 0
        while m0 < NM:
            m1 = min(NM, m0 + 19)
            n = m1 - m0
            s = stage_load(lambda t, n=n: t[:, 0:n * 128].rearrange("k (m q) -> k m q", q=128),
                           mB_d[m0:m1].rearrange("m k q -> k m q"))
            convert(maskB[:, m0:m1, :], stg[s][:, 0:n * 128].rearrange("k (m q) -> k m q", q=128), [('stg', s)], ['maskB'])
            m0 = m1
        for h in range(8):
            s = stage_load(lambda t: t[:, 0:896].rearrange("k (m q) -> k m q", q=128),
                           bB_d[h].rearrange("m k q -> k m q"))
            convert(biasB[:, h, :, :], stg[s][:, 0:896].rearrange("k (m q) -> k m q", q=128), [('stg', s)], ['biasB'])
        for k in range(8):
            s = stage_load(lambda t: t[:], win_d[k * 128:(k + 1) * 128, :])
            convert(win[:, k, :], stg[s][:], [('stg', s)], ['win'])
        for k in range(8):
            s = stage_load(lambda t: t[:, 0:D], wo_d[k * 128:(k + 1) * 128, :])
            convert(wo[:, k, :], stg[s][:, 0:D], [('stg', s)], ['wo'])

        pgi = [0]

        def next_pg():
            i = pgi[0] % 2
            pgi[0] += 1
            return i

        projected = set()

        def project(t):
            projected.add(t)
            s = t % 2
            ks = t % RING
            qs = t % 4
            A('sync', lambda e: e.dma_start(out=xf[s][:], in_=x_d[t * 128:(t + 1) * 128, :]), writes=[('xf', s)], dma=('xf', s))
            A('vector', lambda e: e.tensor_copy(out=xb[:], in_=xf[s][:]), [('xf', s)], ['xb'])
            for k in range(8):
                A('tensor', lambda e, k=k: e.transpose(out=pT[:, k * 128:(k + 1) * 128], in_=xb[:, k * 128:(k + 1) * 128], identity=ident[:]),
                  ['xb', 'ident'], ['pT'])
            A('scalar', lambda e: e.copy(out=xT[:].rearrange("p a b -> p (a b)"), in_=pT[:]), ['pT'], ['xT'])
            if dbg_t == t:
                A('sync', lambda e: e.dma_start(out=dxT_d, in_=xT[:]), reads=['xT'], dma='dbg')
            groups = [
                ('q', [0, 128, 256, 384], 0),
                ('q', [768, 896, 1024, 1152], 4),
                ('kb', [1280, 1408, 1536, 1664], 1),
                ('ka', [512, 2304], 0),
            ]
            for gi, (kind, cols, j0) in enumerate(groups):
                b = next_pg()
                for j, col in enumerate(cols):
                    for k in range(8):
                        A('tensor', lambda e, b=b, j=j, col=col, k=k: e.matmul(
                            pg[b][:, j * 128:(j + 1) * 128], win[:, k, col:col + 128], xT[:, k, :],
                            start=(k == 0), stop=(k == 7)), ['xT', 'win'], [('pg', b)])
                if kind == 'q':
                    if gi == 0:
                        A('scalar', lambda e, b=b, j0=j0: e.mul(out=qring[:, qs, j0 * 128:(j0 + 4) * 128], in_=pg[b][:, 0:512], mul=0.125),
                          [('pg', b)], [('q', qs)])
                    else:
                        A('vector', lambda e, b=b, j0=j0: e.tensor_scalar(out=qring[:, qs, j0 * 128:(j0 + 4) * 128], in0=pg[b][:, 0:512],
                                                                          scalar1=0.125, scalar2=None, op0=ALU.mult),
                          [('pg', b)], [('q', qs)])
                elif kind == 'kb':
                    A('scalar', lambda e, b=b: e.copy(out=kring[:, 1:5, ks * 128:(ks + 1) * 128],
                                                      in_=pg[b][:, 0:512].rearrange("p (a b) -> p a b", b=128)),
                      [('pg', b)], [('k', ks)])
                else:
                    A('vector', lambda e, b=b: e.tensor_copy(out=kring[:, 0, ks * 128:(ks + 1) * 128], in_=pg[b][:, 0:128]),
                      [('pg', b)], [('k', ks)])
                    A('vector', lambda e, b=b: e.tensor_copy(out=kring[:, 5, ks * 128:(ks + 1) * 128], in_=pg[b][:, 128:256]),
                      [('pg', b)], [('k', ks)])
            for (col, n, h0, nh) in [(640, 128, 0, 2), (1792, 512, 2, 8)]:
                b = next_pg()
                for k in range(8):
                    A('tensor', lambda e, b=b, col=col, n=n, k=k: e.matmul(
                        pg[b][:, 0:n], xT[:, k, :], win[:, k, col:col + n], start=(k == 0), stop=(k == 7)),
                      ['xT', 'win'], [('pg', b)])
                eng = 'scalar' if nh == 8 else 'vector'
                if eng == 'scalar':
                    A('scalar', lambda e, b=b, n=n, h0=h0, nh=nh: e.copy(
                        out=vring[:, ks, h0:h0 + nh, 0:64], in_=pg[b][:, 0:n].rearrange("p (a b) -> p a b", b=64)),
                      [('pg', b)], [('v', ks)])
                else:
                    A('vector', lambda e, b=b, n=n, h0=h0, nh=nh: e.tensor_copy(
                        out=vring[:, ks, h0:h0 + nh, 0:64], in_=pg[b][:, 0:n].rearrange("p (a b) -> p a b", b=64)),
                      [('pg', b)], [('v', ks)])

        ecnt = [0]

        def attend(t):
            qs = t % 4
            def head_a(h):
                g = h // 4
                base = (h % 2) * 64
                kch = 0 if (h % 2) == g else 5
                blocks = [c for c in (0, 1, 2) if 0 <= t + c - 1 < NT]
                for c in blocks:
                    ks = (t + c - 1) % RING
                    A('tensor', lambda e, c=c, ks=ks: e.matmul(
                        pSA[:, c * 128:(c + 1) * 128], kring[base:base + 64, kch, ks * 128:(ks + 1) * 128],
                        qring[base:base + 64, qs, (h // 2) * 128:(h // 2 + 1) * 128], start=True, stop=False),
                      [('k', ks), ('q', qs)], ['pSA'])
                    A('tensor', lambda e, c=c: e.matmul(
                        pSA[:, c * 128:(c + 1) * 128], ident[:], biasA[:, h, c * 128:(c + 1) * 128], start=False, stop=True),
                      ['ident', 'biasA'], ['pSA'])
                c0, c1 = blocks[0], blocks[-1]
                es = ecnt[0] % 2
                ecnt[0] += 1
                A('scalar', lambda e, es=es: e.activation(out=eA[es][:, c0 * 128:(c1 + 1) * 128], in_=pSA[:, c0 * 128:(c1 + 1) * 128], func=AF.Exp),
                  ['pSA'], [('eA', es)])
                r = h % 4
                for c in blocks:
                    ks = (t + c - 1) % RING
                    A('tensor', lambda e, c=c, ks=ks, es=es: e.matmul(
                        pPA[:, r * 128:r * 128 + 65], eA[es][:, c * 128:(c + 1) * 128], vring[:, ks, g, 0:65],
                        start=(c == c0), stop=(c == c1)), [('eA', es), ('v', ks)], [('pPA', r)])
                A('vector', lambda e: e.tensor_scalar(out=dn[:, h:h + 1], in0=pPA[:, r * 128 + 64:r * 128 + 65], scalar1=sinkexp[:, h:h + 1],
                                                      scalar2=None, op0=ALU.add), [('pPA', r), 'sinkexp'], [('dn', h)])
                A('vector', lambda e: e.reciprocal(out=rd[:, h:h + 1], in_=dn[:, h:h + 1]), [('dn', h)], [('rd', h)])
                A('vector', lambda e: e.tensor_scalar(out=ya[:, h * 64:(h + 1) * 64], in0=pPA[:, r * 128:r * 128 + 64], scalar1=rd[:, h:h + 1],
                                                      scalar2=None, op0=ALU.mult), [('pPA', r), ('rd', h)], [('ya', 0)])
            for h in range(8):
                head_a(h)
            kts = geo[t]
            nb = len(kts)
            assert all(kt in projected for kt, _, _ in kts) and nb <= 5

            def head_b(h):
                base = (h % 2) * 64
                qch = 4 + h // 2
                kch = 1 + h // 2

                def region(i):
                    return pSB[0][:, i * 128:(i + 1) * 128] if i < 4 else pSB[1][:, (i - 4) * 128:(i - 3) * 128]

                for i, (kt, delta, mid) in enumerate(kts):
                    ks = kt % RING
                    A('tensor', lambda e, i=i, ks=ks: e.matmul(
                        region(i), kring[base:base + 64, kch, ks * 128:(ks + 1) * 128],
                        qring[base:base + 64, qs, qch * 128:(qch + 1) * 128], start=True, stop=False),
                      [('k', ks), ('q', qs)], ['pSB'])
                    A('tensor', lambda e, i=i, delta=delta: e.matmul(
                        region(i), ident[:], biasB[:, h, delta + 3, :], start=False, stop=False), ['ident', 'biasB'], ['pSB'])
                    A('tensor', lambda e, i=i, mid=mid: e.matmul(
                        region(i), ident[:], maskB[:, mid, :], start=False, stop=True), ['ident', 'maskB'], ['pSB'])
                es = ecnt[0] % 2
                ecnt[0] += 1
                n4 = min(nb, 4)
                A('scalar', lambda e, es=es: e.activation(out=eB[es][:, 0:n4 * 128], in_=pSB[0][:, 0:n4 * 128], func=AF.Exp),
                  ['pSB'], [('eB', es)])
                if nb > 4:
                    A('scalar', lambda e, es=es: e.activation(out=eB[es][:, 512:512 + (nb - 4) * 128], in_=pSB[1][:, 0:(nb - 4) * 128], func=AF.Exp),
                      ['pSB'], [('eB', es)])
                r = h % 4
                for i, (kt, delta, mid) in enumerate(kts):
                    ks = kt % RING
                    A('tensor', lambda e, i=i, ks=ks, es=es: e.matmul(
                        pPB[:, r * 128:r * 128 + 65], eB[es][:, i * 128:(i + 1) * 128], vring[:, ks, 2 + h, 0:65],
                        start=(i == 0), stop=(i == nb - 1)), [('eB', es), ('v', ks)], [('pPB', r)])
                A('vector', lambda e: e.reciprocal(out=rd[:, 8 + h:9 + h], in_=pPB[:, r * 128 + 64:r * 128 + 65]), [('pPB', r)], [('rd', 8 + h)])
                A('vector', lambda e: e.tensor_scalar(out=ya[:, 512 + h * 64:512 + (h + 1) * 64], in0=pPB[:, r * 128:r * 128 + 64],
                                                      scalar1=rd[:, 8 + h:9 + h], scalar2=None, op0=ALU.mult),
                  [('pPB', r), ('rd', 8 + h)], [('ya', 1)])
            for h in range(8):
                head_b(h)
            if dbg_t == t:
                A('sync', lambda e: e.dma_start(out=dq_d, in_=qring[:, qs, :]), reads=[('q', qs)], dma='dbg')
                A('sync', lambda e: e.dma_start(out=dk_d, in_=kring[:]), reads=[('k', i) for i in range(RING)], dma='dbg')
                A('sync', lambda e: e.dma_start(out=dv_d, in_=vring[:]), reads=[('v', i) for i in range(RING)], dma='dbg')
                A('sync', lambda e: e.dma_start(out=dya_d, in_=ya[:]), reads=[('ya', 0), ('ya', 1)], dma='dbg')
            A('vector', lambda e: e.tensor_tensor(out=sq[:], in0=ya[:], in1=ya[:], op=ALU.mult), [('ya', 0), ('ya', 1)], ['sq'])
            A('vector', lambda e: e.tensor_reduce(out=sm[:, 0:2], in_=sq[:].rearrange("p (a b) -> p a b", b=512), axis=AX.X, op=ALU.add),
              ['sq'], ['sm01'])
            A('vector', lambda e: e.tensor_scalar(out=sm[:, 2:4], in0=sm[:, 0:2], scalar1=1.0 / 512, scalar2=LN_EPS, op0=ALU.mult, op1=ALU.add),
              ['sm01'], ['sm23'])
            A('scalar', lambda e: e.activation(out=sm[:, 4:6], in_=sm[:, 2:4], func=AF.Sqrt), ['sm23'], ['sm45'])
            A('vector', lambda e: e.reciprocal(out=sm[:, 6:8], in_=sm[:, 4:6]), ['sm45'], ['sm67'])
            for j in range(2):
                A('vector', lambda e, j=j: e.scalar_tensor_tensor(out=y16[:, j * 512:(j + 1) * 512], in0=ya[:, j * 512:(j + 1) * 512],
                                                                  scalar=sm[:, 6 + j:7 + j], in1=pbc[:, j * 512:(j + 1) * 512],
                                                                  op0=ALU.mult, op1=ALU.mult),
                  [('ya', j), 'sm67', 'pbc'], ['y16'])
            for k in range(8):
                A('tensor', lambda e, k=k: e.transpose(out=pT[:, k * 128:(k + 1) * 128], in_=y16[:, k * 128:(k + 1) * 128], identity=ident[:]),
                  ['y16', 'ident'], ['pT'])
            A('scalar', lambda e: e.copy(out=yT[:].rearrange("p a b -> p (a b)"), in_=pT[:]), ['pT'], ['yT'])
            for n in range(2):
                for k in range(8):
                    A('tensor', lambda e, n=n, k=k: e.matmul(pg[n][:, 0:512], yT[:, k, :], wo[:, k, n * 512:(n + 1) * 512],
                                                             start=(k == 0), stop=(k == 7)), ['yT', 'wo'], [('pg', n)])
            A('sync', lambda e: e.dma_start(out=xres[:], in_=x_d[t * 128:(t + 1) * 128, :]), writes=['xres'], dma='xres')
            for n in range(2):
                A('vector', lambda e, n=n: e.scalar_tensor_tensor(out=res[:, n * 512:(n + 1) * 512], in0=xres[:, n * 512:(n + 1) * 512],
                                                                  scalar=float(ALPHA), in1=pg[n][:, 0:512], op0=ALU.mult, op1=ALU.add),
                  ['xres', ('pg', n)], ['res'])
            if dbg_t == t:
                A('sync', lambda e: e.dma_start(out=dy16_d, in_=y16[:]), reads=['y16'], dma='dbg')
                A('sync', lambda e: e.dma_start(out=dres_d, in_=res[:]), reads=['res'], dma='dbg')
            layer_norm(A, res, sq, sm, pbc, D, 2 * D, outt[t % 2], ('outt', t % 2))
            A('sync', lambda e: e.dma_start(out=y_d[t * 128:(t + 1) * 128, :], in_=outt[t % 2][:]), reads=[('outt', t % 2)], dma=('outt', t % 2))

        LAG = 3
        for t in range(NT):
            project(t)
            if t >= LAG:
                attend(t - LAG)
        for t in range(max(NT - LAG, 0), NT):
            attend(t)
        stats = P.emit()
    return stats


def layer_norm(A, res, sq, sm, pbc, goff, boff, out_t, out_key):
    A('vector', lambda e: e.tensor_reduce(out=sm[:, 8:9], in_=res[:], axis=AX.X, op=ALU.add), ['res'], ['ln_s'])
    A('vector', lambda e: e.tensor_scalar(out=sm[:, 9:10], in0=sm[:, 8:9], scalar1=1.0 / D, scalar2=None, op0=ALU.mult), ['ln_s'], ['ln_m'])
    A('vector', lambda e: e.tensor_scalar(out=res[:], in0=res[:], scalar1=sm[:, 9:10], scalar2=None, op0=ALU.subtract), ['res', 'ln_m'], ['res'])
    A('vector', lambda e: e.tensor_tensor(out=sq[:], in0=res[:], in1=res[:], op=ALU.mult), ['res'], ['sq'])
    A('vector', lambda e: e.tensor_reduce(out=sm[:, 10:11], in_=sq[:], axis=AX.X, op=ALU.add), ['sq'], ['ln_v'])
    A('vector', lambda e: e.tensor_scalar(out=sm[:, 11:12], in0=sm[:, 10:11], scalar1=1.0 / D, scalar2=LN_EPS, op0=ALU.mult, op1=ALU.add),
      ['ln_v'], ['ln_ve'])
    A('scalar', lambda e: e.activation(out=sm[:, 12:13], in_=sm[:, 11:12], func=AF.Sqrt), ['ln_ve'], ['ln_sd'])
    A('vector', lambda e: e.reciprocal(out=sm[:, 13:14], in_=sm[:, 12:13]), ['ln_sd'], ['ln_r'])
    A('vector', lambda e: e.scalar_tensor_tensor(out=res[:], in0=res[:], scalar=sm[:, 13:14], in1=pbc[:, goff:goff + D], op0=ALU.mult, op1=ALU.mult),
      ['res', 'ln_r', 'pbc'], ['res'])
    A('gpsimd', lambda e: e.tensor_tensor(out=out_t[:], in0=res[:], in1=pbc[:, boff:boff + D], op=ALU.add), ['res', 'pbc'], [out_key])


NS = 8


def emit_peer(nc, pool, T, tag, x_d, y_d, wq_d, keys_d, uv_d, pbc_d, id_d, io_d, e_off):
    NT = T // 128
    with ExitStack() as st:
        def sb(name, shape, dt):
            return st.enter_context(nc.sbuf_tensor(name + tag, shape, dt))

        def ps(name, shape, dt):
            return st.enter_context(nc.psum_tensor(name + tag, shape, dt))

        identf = sb("identf", [128, 128], F32)
        ident = sb("ident_bf", [128, 128], BF16)
        iota = sb("iota", [128, 16], F32)
        pbc = sb("pbc_s", [128, 2 * D], F32)
        wq = sb("wq_bf", [128, 8, 2048], BF16)
        keysb = sb("keys_bf", [128, 16, 128], BF16)
        keysT = sb("keysT", [128, 16, 128], BF16)
        stg = [sb("stg%d" % i, [128, 2048], F32) for i in range(2)]
        xf = [sb("xf%d" % i, [128, D], F32) for i in range(2)]
        xb = sb("xb", [128, D], BF16)
        xT = sb("xT", [128, 8, 128], BF16)
        qT = sb("qT", [128, 16, 128], BF16)
        sc = sb("sc", [128, 2048], F32)
        tmp = sb("tmp", [128, 256], F32)
        stop_ = sb("s_top", [128, 256], F32)
        itop = sb("i_top", [128, 256], U32)
        itopf = sb("i_topf", [128, 256], F32)
        cand = sb("cand", [128, 2048], F32)
        oh = sb("oh", [128, 2048], F32)
        fs = sb("fs", [128, 128], F32)
        fpos = sb("fpos", [128, 128], U32)
        abu = sb("abu", [128, 256], U32)
        abf = sb("abf", [128, 256], F32)
        isel = sb("isel", [128, 256], F32)
        ef = sb("ef", [128, 128], F32)
        eidx = [sb("eidx%d" % i, [128, 128], I32) for i in range(2)]
        gate = [sb("gate%d" % i, [128, 128], F32) for i in range(2)]
        ge = sb("ge", [128, 128], F32)
        gs = sb("gs", [128, 16], F32)
        hraw = sb("hraw", [128, 128], F32)
        aa = sb("aa", [128, 128], F32)
        uv = sb("uvring", [128, NS, 2048], F32)
        acc = sb("acc", [128, D], F32)
        res = sb("res", [128, D], F32)
        sq = sb("sq", [128, D], F32)
        sm = sb("sm", [128, 16], F32)
        outt = [sb("outt%d" % i, [128, D], F32) for i in range(2)]

        pT = ps("pT", [128, 1024], BF16)
        pg = [ps("pg%d" % i, [128, 512], F32) for i in range(2)]
        pS = [ps("pS%d" % i, [128, 512], F32) for i in range(2)]

        P = Prog(nc, pool)
        A = P.add
        cvt_rr = [0]

        def convert(out_ap, in_ap, reads, writes):
            e = ['vector', 'gpsimd', 'scalar'][cvt_rr[0] % 3]
            cvt_rr[0] += 1
            if e == 'scalar':
                A(e, lambda g: g.copy(out=out_ap, in_=in_ap), reads, writes)
            else:
                A(e, lambda g: g.tensor_copy(out=out_ap, in_=in_ap), reads, writes)

        A('sync', lambda e: e.dma_start(out=identf[:], in_=id_d), writes=['identf'], dma='identf')
        A('vector', lambda e: e.tensor_copy(out=ident[:], in_=identf[:]), ['identf'], ['ident'])
        A('sync', lambda e: e.dma_start(out=iota[:], in_=io_d), writes=['iota'], dma='iota')
        A('sync', lambda e: e.dma_start(out=pbc[:], in_=pbc_d), writes=['pbc'], dma='pbc')
        for k in range(8):
            s = k % 2
            A('sync', lambda e, s=s, k=k: e.dma_start(out=stg[s][:], in_=wq_d[k * 128:(k + 1) * 128, :]), writes=[('stg', s)], dma=('stg', s))
            convert(wq[:, k, :], stg[s][:], [('stg', s)], ['wq'])
        A('sync', lambda e: e.dma_start(out=stg[0][:].rearrange("k (m c) -> k m c", c=128), in_=keys_d.rearrange("m k c -> k m c")),
          writes=[('stg', 0)], dma=('stg', 0))
        A('vector', lambda e: e.tensor_copy(out=keysb[:].rearrange("p a b -> p (a b)"), in_=stg[0][:]), [('stg', 0)], ['keysb'])
        for half in range(2):
            for j in range(8):
                m = half * 8 + j
                A('tensor', lambda e, m=m, j=j: e.transpose(out=pT[:, j * 128:(j + 1) * 128], in_=keysb[:, m, :], identity=ident[:]),
                  ['keysb', 'ident'], ['pT'])
            A('scalar', lambda e, half=half: e.copy(out=keysT[:, half * 8:(half + 1) * 8, :].rearrange("p a b -> p (a b)"), in_=pT[:]),
              ['pT'], ['keysT'])

        def top16(seg, n, vals, idxs, o, use_tmp):
            A('vector', lambda e: e.max(out=vals[:, o:o + 8], in_=seg), ['segsrc'], ['tk_v0'])
            A('vector', lambda e: e.match_replace(out=use_tmp[:, 0:n], in_to_replace=vals[:, o:o + 8], in_values=seg, imm_value=-1e30),
              ['segsrc', 'tk_v0'], ['tk_tmp'])
            A('vector', lambda e: e.max(out=vals[:, o + 8:o + 16], in_=use_tmp[:, 0:n]), ['tk_tmp'], ['tk_v1'])
            A('vector', lambda e: e.max_index(out=idxs[:, o:o + 8], in_max=vals[:, o:o + 8], in_values=seg), ['segsrc', 'tk_v0'], ['tk_i'])
            A('vector', lambda e: e.max_index(out=idxs[:, o + 8:o + 16], in_max=vals[:, o + 8:o + 16], in_values=seg), ['segsrc', 'tk_v1'], ['tk_i'])

        hkc = [0]

        def route(t):
            s = t % 2
            A('sync', lambda e: e.dma_start(out=xf[s][:], in_=x_d[t * 128:(t + 1) * 128, :]), writes=[('xf', s)], dma=('xf', s))
            A('gpsimd', lambda e: e.tensor_copy(out=xb[:], in_=xf[s][:]), [('xf', s)], ['xb'])
            for k in range(8):
                A('tensor', lambda e, k=k: e.transpose(out=pT[:, k * 128:(k + 1) * 128], in_=xb[:, k * 128:(k + 1) * 128], identity=ident[:]),
                  ['xb', 'ident'], ['pT'])
            A('scalar', lambda e: e.copy(out=xT[:].rearrange("p a b -> p (a b)"), in_=pT[:]), ['pT'], ['xT'])
            for g4 in range(4):
                b = g4 % 2
                for j in range(4):
                    m = g4 * 4 + j
                    for k in range(8):
                        A('tensor', lambda e, b=b, j=j, m=m, k=k: e.matmul(
                            pg[b][:, j * 128:(j + 1) * 128], wq[:, k, m * 128:(m + 1) * 128], xT[:, k, :], start=(k == 0), stop=(k == 7)),
                          ['xT', 'wq'], [('pg', b)])
                A('scalar', lambda e, b=b, g4=g4: e.copy(out=qT[:, g4 * 4:(g4 + 1) * 4, :].rearrange("p a b -> p (a b)"), in_=pg[b][:, 0:512]),
                  [('pg', b)], [('qT', g4)])
            for g4 in range(4):
                b = g4 % 2
                for j in range(4):
                    m = g4 * 4 + j
                    A('tensor', lambda e, b=b, j=j, m=m: e.matmul(pS[b][:, j * 128:(j + 1) * 128], qT[:, m, :], keysT[:, m, :], start=True, stop=True),
                      [('qT', g4), 'keysT'], [('pS', b)])
                A('vector', lambda e, b=b, g4=g4: e.tensor_copy(out=sc[:, g4 * 512:(g4 + 1) * 512], in_=pS[b][:, 0:512]), [('pS', b)], ['segsrc'])
            for m in range(16):
                top16(sc[:, m * 128:(m + 1) * 128], 128, stop_, itop, m * 16, tmp)
            A('vector', lambda e: e.tensor_copy(out=itopf[:], in_=itop[:]), ['tk_i', 'tk_v0', 'tk_v1'], ['itopf'])
            A('vector', lambda e: e.tensor_tensor(out=mkap(cand, 0, [[256, 8], [16, 16], [1, 16]]),
                                                  in0=mkap(stop_, 0, [[32, 8], [1, 16], [0, 16]]),
                                                  in1=mkap(stop_, 16, [[32, 8], [0, 16], [1, 16]]), op=ALU.add),
              ['tk_v0', 'tk_v1', 'itopf'], ['segsrc', 'cand'])
            for h in range(8):
                top16(cand[:, h * 256:(h + 1) * 256], 256, fs, fpos, h * 16, tmp)
            A('vector', lambda e: e.tensor_scalar(out=abu[:, 0:128], in0=fpos[:], scalar1=4, scalar2=None, op0=ALU.logical_shift_right),
              ['tk_i', 'tk_v0', 'tk_v1'], ['abu0'])
            A('vector', lambda e: e.tensor_scalar(out=abu[:, 128:256], in0=fpos[:], scalar1=15, scalar2=None, op0=ALU.bitwise_and),
              ['tk_i'], ['abu1'])
            A('vector', lambda e: e.tensor_copy(out=abf[:], in_=abu[:]), ['abu0', 'abu1'], ['abf'])
            for p in range(2):
                A('vector', lambda e, p=p: e.tensor_tensor(out=mkap(oh, 0, [[256, 8], [16, 16], [1, 16]]),
                                                           in0=mkap(abf, p * 128, [[16, 8], [1, 16], [0, 16]]),
                                                           in1=mkap(iota, 0, [[0, 8], [0, 16], [1, 16]]), op=ALU.is_equal),
                  ['abf', 'iota'], ['oh'])
                A('vector', lambda e, p=p: e.tensor_tensor(out=mkap(oh, 0, [[256, 8], [16, 16], [1, 16]]),
                                                           in0=mkap(oh, 0, [[256, 8], [16, 16], [1, 16]]),
                                                           in1=mkap(itopf, p * 16, [[32, 8], [0, 16], [1, 16]]), op=ALU.mult),
                  ['oh', 'itopf'], ['oh'])
                A('vector', lambda e, p=p: e.tensor_reduce(out=mkap(isel, p * 128, [[16, 8], [1, 16]]),
                                                           in_=mkap(oh, 0, [[256, 8], [16, 16], [1, 16]]), axis=AX.X, op=ALU.add),
                  ['oh'], [('isel', p)])
            A('vector', lambda e: e.scalar_tensor_tensor(out=ef[:], in0=isel[:, 0:128], scalar=128.0, in1=isel[:, 128:256], op0=ALU.mult, op1=ALU.add),
              [('isel', 0), ('isel', 1)], ['ef'])
            if e_off:
                A('vector', lambda e: e.tensor_scalar(out=ef[:], in0=ef[:], scalar1=float(e_off), scalar2=None, op0=ALU.add), ['ef'], ['ef'])
            A('vector', lambda e: e.tensor_copy(out=eidx[s][:], in_=ef[:]), ['ef'], [('eidx', s)])
            A('vector', lambda e: e.tensor_tensor(out=mkap(ge, 0, [[16, 8], [1, 16]]), in0=mkap(fs, 0, [[16, 8], [1, 16]]),
                                                  in1=mkap(fs, 0, [[16, 8], [0, 16]]), op=ALU.subtract), ['tk_v0', 'tk_v1', 'abu0'], ['ge'])
            A('scalar', lambda e: e.activation(out=ge[:], in_=ge[:], func=AF.Exp), ['ge'], ['ge'])
            A('vector', lambda e: e.tensor_reduce(out=gs[:, 0:8], in_=mkap(ge, 0, [[16, 8], [1, 16]]), axis=AX.X, op=ALU.add), ['ge'], ['gs0'])
            A('vector', lambda e: e.reciprocal(out=gs[:, 8:16], in_=gs[:, 0:8]), ['gs0'], ['gs1'])
            A('vector', lambda e: e.tensor_tensor(out=mkap(gate[s], 0, [[16, 8], [1, 16]]), in0=mkap(ge, 0, [[16, 8], [1, 16]]),
                                                  in1=mkap(gs, 8, [[1, 8], [0, 16]]), op=ALU.mult), ['ge', 'gs1'], [('gate', s)])

        def experts(t):
            s = t % 2
            A('gpsimd', lambda e: e.memset(acc[:], 0.0), [], ['acc'])
            for g in range(32):
                slots = []
                for j in range(4):
                    hk = g * 4 + j
                    sl = hkc[0] % NS
                    hkc[0] += 1
                    slots.append(sl)
                    A('gpsimd', lambda e, hk=hk, sl=sl: e.indirect_dma_start(
                        out=uv[:, sl, :], out_offset=None, in_=uv_d,
                        in_offset=bass.IndirectOffsetOnAxis(ap=eidx[s][:, hk:hk + 1], axis=0)),
                      reads=[('eidx', s)], writes=[('uv', sl)], dma=('uv', sl))
                s0 = slots[0]
                assert slots == list(range(s0, s0 + 4))
                A('vector', lambda e, s0=s0: e.tensor_tensor(out=uv[:, s0:s0 + 4, 0:D], in0=uv[:, s0:s0 + 4, 0:D],
                                                             in1=mkap(xf[s], 0, [[0, 4], [1, D]]), op=ALU.mult),
                  [('uv', x) for x in slots] + [('xf', s)], [('uvu', x) for x in slots])
                A('vector', lambda e, s0=s0, g=g: e.tensor_reduce(out=hraw[:, g * 4:(g + 1) * 4], in_=uv[:, s0:s0 + 4, 0:D], axis=AX.X, op=ALU.add),
                  [('uvu', x) for x in slots], [('hraw', g)])
                A('scalar', lambda e, g=g: e.activation(out=aa[:, g * 4:(g + 1) * 4], in_=hraw[:, g * 4:(g + 1) * 4], func=AF.Gelu),
                  [('hraw', g)], [('aa', g)])
                A('vector', lambda e, g=g: e.tensor_tensor(out=aa[:, g * 4:(g + 1) * 4], in0=aa[:, g * 4:(g + 1) * 4],
                                                           in1=gate[s][:, g * 4:(g + 1) * 4], op=ALU.mult),
                  [('aa', g), ('gate', s)], [('aa', g)])
                for j in range(4):
                    hk = g * 4 + j
                    sl = slots[j]
                    A('vector', lambda e, hk=hk, sl=sl: e.scalar_tensor_tensor(out=acc[:], in0=uv[:, sl, D:2 * D], scalar=aa[:, hk:hk + 1],
                                                                               in1=acc[:], op0=ALU.mult, op1=ALU.add),
                      [('uv', sl), ('uvu', sl), ('aa', g), 'acc'], ['acc', ('uv', sl), ('uvu', sl)])
            A('vector', lambda e: e.scalar_tensor_tensor(out=res[:], in0=xf[s][:], scalar=float(ALPHA), in1=acc[:], op0=ALU.mult, op1=ALU.add),
              [('xf', s), 'acc'], ['res'])
            layer_norm(A, res, sq, sm, pbc, 0, D, outt[s], ('outt', s))
            A('sync', lambda e: e.dma_start(out=y_d[t * 128:(t + 1) * 128, :], in_=outt[s][:]), reads=[('outt', s)], dma=('outt', s))

        route(0)
        for t in range(NT):
            if t + 1 < NT:
                route(t + 1)
            experts(t)
        stats = P.emit()
    return stats


_CACHE = {}


def build_fused(T, depth):
    geo, masks = _na_geometry(T)
    NM = masks.shape[0]
    nc = bass.Bass("TRN2", target_bir_lowering=False)
    x_d = nc.dram_tensor("x", [T, D], F32, kind="ExternalInput").ap()
    win_d = nc.dram_tensor("win", [depth, D, PROJX], F32, kind="ExternalInput").ap()
    wo_d = nc.dram_tensor("wo", [depth, D, D], F32, kind="ExternalInput").ap()
    bA_d = nc.dram_tensor("biasA", [8, 128, 384], F32, kind="ExternalInput").ap()
    bB_d = nc.dram_tensor("biasB", [depth, 8, 7, 128, 128], F32, kind="ExternalInput").ap()
    mB_d = nc.dram_tensor("maskB", [NM, 128, 128], F32, kind="ExternalInput").ap()
    pa_d = nc.dram_tensor("pbcA", [depth, 128, 3 * D + 8], F32, kind="ExternalInput").ap()
    pp_d = nc.dram_tensor("pbcP", [depth, 128, 2 * D], F32, kind="ExternalInput").ap()
    id_d = nc.dram_tensor("ident", [128, 128], F32, kind="ExternalInput").ap()
    io_d = nc.dram_tensor("iota16", [128, 16], F32, kind="ExternalInput").ap()
    wq_d = nc.dram_tensor("wq", [depth, D, 2048], F32, kind="ExternalInput").ap()
    keys_d = nc.dram_tensor("keys", [depth, 16, 128, 128], F32, kind="ExternalInput").ap()
    uv_d = nc.dram_tensor("uv", [depth * 16384, 2048], F32, kind="ExternalInput").ap()
    y_d = nc.dram_tensor("y", [T, D], F32, kind="ExternalOutput").ap()
    scr_a = nc.dram_tensor("scr_a", [T, D], F32).ap()
    scr_b = nc.dram_tensor("scr_b", [T, D], F32).ap()
    stats = []
    with ExitStack() as gs:
        dmas = [DmaSems(nc, gs, 24, "x"), DmaSems(nc, gs, 24, "y")]
        cur = x_d
        for l in range(depth):
            dma = dmas[0] if l < 2 else dmas[1]
            pool = SemPool(nc, gs, dma, "a%d" % l)
            stats.append(emit_attn(nc, pool, T, "_a%d" % l, cur, scr_a, win_d[l], wo_d[l], bA_d, bB_d[l], mB_d, pa_d[l], id_d, geo, NM))
            pool = SemPool(nc, gs, dma, "p%d" % l)
            dst = y_d if l == depth - 1 else scr_b
            stats.append(emit_peer(nc, pool, T, "_p%d" % l, scr_a, dst, wq_d[l], keys_d[l], uv_d, pp_d[l], id_d, io_d, l * 16384))
            cur = scr_b
    return nc, stats, masks


def host_inputs(depth, w_in, w_o, attn_sink, na_rpb, t5_table, gnorm_a, gnorm_b,
                ln1_g, ln1_b, ln2_g, ln2_b, peer_wq, peer_keys, peer_u, peer_v, masks):
    f = np.float32
    w_in = np.asarray(w_in, f)[:depth]
    win_ext = np.ascontiguousarray(np.concatenate([w_in, w_in[:, :, 576:640], w_in[:, :, 512:576]], axis=2))
    pa = np.concatenate([np.asarray(gnorm_a, f)[:depth], np.asarray(gnorm_b, f)[:depth], np.asarray(ln1_g, f)[:depth],
                         np.asarray(ln1_b, f)[:depth], np.asarray(attn_sink, f)[:depth]], axis=1)
    pa = np.ascontiguousarray(np.broadcast_to(pa[:, None, :], (depth, 128, pa.shape[1])))
    pp = np.concatenate([np.asarray(ln2_g, f)[:depth], np.asarray(ln2_b, f)[:depth]], axis=1)
    pp = np.ascontiguousarray(np.broadcast_to(pp[:, None, :], (depth, 128, pp.shape[1])))
    uv = np.ascontiguousarray(np.concatenate([np.asarray(peer_u, f)[:depth], np.asarray(peer_v, f)[:depth]], axis=2)).reshape(depth * 16384, 2048)
    return {
        "win": win_ext, "wo": np.ascontiguousarray(np.asarray(w_o, f)[:depth]),
        "biasA": _biasA_table(np.asarray(t5_table, f)),
        "biasB": np.stack([_biasB_table(np.asarray(na_rpb[l], f)) for l in range(depth)]),
        "maskB": masks, "pbcA": pa, "pbcP": pp,
        "ident": np.eye(128, dtype=f),
        "iota16": np.ascontiguousarray(np.broadcast_to(np.arange(16, dtype=f), (128, 16))),
        "wq": np.ascontiguousarray(np.asarray(peer_wq, f)[:depth]),
        "keys": np.ascontiguousarray(np.asarray(peer_keys, f)[:depth].reshape(depth, 16, 128, 128)),
        "uv": uv,
    }


def kernel(x, w_in, w_o, attn_sink, na_rpb, t5_table, gnorm_a, gnorm_b,
           ln1_g, ln1_b, ln2_g, ln2_b, peer_wq, peer_keys, peer_u, peer_v):
    x = np.asarray(x, np.float32)
    B, T, _ = x.shape
    key = (T, DEPTH)
    if key not in _CACHE:
        _CACHE[key] = build_fused(T, DEPTH)
    nc, _, masks = _CACHE[key]
    shared = host_inputs(DEPTH, w_in, w_o, attn_sink, na_rpb, t5_table, gnorm_a, gnorm_b,
                         ln1_g, ln1_b, ln2_g, ln2_b, peer_wq, peer_keys, peer_u, peer_v, masks)
    in_maps = [dict(shared, x=np.ascontiguousarray(x[b])) for b in range(B)]
    r = run_bass_kernel_spmd(nc, in_maps, core_ids=list(range(B)))
    return np.stack([np.asarray(r.results[b]["y"], np.float32) for b in range(B)], axis=0)
```

```python
import math
import os
from contextlib import ExitStack

import numpy as np
import concourse.bass as bass
import concourse.mybir as mybir
from concourse.bass_utils import run_bass_kernel_spmd

F32 = mybir.dt.float32
BF16 = mybir.dt.bfloat16
U32 = mybir.dt.uint32
I32 = mybir.dt.int32
ALU = mybir.AluOpType
AF = mybir.ActivationFunctionType
AX = mybir.AxisListType

D = 1024
DEPTH = 4
SEQ = 4096
NCORES = 8
ALPHA = (2 * DEPTH) ** 0.25
LN_EPS = 1e-5
PROJX = 2432
NEGM = -30000.0
RING = 8
ENGS = ['sync', 'scalar', 'vector', 'gpsimd', 'tensor']


class DmaSems:
    def __init__(self, nc, stack, n, tag):
        self.sems = [stack.enter_context(nc.semaphore("d%s_%d" % (tag, i))) for i in range(n)]
        self.cnt = [0] * n


class SemPool:
    def __init__(self, nc, stack, dma, tag):
        self.eng_sem = {e: stack.enter_context(nc.semaphore("e%s_%s" % (tag, e))) for e in ENGS if e != 'sync'}
        self.eng_cnt = {e: 0 for e in self.eng_sem}
        self.dma_sems = dma.sems
        self.dma_cnt = dma.cnt


class _Op:
    __slots__ = ('eng', 'fn', 'deps', 'dma', 'ev', 'waits', 'clock')


class Prog:
    def __init__(self, nc, pool):
        self.nc = nc
        self.pool = pool
        self.ops = []
        self.last_write = {}
        self.readers = {}
        self.slot_idx = {}

    def add(self, eng, fn, reads=(), writes=(), dma=None):
        op = _Op()
        op.eng, op.fn, op.dma = eng, fn, dma
        deps = set()
        for r in reads:
            w = self.last_write.get(r)
            if w is not None:
                deps.add(w)
        for r in writes:
            w = self.last_write.get(r)
            if w is not None:
                deps.add(w)
            for x in self.readers.get(r, ()):
                deps.add(x)
        idx = len(self.ops)
        op.deps = deps
        for r in reads:
            self.readers.setdefault(r, []).append(idx)
        for r in writes:
            self.last_write[r] = idx
            self.readers[r] = []
        self.ops.append(op)
        return idx

    def emit(self):
        nc, pool, ops = self.nc, self.pool, self.ops
        for op in ops:
            if op.dma is None:
                pool.eng_cnt[op.eng] += 1
                op.ev = (('E', op.eng), pool.eng_cnt[op.eng])
            else:
                if op.dma not in self.slot_idx:
                    self.slot_idx[op.dma] = len(self.slot_idx)
                    assert len(self.slot_idx) <= len(pool.dma_sems), "too many dma slots"
                si = self.slot_idx[op.dma]
                pool.dma_cnt[si] += 16
                op.ev = (('D', si), pool.dma_cnt[si])
        clock = {e: {} for e in ENGS}
        for op in ops:
            clk = clock[op.eng]
            waits = {}
            for d in sorted(op.deps, reverse=True):
                A = ops[d]
                if A.eng == 'tensor' and op.eng == 'tensor' and A.dma is None and op.dma is None:
                    continue
                k, v = A.ev
                if clk.get(k, 0) >= v:
                    continue
                waits[k] = max(waits.get(k, 0), v)
                for kk, vv in A.clock.items():
                    if clk.get(kk, 0) < vv:
                        clk[kk] = vv
            op.waits = list(waits.items())
            c = dict(clk)
            c[op.ev[0]] = op.ev[1]
            op.clock = c
        final = {}
        for op in ops:
            final[op.ev[0]] = max(final.get(op.ev[0], 0), op.ev[1])

        def sem_of(k):
            return pool.eng_sem[k[1]] if k[0] == 'E' else pool.dma_sems[k[1]]

        by_eng = {e: [op for op in ops if op.eng == e] for e in ENGS}

        def body(ename):
            def f(eng):
                for op in by_eng[ename]:
                    for k, v in op.waits:
                        eng.wait_ge(sem_of(k), v)
                    ins = op.fn(eng)
                    ins.then_inc(sem_of(op.ev[0]), 16 if op.dma is not None else 1)
                clk = clock[ename]
                for k, v in final.items():
                    if clk.get(k, 0) < v:
                        eng.wait_ge(sem_of(k), v)
            return f

        with nc.Block() as block:
            block.sync(body('sync'))
            block.scalar(body('scalar'))
            block.vector(body('vector'))
            block.gpsimd(body('gpsimd'))
            block.tensor(body('tensor'))
        n = len(ops)
        nw = sum(len(op.waits) for op in ops)
        for op in ops:
            op.clock = None
        return n, nw


def mkap(t, off, dims):
    pst = t[:].ap[0][0]
    return bass.AP(t, off, [[pst, 128]] + [list(d) for d in dims])


def _t5_bucket(rel):
    nb = 16
    max_exact = 8
    ret = (rel > 0).astype(np.int32) * nb
    n = np.abs(rel).astype(np.int32)
    nf = np.maximum(n, 1).astype(np.float32)
    large = max_exact + (np.log(nf / np.float32(max_exact)) / np.float32(math.log(128 / max_exact))
                         * np.float32(nb - max_exact)).astype(np.int32)
    large = np.minimum(large, nb - 1)
    return ret + np.where(n < max_exact, n, large)


def _biasA_table(t5_table):
    kp = np.arange(128)[:, None, None]
    c = np.arange(3)[None, :, None]
    q = np.arange(128)[None, None, :]
    rel = (c - 1) * 128 + kp - q
    valid = np.abs(rel) <= 128
    b = t5_table[_t5_bucket(rel)]
    b = np.where(valid[..., None], b, np.float32(NEGM)).astype(np.float32)
    return np.ascontiguousarray(b.transpose(3, 0, 1, 2)).reshape(8, 128, 384)


def _na_geometry(T):
    rows = T // 64
    kh = min(8, rows)
    nt = T // 128

    def rs(r):
        return min(max(r - kh // 2, 0), rows - kh)

    def cs(c):
        return min(max(c - 8, 0), 64 - 16)

    kl = np.arange(128)
    krl, kc = kl // 64, kl % 64
    masks = []
    mask_ids = {}
    geo = []
    csq = np.array([cs(c) for c in range(64)])
    for t in range(nt):
        lo = min(rs(2 * t), rs(2 * t + 1))
        hi = max(rs(2 * t), rs(2 * t + 1)) + kh - 1
        lst = []
        for kt in range(lo // 2, hi // 2 + 1):
            krow = 2 * kt + krl[:, None]
            qrow = 2 * t + (kl // 64)[None, :]
            rsq = np.array([rs(2 * t), rs(2 * t + 1)])[(kl // 64)][None, :]
            qc = (kl % 64)[None, :]
            valid = (krow >= rsq) & (krow < rsq + kh) & (kc[:, None] >= csq[qc]) & (kc[:, None] < csq[qc] + 16)
            m = np.where(valid, np.float32(0.0), np.float32(NEGM)).astype(np.float32)
            key = m.tobytes()
            if key not in mask_ids:
                mask_ids[key] = len(masks)
                masks.append(m)
            lst.append((kt, kt - t, mask_ids[key]))
        geo.append(lst)
    return geo, np.stack(masks)


def _biasB_table(rpb_l):
    kl = np.arange(128)
    krl, kc = kl // 64, kl % 64
    out = np.zeros((8, 7, 128, 128), np.float32)
    for di, delta in enumerate(range(-3, 4)):
        dr = 2 * delta + krl[:, None] - krl[None, :] + 7
        dc = np.clip(kc[:, None] - kc[None, :], -15, 15) + 15
        ok = (dr >= 0) & (dr <= 14)
        g = rpb_l[:, np.clip(dr, 0, 14), dc]
        out[:, di] = np.where(ok[None], g, np.float32(0.0))
    return out


def emit_attn(nc, pool, T, tag, x_d, y_d, win_d, wo_d, bA_d, bB_d, mB_d, pbc_d, id_d, geo, NM):
    NT = T // 128
    dbg_t = None
    with ExitStack() as st:
        def sb(name, shape, dt):
            return st.enter_context(nc.sbuf_tensor(name + tag, shape, dt))

        def ps(name, shape, dt):
            return st.enter_context(nc.psum_tensor(name + tag, shape, dt))

        identf = sb("identf", [128, 128], F32)
        ident = sb("ident_bf", [128, 128], BF16)
        pbc = sb("pbc_s", [128, 3 * D + 8], F32)
        sinkexp = sb("sinkexp", [128, 8], F32)
        biasA = sb("biasA_bf", [128, 8, 384], BF16)
        biasB = sb("biasB_bf", [128, 8, 7, 128], BF16)
        maskB = sb("maskB_bf", [128, NM, 128], BF16)
        win = sb("win_bf", [128, 8, PROJX], BF16)
        wo = sb("wo_bf", [128, 8, D], BF16)
        stg = [sb("stg%d" % i, [128, PROJX], F32) for i in range(2)]
        kring = sb("kring", [128, 6, RING * 128], BF16)
        vring = sb("vring", [128, RING, 10, 66], BF16)
        qring = sb("qring", [128, 4, 1024], BF16)
        xf = [sb("xf%d" % i, [128, D], F32) for i in range(2)]
        xb = sb("xb", [128, D], BF16)
        xT = sb("xT", [128, 8, 128], BF16)
        eA = [sb("eA%d" % i, [128, 384], BF16) for i in range(2)]
        eB = [sb("eB%d" % i, [128, 640], BF16) for i in range(2)]
        ya = sb("ya", [128, D], F32)
        sq = sb("sq", [128, D], F32)
        y16 = sb("y16", [128, D], BF16)
        yT = sb("yT", [128, 8, 128], BF16)
        xres = sb("xres", [128, D], F32)
        res = sb("res", [128, D], F32)
        outt = [sb("outt%d" % i, [128, D], F32) for i in range(2)]
        dn = sb("dn", [128, 16], F32)
        rd = sb("rd", [128, 16], F32)
        sm = sb("sm", [128, 16], F32)

        pT = ps("pT", [128, 1024], BF16)
        pg = [ps("pg%d" % i, [128, 512], F32) for i in range(2)]
        pSA = ps("pSA", [128, 512], F32)
        pSB = [ps("pSB%d" % i, [128, 512], F32) for i in range(2)]
        pPA = ps("pPA", [128, 512], F32)
        pPB = ps("pPB", [128, 512], F32)

        P = Prog(nc, pool)
        A = P.add
        cvt_rr = [0]
        if dbg_t is not None:
            dq_d = nc.dram_tensor("dbg_q", [128, 1024], BF16, kind="ExternalOutput").ap()
            dk_d = nc.dram_tensor("dbg_k", [128, 6, RING * 128], BF16, kind="ExternalOutput").ap()
            dv_d = nc.dram_tensor("dbg_v", [128, RING, 10, 66], BF16, kind="ExternalOutput").ap()
            dya_d = nc.dram_tensor("dbg_ya", [128, 1024], F32, kind="ExternalOutput").ap()
            dy16_d = nc.dram_tensor("dbg_y16", [128, 1024], BF16, kind="ExternalOutput").ap()
            dres_d = nc.dram_tensor("dbg_res", [128, 1024], F32, kind="ExternalOutput").ap()
            dxT_d = nc.dram_tensor("dbg_xT", [128, 8, 128], BF16, kind="ExternalOutput").ap()

        def convert(out_ap, in_ap, reads, writes):
            e = ['vector', 'gpsimd', 'scalar'][cvt_rr[0] % 3]
            cvt_rr[0] += 1
            if e == 'scalar':
                A(e, lambda g: g.copy(out=out_ap, in_=in_ap), reads, writes)
            else:
                A(e, lambda g: g.tensor_copy(out=out_ap, in_=in_ap), reads, writes)

        A('sync', lambda e: e.dma_start(out=identf[:], in_=id_d), writes=['identf'], dma='identf')
        A('vector', lambda e: e.tensor_copy(out=ident[:], in_=identf[:]), ['identf'], ['ident'])
        A('sync', lambda e: e.dma_start(out=pbc[:], in_=pbc_d), writes=['pbc'], dma='pbc')
        A('scalar', lambda e: e.activation(out=sinkexp[:], in_=pbc[:, 3 * D:3 * D + 8], func=AF.Exp), ['pbc'], ['sinkexp'])
        A('vector', lambda e: e.memset(vring[:], 1.0), [], [('v', i) for i in range(RING)])
        si = [0]

        def stage_load(dst_view, src_ap):
            s = si[0] % 2
            si[0] += 1
            A('sync', lambda e: e.dma_start(out=dst_view(stg[s]), in_=src_ap), writes=[('stg', s)], dma=('stg', s))
            return s

        for h in range(8):
            s = stage_load(lambda t: t[:, 0:384], bA_d[h])
            convert(biasA[:, h, :], stg[s][:, 0:384], [('stg', s)], ['biasA'])
        m0 = 0
        while m0 < NM:
            m1 = min(NM, m0 + 19)
            n = m1 - m0
            s = stage_load(lambda t, n=n: t[:, 0:n * 128].rearrange("k (m q) -> k m q", q=128),
                           mB_d[m0:m1].rearrange("m k q -> k m q"))
            convert(maskB[:, m0:m1, :], stg[s][:, 0:n * 128].rearrange("k (m q) -> k m q", q=128), [('stg', s)], ['maskB'])
            m0 = m1
        for h in range(8):
            s = stage_load(lambda t: t[:, 0:896].rearrange("k (m q) -> k m q", q=128),
                           bB_d[h].rearrange("m k q -> k m q"))
            convert(biasB[:, h, :, :], stg[s][:, 0:896].rearrange("k (m q) -> k m q", q=128), [('stg', s)], ['biasB'])
        for k in range(8):
            s = stage_load(lambda t: t[:], win_d[k * 128:(k + 1) * 128, :])
            convert(win[:, k, :], stg[s][:], [('stg', s)], ['win'])
        for k in range(8):
            s = stage_load(lambda t: t[:, 0:D], wo_d[k * 128:(k + 1) * 128, :])
            convert(wo[:, k, :], stg[s][:, 0:D], [('stg', s)], ['wo'])

        pgi = [0]

        def next_pg():
            i = pgi[0] % 2
            pgi[0] += 1
            return i

        projected = set()

        def project(t):
            projected.add(t)
            s = t % 2
            ks = t % RING
            qs = t % 4
            A('sync', lambda e: e.dma_start(out=xf[s][:], in_=x_d[t * 128:(t + 1) * 128, :]), writes=[('xf', s)], dma=('xf', s))
            A('vector', lambda e: e.tensor_copy(out=xb[:], in_=xf[s][:]), [('xf', s)], ['xb'])
            for k in range(8):
                A('tensor', lambda e, k=k: e.transpose(out=pT[:, k * 128:(k + 1) * 128], in_=xb[:, k * 128:(k + 1) * 128], identity=ident[:]),
                  ['xb', 'ident'], ['pT'])
            A('scalar', lambda e: e.copy(out=xT[:].rearrange("p a b -> p (a b)"), in_=pT[:]), ['pT'], ['xT'])
            if dbg_t == t:
                A('sync', lambda e: e.dma_start(out=dxT_d, in_=xT[:]), reads=['xT'], dma='dbg')
            groups = [
                ('q', [0, 128, 256, 384], 0),
                ('q', [768, 896, 1024, 1152], 4),
                ('kb', [1280, 1408, 1536, 1664], 1),
                ('ka', [512, 2304], 0),
            ]
            for gi, (kind, cols, j0) in enumerate(groups):
                b = next_pg()
                for j, col in enumerate(cols):
                    for k in range(8):
                        A('tensor', lambda e, b=b, j=j, col=col, k=k: e.matmul(
                            pg[b][:, j * 128:(j + 1) * 128], win[:, k, col:col + 128], xT[:, k, :],
                            start=(k == 0), stop=(k == 7)), ['xT', 'win'], [('pg', b)])
                if kind == 'q':
                    if gi == 0:
                        A('scalar', lambda e, b=b, j0=j0: e.mul(out=qring[:, qs, j0 * 128:(j0 + 4) * 128], in_=pg[b][:, 0:512], mul=0.125),
                          [('pg', b)], [('q', qs)])
                    else:
                        A('vector', lambda e, b=b, j0=j0: e.tensor_scalar(out=qring[:, qs, j0 * 128:(j0 + 4) * 128], in0=pg[b][:, 0:512],
                                                                          scalar1=0.125, scalar2=None, op0=ALU.mult),
                          [('pg', b)], [('q', qs)])
                elif kind == 'kb':
                    A('scalar', lambda e, b=b: e.copy(out=kring[:, 1:5, ks * 128:(ks + 1) * 128],
                                                      in_=pg[b][:, 0:512].rearrange("p (a b) -> p a b", b=128)),
                      [('pg', b)], [('k', ks)])
                else:
                    A('vector', lambda e, b=b: e.tensor_copy(out=kring[:, 0, ks * 128:(ks + 1) * 128], in_=pg[b][:, 0:128]),
                      [('pg', b)], [('k', ks)])
                    A('vector', lambda e, b=b: e.tensor_copy(out=kring[:, 5, ks * 128:(ks + 1) * 128], in_=pg[b][:, 128:256]),
                      [('pg', b)], [('k', ks)])
            for (col, n, h0, nh) in [(640, 128, 0, 2), (1792, 512, 2, 8)]:
                b = next_pg()
                for k in range(8):
                    A('tensor', lambda e, b=b, col=col, n=n, k=k: e.matmul(
                        pg[b][:, 0:n], xT[:, k, :], win[:, k, col:col + n], start=(k == 0), stop=(k == 7)),
                      ['xT', 'win'], [('pg', b)])
                eng = 'scalar' if nh == 8 else 'vector'
                if eng == 'scalar':
                    A('scalar', lambda e, b=b, n=n, h0=h0, nh=nh: e.copy(
                        out=vring[:, ks, h0:h0 + nh, 0:64], in_=pg[b][:, 0:n].rearrange("p (a b) -> p a b", b=64)),
                      [('pg', b)], [('v', ks)])
                else:
                    A('vector', lambda e, b=b, n=n, h0=h0, nh=nh: e.tensor_copy(
                        out=vring[:, ks, h0:h0 + nh, 0:64], in_=pg[b][:, 0:n].rearrange("p (a b) -> p a b", b=64)),
                      [('pg', b)], [('v', ks)])

        ecnt = [0]

        def attend(t):
            qs = t % 4
            def head_a(h):
                g = h // 4
                base = (h % 2) * 64
                kch = 0 if (h % 2) == g else 5
                blocks = [c for c in (0, 1, 2) if 0 <= t + c - 1 < NT]
                for c in blocks:
                    ks = (t + c - 1) % RING
                    A('tensor', lambda e, c=c, ks=ks: e.matmul(
                        pSA[:, c * 128:(c + 1) * 128], kring[base:base + 64, kch, ks * 128:(ks + 1) * 128],
                        qring[base:base + 64, qs, (h // 2) * 128:(h // 2 + 1) * 128], start=True, stop=False),
                      [('k', ks), ('q', qs)], ['pSA'])
                    A('tensor', lambda e, c=c: e.matmul(
                        pSA[:, c * 128:(c + 1) * 128], ident[:], biasA[:, h, c * 128:(c + 1) * 128], start=False, stop=True),
                      ['ident', 'biasA'], ['pSA'])
                c0, c1 = blocks[0], blocks[-1]
                es = ecnt[0] % 2
                ecnt[0] += 1
                A('scalar', lambda e, es=es: e.activation(out=eA[es][:, c0 * 128:(c1 + 1) * 128], in_=pSA[:, c0 * 128:(c1 + 1) * 128], func=AF.Exp),
                  ['pSA'], [('eA', es)])
                r = h % 4
                for c in blocks:
                    ks = (t + c - 1) % RING
                    A('tensor', lambda e, c=c, ks=ks, es=es: e.matmul(
                        pPA[:, r * 128:r * 128 + 65], eA[es][:, c * 128:(c + 1) * 128], vring[:, ks, g, 0:65],
                        start=(c == c0), stop=(c == c1)), [('eA', es), ('v', ks)], [('pPA', r)])
                A('vector', lambda e: e.tensor_scalar(out=dn[:, h:h + 1], in0=pPA[:, r * 128 + 64:r * 128 + 65], scalar1=sinkexp[:, h:h + 1],
                                                      scalar2=None, op0=ALU.add), [('pPA', r), 'sinkexp'], [('dn', h)])
                A('vector', lambda e: e.reciprocal(out=rd[:, h:h + 1], in_=dn[:, h:h + 1]), [('dn', h)], [('rd', h)])
                A('vector', lambda e: e.tensor_scalar(out=ya[:, h * 64:(h + 1) * 64], in0=pPA[:, r * 128:r * 128 + 64], scalar1=rd[:, h:h + 1],
                                                      scalar2=None, op0=ALU.mult), [('pPA', r), ('rd', h)], [('ya', 0)])
            for h in range(8):
                head_a(h)
            kts = geo[t]
            nb = len(kts)
            assert all(kt in projected for kt, _, _ in kts) and nb <= 5

            def head_b(h):
                base = (h % 2) * 64
                qch = 4 + h // 2
                kch = 1 + h // 2

                def region(i):
                    return pSB[0][:, i * 128:(i + 1) * 128] if i < 4 else pSB[1][:, (i - 4) * 128:(i - 3) * 128]

                for i, (kt, delta, mid) in enumerate(kts):
                    ks = kt % RING
                    A('tensor', lambda e, i=i, ks=ks: e.matmul(
                        region(i), kring[base:base + 64, kch, ks * 128:(ks + 1) * 128],
                        qring[base:base + 64, qs, qch * 128:(qch + 1) * 128], start=True, stop=False),
                      [('k', ks), ('q', qs)], ['pSB'])
                    A('tensor', lambda e, i=i, delta=delta: e.matmul(
                        region(i), ident[:], biasB[:, h, delta + 3, :], start=False, stop=False), ['ident', 'biasB'], ['pSB'])
                    A('tensor', lambda e, i=i, mid=mid: e.matmul(
                        region(i), ident[:], maskB[:, mid, :], start=False, stop=True), ['ident', 'maskB'], ['pSB'])
                es = ecnt[0] % 2
                ecnt[0] += 1
                n4 = min(nb, 4)
                A('scalar', lambda e, es=es: e.activation(out=eB[es][:, 0:n4 * 128], in_=pSB[0][:, 0:n4 * 128], func=AF.Exp),
                  ['pSB'], [('eB', es)])
                if nb > 4:
                    A('scalar', lambda e, es=es: e.activation(out=eB[es][:, 512:512 + (nb - 4) * 128], in_=pSB[1][:, 0:(nb - 4) * 128], func=AF.Exp),
                      ['pSB'], [('eB', es)])
                r = h % 4
                for i, (kt, delta, mid) in enumerate(kts):
                    ks = kt % RING
                    A('tensor', lambda e, i=i, ks=ks, es=es: e.matmul(
                        pPB[:, r * 128:r * 128 + 65], eB[es][:, i * 128:(i + 1) * 128], vring[:, ks, 2 + h, 0:65],
                        start=(i == 0), stop=(i == nb - 1)), [('eB', es), ('v', ks)], [('pPB', r)])
                A('vector', lambda e: e.reciprocal(out=rd[:, 8 + h:9 + h], in_=pPB[:, r * 128 + 64:r * 128 + 65]), [('pPB', r)], [('rd', 8 + h)])
                A('vector', lambda e: e.tensor_scalar(out=ya[:, 512 + h * 64:512 + (h + 1) * 64], in0=pPB[:, r * 128:r * 128 + 64],
                                                      scalar1=rd[:, 8 + h:9 + h], scalar2=None, op0=ALU.mult),
                  [('pPB', r), ('rd', 8 + h)], [('ya', 1)])
            for h in range(8):
                head_b(h)
            if dbg_t == t:
                A('sync', lambda e: e.dma_start(out=dq_d, in_=qring[:, qs, :]), reads=[('q', qs)], dma='dbg')
                A('sync', lambda e: e.dma_start(out=dk_d, in_=kring[:]), reads=[('k', i) for i in range(RING)], dma='dbg')
                A('sync', lambda e: e.dma_start(out=dv_d, in_=vring[:]), reads=[('v', i) for i in range(RING)], dma='dbg')
                A('sync', lambda e: e.dma_start(out=dya_d, in_=ya[:]), reads=[('ya', 0), ('ya', 1)], dma='dbg')
            A('vector', lambda e: e.tensor_tensor(out=sq[:], in0=ya[:], in1=ya[:], op=ALU.mult), [('ya', 0), ('ya', 1)], ['sq'])
            A('vector', lambda e: e.tensor_reduce(out=sm[:, 0:2], in_=sq[:].rearrange("p (a b) -> p a b", b=512), axis=AX.X, op=ALU.add),
              ['sq'], ['sm01'])
            A('vector', lambda e: e.tensor_scalar(out=sm[:, 2:4], in0=sm[:, 0:2], scalar1=1.0 / 512, scalar2=LN_EPS, op0=ALU.mult, op1=ALU.add),
              ['sm01'], ['sm23'])
            A('scalar', lambda e: e.activation(out=sm[:, 4:6], in_=sm[:, 2:4], func=AF.Sqrt), ['sm23'], ['sm45'])
            A('vector', lambda e: e.reciprocal(out=sm[:, 6:8], in_=sm[:, 4:6]), ['sm45'], ['sm67'])
            for j in range(2):
                A('vector', lambda e, j=j: e.scalar_tensor_tensor(out=y16[:, j * 512:(j + 1) * 512], in0=ya[:, j * 512:(j + 1) * 512],
                                                                  scalar=sm[:, 6 + j:7 + j], in1=pbc[:, j * 512:(j + 1) * 512],
                                                                  op0=ALU.mult, op1=ALU.mult),
                  [('ya', j), 'sm67', 'pbc'], ['y16'])
            for k in range(8):
                A('tensor', lambda e, k=k: e.transpose(out=pT[:, k * 128:(k + 1) * 128], in_=y16[:, k * 128:(k + 1) * 128], identity=ident[:]),
                  ['y16', 'ident'], ['pT'])
            A('scalar', lambda e: e.copy(out=yT[:].rearrange("p a b -> p (a b)"), in_=pT[:]), ['pT'], ['yT'])
            for n in range(2):
                for k in range(8):
                    A('tensor', lambda e, n=n, k=k: e.matmul(pg[n][:, 0:512], yT[:, k, :], wo[:, k, n * 512:(n + 1) * 512],
                                                             start=(k == 0), stop=(k == 7)), ['yT', 'wo'], [('pg', n)])
            A('sync', lambda e: e.dma_start(out=xres[:], in_=x_d[t * 128:(t + 1) * 128, :]), writes=['xres'], dma='xres')
            for n in range(2):
                A('vector', lambda e, n=n: e.scalar_tensor_tensor(out=res[:, n * 512:(n + 1) * 512], in0=xres[:, n * 512:(n + 1) * 512],
                                                                  scalar=float(ALPHA), in1=pg[n][:, 0:512], op0=ALU.mult, op1=ALU.add),
                  ['xres', ('pg', n)], ['res'])
            if dbg_t == t:
                A('sync', lambda e: e.dma_start(out=dy16_d, in_=y16[:]), reads=['y16'], dma='dbg')
                A('sync', lambda e: e.dma_start(out=dres_d, in_=res[:]), reads=['res'], dma='dbg')
            layer_norm(A, res, sq, sm, pbc, D, 2 * D, outt[t % 2], ('outt', t % 2))
            A('sync', lambda e: e.dma_start(out=y_d[t * 128:(t + 1) * 128, :], in_=outt[t % 2][:]), reads=[('outt', t % 2)], dma=('outt', t % 2))

        LAG = 3
        for t in range(NT):
            project(t)
            if t >= LAG:
                attend(t - LAG)
        for t in range(max(NT - LAG, 0), NT):
            attend(t)
        stats = P.emit()
    return stats


def layer_norm(A, res, sq, sm, pbc, goff, boff, out_t, out_key):
    A('vector', lambda e: e.tensor_reduce(out=sm[:, 8:9], in_=res[:], axis=AX.X, op=ALU.add), ['res'], ['ln_s'])
    A('vector', lambda e: e.tensor_scalar(out=sm[:, 9:10], in0=sm[:, 8:9], scalar1=1.0 / D, scalar2=None, op0=ALU.mult), ['ln_s'], ['ln_m'])
    A('vector', lambda e: e.tensor_scalar(out=res[:], in0=res[:], scalar1=sm[:, 9:10], scalar2=None, op0=ALU.subtract), ['res', 'ln_m'], ['res'])
    A('vector', lambda e: e.tensor_tensor(out=sq[:], in0=res[:], in1=res[:], op=ALU.mult), ['res'], ['sq'])
    A('vector', lambda e: e.tensor_reduce(out=sm[:, 10:11], in_=sq[:], axis=AX.X, op=ALU.add), ['sq'], ['ln_v'])
    A('vector', lambda e: e.tensor_scalar(out=sm[:, 11:12], in0=sm[:, 10:11], scalar1=1.0 / D, scalar2=LN_EPS, op0=ALU.mult, op1=ALU.add),
      ['ln_v'], ['ln_ve'])
    A('scalar', lambda e: e.activation(out=sm[:, 12:13], in_=sm[:, 11:12], func=AF.Sqrt), ['ln_ve'], ['ln_sd'])
    A('vector', lambda e: e.reciprocal(out=sm[:, 13:14], in_=sm[:, 12:13]), ['ln_sd'], ['ln_r'])
    A('vector', lambda e: e.scalar_tensor_tensor(out=res[:], in0=res[:], scalar=sm[:, 13:14], in1=pbc[:, goff:goff + D], op0=ALU.mult, op1=ALU.mult),
      ['res', 'ln_r', 'pbc'], ['res'])
    A('gpsimd', lambda e: e.tensor_tensor(out=out_t[:], in0=res[:], in1=pbc[:, boff:boff + D], op=ALU.add), ['res', 'pbc'], [out_key])


NS = 12


def emit_peer(nc, pool, T, tag, x_d, y_d, wq_d, keys_d, uv_d, pbc_d, id_d, io_d, e_off):
    NT = T // 128
    with ExitStack() as st:
        def sb(name, shape, dt):
            return st.enter_context(nc.sbuf_tensor(name + tag, shape, dt))

        def ps(name, shape, dt):
            return st.enter_context(nc.psum_tensor(name + tag, shape, dt))

        identf = sb("identf", [128, 128], F32)
        ident = sb("ident_bf", [128, 128], BF16)
        iota = sb("iota", [128, 16], F32)
        pbc = sb("pbc_s", [128, 2 * D], F32)
        wq = sb("wq_bf", [128, 8, 2048], BF16)
        keysb = sb("keys_bf", [128, 16, 128], BF16)
        keysT = sb("keysT", [128, 16, 128], BF16)
        stg = [sb("stg%d" % i, [128, 2048], F32) for i in range(2)]
        xf = [sb("xf%d" % i, [128, D], F32) for i in range(2)]
        xb = sb("xb", [128, D], BF16)
        xT = sb("xT", [128, 8, 128], BF16)
        qT = sb("qT", [128, 16, 128], BF16)
        sc = sb("sc", [128, 2048], F32)
        tmp = sb("tmp", [128, 256], F32)
        stop_ = sb("s_top", [128, 256], F32)
        itop = sb("i_top", [128, 256], U32)
        itopf = sb("i_topf", [128, 256], F32)
        cand = sb("cand", [128, 2048], F32)
        oh = sb("oh", [128, 2048], F32)
        fs = sb("fs", [128, 128], F32)
        fpos = sb("fpos", [128, 128], U32)
        abu = sb("abu", [128, 256], U32)
        abf = sb("abf", [128, 256], F32)
        isel = sb("isel", [128, 256], F32)
        ef = sb("ef", [128, 128], F32)
        eidx = [sb("eidx%d" % i, [128, 128], I32) for i in range(2)]
        gate = [sb("gate%d" % i, [128, 128], F32) for i in range(2)]
        ge = sb("ge", [128, 128], F32)
        gs = sb("gs", [128, 16], F32)
        hraw = sb("hraw", [128, 128], F32)
        aa = sb("aa", [128, 128], F32)
        uv = sb("uvring", [128, NS, 2048], BF16)
        diag = [sb("diag%d" % i, [128, 4, 128], BF16) for i in range(2)]
        junk = sb("junk", [128, D], BF16)
        res = sb("res", [128, D], F32)
        sq = sb("sq", [128, D], F32)
        sm = sb("sm", [128, 16], F32)
        outt = [sb("outt%d" % i, [128, D], F32) for i in range(2)]

        pT = ps("pT", [128, 1024], BF16)
        pg = [ps("pg%d" % i, [128, 512], F32) for i in range(2)]
        pS = [ps("pS%d" % i, [128, 512], F32) for i in range(2)]
        pacc = [ps("pacc%d" % i, [128, 512], F32) for i in range(2)]

        P = Prog(nc, pool)
        A = P.add
        cvt_rr = [0]

        def convert(out_ap, in_ap, reads, writes):
            e = ['vector', 'gpsimd', 'scalar'][cvt_rr[0] % 3]
            cvt_rr[0] += 1
            if e == 'scalar':
                A(e, lambda g: g.copy(out=out_ap, in_=in_ap), reads, writes)
            else:
                A(e, lambda g: g.tensor_copy(out=out_ap, in_=in_ap), reads, writes)

        A('sync', lambda e: e.dma_start(out=identf[:], in_=id_d), writes=['identf'], dma='identf')
        A('vector', lambda e: e.tensor_copy(out=ident[:], in_=identf[:]), ['identf'], ['ident'])
        A('sync', lambda e: e.dma_start(out=iota[:], in_=io_d), writes=['iota'], dma='iota')
        A('sync', lambda e: e.dma_start(out=pbc[:], in_=pbc_d), writes=['pbc'], dma='pbc')
        for k in range(8):
            s = k % 2
            A('sync', lambda e, s=s, k=k: e.dma_start(out=stg[s][:], in_=wq_d[k * 128:(k + 1) * 128, :]), writes=[('stg', s)], dma=('stg', s))
            convert(wq[:, k, :], stg[s][:], [('stg', s)], ['wq'])
        A('sync', lambda e: e.dma_start(out=stg[0][:].rearrange("k (m c) -> k m c", c=128), in_=keys_d.rearrange("m k c -> k m c")),
          writes=[('stg', 0)], dma=('stg', 0))
        A('vector', lambda e: e.tensor_copy(out=keysb[:].rearrange("p a b -> p (a b)"), in_=stg[0][:]), [('stg', 0)], ['keysb'])
        for half in range(2):
            for j in range(8):
                m = half * 8 + j
                A('tensor', lambda e, m=m, j=j: e.transpose(out=pT[:, j * 128:(j + 1) * 128], in_=keysb[:, m, :], identity=ident[:]),
                  ['keysb', 'ident'], ['pT'])
            A('scalar', lambda e, half=half: e.copy(out=keysT[:, half * 8:(half + 1) * 8, :].rearrange("p a b -> p (a b)"), in_=pT[:]),
              ['pT'], ['keysT'])

        def top16(seg, n, vals, idxs, o, use_tmp):
            A('vector', lambda e: e.max(out=vals[:, o:o + 8], in_=seg), ['segsrc'], ['tk_v0'])
            A('vector', lambda e: e.match_replace(out=use_tmp[:, 0:n], in_to_replace=vals[:, o:o + 8], in_values=seg, imm_value=-1e30),
              ['segsrc', 'tk_v0'], ['tk_tmp'])
            A('vector', lambda e: e.max(out=vals[:, o + 8:o + 16], in_=use_tmp[:, 0:n]), ['tk_tmp'], ['tk_v1'])
            A('vector', lambda e: e.max_index(out=idxs[:, o:o + 8], in_max=vals[:, o:o + 8], in_values=seg), ['segsrc', 'tk_v0'], ['tk_i'])
            A('vector', lambda e: e.max_index(out=idxs[:, o + 8:o + 16], in_max=vals[:, o + 8:o + 16], in_values=seg), ['segsrc', 'tk_v1'], ['tk_i'])

        hkc = [0]

        def route(t):
            s = t % 2
            A('sync', lambda e: e.dma_start(out=xf[s][:], in_=x_d[t * 128:(t + 1) * 128, :]), writes=[('xf', s)], dma=('xf', s))
            A('gpsimd', lambda e: e.tensor_copy(out=xb[:], in_=xf[s][:]), [('xf', s)], ['xb'])
            for k in range(8):
                A('tensor', lambda e, k=k: e.transpose(out=pT[:, k * 128:(k + 1) * 128], in_=xb[:, k * 128:(k + 1) * 128], identity=ident[:]),
                  ['xb', 'ident'], ['pT'])
            A('scalar', lambda e: e.copy(out=xT[:].rearrange("p a b -> p (a b)"), in_=pT[:]), ['pT'], ['xT'])
            for g4 in range(4):
                b = g4 % 2
                for j in range(4):
                    m = g4 * 4 + j
                    for k in range(8):
                        A('tensor', lambda e, b=b, j=j, m=m, k=k: e.matmul(
                            pg[b][:, j * 128:(j + 1) * 128], wq[:, k, m * 128:(m + 1) * 128], xT[:, k, :], start=(k == 0), stop=(k == 7)),
                          ['xT', 'wq'], [('pg', b)])
                A('scalar', lambda e, b=b, g4=g4: e.copy(out=qT[:, g4 * 4:(g4 + 1) * 4, :].rearrange("p a b -> p (a b)"), in_=pg[b][:, 0:512]),
                  [('pg', b)], [('qT', g4)])
            for g4 in range(4):
                b = g4 % 2
                for j in range(4):
                    m = g4 * 4 + j
                    A('tensor', lambda e, b=b, j=j, m=m: e.matmul(pS[b][:, j * 128:(j + 1) * 128], qT[:, m, :], keysT[:, m, :], start=True, stop=True),
                      [('qT', g4), 'keysT'], [('pS', b)])
                A('vector', lambda e, b=b, g4=g4: e.tensor_copy(out=sc[:, g4 * 512:(g4 + 1) * 512], in_=pS[b][:, 0:512]), [('pS', b)], ['segsrc'])
            for m in range(16):
                top16(sc[:, m * 128:(m + 1) * 128], 128, stop_, itop, m * 16, tmp)
            A('vector', lambda e: e.tensor_copy(out=itopf[:], in_=itop[:]), ['tk_i', 'tk_v0', 'tk_v1'], ['itopf'])
            A('vector', lambda e: e.tensor_tensor(out=mkap(cand, 0, [[256, 8], [16, 16], [1, 16]]),
                                                  in0=mkap(stop_, 0, [[32, 8], [1, 16], [0, 16]]),
                                                  in1=mkap(stop_, 16, [[32, 8], [0, 16], [1, 16]]), op=ALU.add),
              ['tk_v0', 'tk_v1', 'itopf'], ['segsrc', 'cand'])
            for h in range(8):
                top16(cand[:, h * 256:(h + 1) * 256], 256, fs, fpos, h * 16, tmp)
            A('vector', lambda e: e.tensor_scalar(out=abu[:, 0:128], in0=fpos[:], scalar1=4, scalar2=None, op0=ALU.logical_shift_right),
              ['tk_i', 'tk_v0', 'tk_v1'], ['abu0'])
            A('vector', lambda e: e.tensor_scalar(out=abu[:, 128:256], in0=fpos[:], scalar1=15, scalar2=None, op0=ALU.bitwise_and),
              ['tk_i'], ['abu1'])
            A('vector', lambda e: e.tensor_copy(out=abf[:], in_=abu[:]), ['abu0', 'abu1'], ['abf'])
            for p in range(2):
                A('vector', lambda e, p=p: e.tensor_tensor(out=mkap(oh, 0, [[256, 8], [16, 16], [1, 16]]),
                                                           in0=mkap(abf, p * 128, [[16, 8], [1, 16], [0, 16]]),
                                                           in1=mkap(iota, 0, [[0, 8], [0, 16], [1, 16]]), op=ALU.is_equal),
                  ['abf', 'iota'], ['oh'])
                A('vector', lambda e, p=p: e.tensor_tensor(out=mkap(oh, 0, [[256, 8], [16, 16], [1, 16]]),
                                                           in0=mkap(oh, 0, [[256, 8], [16, 16], [1, 16]]),
                                                           in1=mkap(itopf, p * 16, [[32, 8], [0, 16], [1, 16]]), op=ALU.mult),
                  ['oh', 'itopf'], ['oh'])
                A('vector', lambda e, p=p: e.tensor_reduce(out=mkap(isel, p * 128, [[16, 8], [1, 16]]),
                                                           in_=mkap(oh, 0, [[256, 8], [16, 16], [1, 16]]), axis=AX.X, op=ALU.add),
                  ['oh'], [('isel', p)])
            A('vector', lambda e: e.scalar_tensor_tensor(out=ef[:], in0=isel[:, 0:128], scalar=128.0, in1=isel[:, 128:256], op0=ALU.mult, op1=ALU.add),
              [('isel', 0), ('isel', 1)], ['ef'])
            if e_off:
                A('vector', lambda e: e.tensor_scalar(out=ef[:], in0=ef[:], scalar1=float(e_off), scalar2=None, op0=ALU.add), ['ef'], ['ef'])
            A('vector', lambda e: e.tensor_copy(out=eidx[s][:], in_=ef[:]), ['ef'], [('eidx', s)])
            A('vector', lambda e: e.tensor_tensor(out=mkap(ge, 0, [[16, 8], [1, 16]]), in0=mkap(fs, 0, [[16, 8], [1, 16]]),
                                                  in1=mkap(fs, 0, [[16, 8], [0, 16]]), op=ALU.subtract), ['tk_v0', 'tk_v1', 'abu0'], ['ge'])
            A('scalar', lambda e: e.activation(out=ge[:], in_=ge[:], func=AF.Exp), ['ge'], ['ge'])
            A('vector', lambda e: e.tensor_reduce(out=gs[:, 0:8], in_=mkap(ge, 0, [[16, 8], [1, 16]]), axis=AX.X, op=ALU.add), ['ge'], ['gs0'])
            A('vector', lambda e: e.reciprocal(out=gs[:, 8:16], in_=gs[:, 0:8]), ['gs0'], ['gs1'])
            A('vector', lambda e: e.tensor_tensor(out=mkap(gate[s], 0, [[16, 8], [1, 16]]), in0=mkap(ge, 0, [[16, 8], [1, 16]]),
                                                  in1=mkap(gs, 8, [[1, 8], [0, 16]]), op=ALU.mult), ['ge', 'gs1'], [('gate', s)])

        SKIP_DVE = os.environ.get("PEER_SKIP_DVE") == "1"
        SKIP_DMA = os.environ.get("PEER_SKIP_DMA") == "1"

        def experts(t):
            s = t % 2
            for g in range(32):
                slots = []
                for j in range(4):
                    hk = g * 4 + j
                    sl = hkc[0] % NS
                    hkc[0] += 1
                    slots.append(sl)
                    if not SKIP_DMA:
                        A('gpsimd', lambda e, hk=hk, sl=sl: e.indirect_dma_start(
                            out=uv[:, sl, :], out_offset=None, in_=uv_d,
                            in_offset=bass.IndirectOffsetOnAxis(ap=eidx[s][:, hk:hk + 1], axis=0)),
                          reads=[('eidx', s)], writes=[('uv', sl)], dma=('uv', sl))
                if SKIP_DVE:
                    continue
                for j in range(4):
                    hk = g * 4 + j
                    sl = slots[j]
                    A('vector', lambda e, hk=hk, sl=sl: e.scalar_tensor_tensor(
                        out=junk[:], in0=uv[:, sl, 0:D], scalar=1.0, in1=xf[s][:], op0=ALU.mult, op1=ALU.mult,
                        accum_out=hraw[:, hk:hk + 1]), [('uv', sl), ('xf', s)], [('hraw', g)])
                A('scalar', lambda e, g=g: e.activation(out=aa[:, g * 4:(g + 1) * 4], in_=hraw[:, g * 4:(g + 1) * 4], func=AF.Gelu),
                  [('hraw', g)], [('aa', g)])
                A('vector', lambda e, g=g: e.tensor_tensor(out=aa[:, g * 4:(g + 1) * 4], in0=aa[:, g * 4:(g + 1) * 4],
                                                           in1=gate[s][:, g * 4:(g + 1) * 4], op=ALU.mult),
                  [('aa', g), ('gate', s)], [('aa', g)])
                d = g % 2
                A('vector', lambda e, g=g, d=d: e.tensor_tensor(out=diag[d][:], in0=mkap(identf, 0, [[0, 4], [1, 128]]),
                                                                in1=mkap(aa, g * 4, [[1, 4], [0, 128]]), op=ALU.mult),
                  [('aa', g), 'identf'], [('diag', d)])
                for j in range(4):
                    hk = g * 4 + j
                    sl = slots[j]
                    for n in range(2):
                        A('tensor', lambda e, hk=hk, sl=sl, j=j, n=n, d=d: e.matmul(
                            pacc[n][:, 0:512], diag[d][:, j, :], uv[:, sl, D + n * 512:D + (n + 1) * 512],
                            start=(hk == 0), stop=(hk == 127)), [('diag', d), ('uv', sl)], [('pacc', n)])
            for n in range(2):
                A('vector', lambda e, n=n: e.scalar_tensor_tensor(out=res[:, n * 512:(n + 1) * 512], in0=xf[s][:, n * 512:(n + 1) * 512],
                                                                  scalar=float(ALPHA), in1=pacc[n][:, 0:512], op0=ALU.mult, op1=ALU.add),
                  [('xf', s), ('pacc', n)], ['res'])
            layer_norm(A, res, sq, sm, pbc, 0, D, outt[s], ('outt', s))
            A('sync', lambda e: e.dma_start(out=y_d[t * 128:(t + 1) * 128, :], in_=outt[s][:]), reads=[('outt', s)], dma=('outt', s))

        route(0)
        for t in range(NT):
            if t + 1 < NT:
                route(t + 1)
            experts(t)
        stats = P.emit()
    return stats


_CACHE = {}


def build_fused(T, depth):
    geo, masks = _na_geometry(T)
    NM = masks.shape[0]
    nc = bass.Bass("TRN2", target_bir_lowering=False)
    x_d = nc.dram_tensor("x", [T, D], F32, kind="ExternalInput").ap()
    win_d = nc.dram_tensor("win", [depth, D, PROJX], F32, kind="ExternalInput").ap()
    wo_d = nc.dram_tensor("wo", [depth, D, D], F32, kind="ExternalInput").ap()
    bA_d = nc.dram_tensor("biasA", [8, 128, 384], F32, kind="ExternalInput").ap()
    bB_d = nc.dram_tensor("biasB", [depth, 8, 7, 128, 128], F32, kind="ExternalInput").ap()
    mB_d = nc.dram_tensor("maskB", [NM, 128, 128], F32, kind="ExternalInput").ap()
    pa_d = nc.dram_tensor("pbcA", [depth, 128, 3 * D + 8], F32, kind="ExternalInput").ap()
    pp_d = nc.dram_tensor("pbcP", [depth, 128, 2 * D], F32, kind="ExternalInput").ap()
    id_d = nc.dram_tensor("ident", [128, 128], F32, kind="ExternalInput").ap()
    io_d = nc.dram_tensor("iota16", [128, 16], F32, kind="ExternalInput").ap()
    wq_d = nc.dram_tensor("wq", [depth, D, 2048], F32, kind="ExternalInput").ap()
    keys_d = nc.dram_tensor("keys", [depth, 16, 128, 128], F32, kind="ExternalInput").ap()
    uv_d = nc.dram_tensor("uv", [depth * 16384, 2048], F32, kind="ExternalInput").ap()
    y_d = nc.dram_tensor("y", [T, D], F32, kind="ExternalOutput").ap()
    scr_a = nc.dram_tensor("scr_a", [T, D], F32).ap()
    scr_b = nc.dram_tensor("scr_b", [T, D], F32).ap()
    stats = []
    with ExitStack() as gs:
        dmas = [DmaSems(nc, gs, 24, "x"), DmaSems(nc, gs, 24, "y")]
        cur = x_d
        for l in range(depth):
            dma = dmas[0] if l < 2 else dmas[1]
            pool = SemPool(nc, gs, dma, "a%d" % l)
            stats.append(emit_attn(nc, pool, T, "_a%d" % l, cur, scr_a, win_d[l], wo_d[l], bA_d, bB_d[l], mB_d, pa_d[l], id_d, geo, NM))
            pool = SemPool(nc, gs, dma, "p%d" % l)
            dst = y_d if l == depth - 1 else scr_b
            stats.append(emit_peer(nc, pool, T, "_p%d" % l, scr_a, dst, wq_d[l], keys_d[l], uv_d, pp_d[l], id_d, io_d, l * 16384))
            cur = scr_b
    return nc, stats, masks


def host_inputs(depth, w_in, w_o, attn_sink, na_rpb, t5_table, gnorm_a, gnorm_b,
                ln1_g, ln1_b, ln2_g, ln2_b, peer_wq, peer_keys, peer_u, peer_v, masks):
    f = np.float32
    w_in = np.asarray(w_in, f)[:depth]
    win_ext = np.ascontiguousarray(np.concatenate([w_in, w_in[:, :, 576:640], w_in[:, :, 512:576]], axis=2))
    pa = np.concatenate([np.asarray(gnorm_a, f)[:depth], np.asarray(gnorm_b, f)[:depth], np.asarray(ln1_g, f)[:depth],
                         np.asarray(ln1_b, f)[:depth], np.asarray(attn_sink, f)[:depth]], axis=1)
    pa = np.ascontiguousarray(np.broadcast_to(pa[:, None, :], (depth, 128, pa.shape[1])))
    pp = np.concatenate([np.asarray(ln2_g, f)[:depth], np.asarray(ln2_b, f)[:depth]], axis=1)
    pp = np.ascontiguousarray(np.broadcast_to(pp[:, None, :], (depth, 128, pp.shape[1])))
    uv = np.ascontiguousarray(np.concatenate([np.asarray(peer_u, f)[:depth], np.asarray(peer_v, f)[:depth]], axis=2)).reshape(depth * 16384, 2048)
    return {
        "win": win_ext, "wo": np.ascontiguousarray(np.asarray(w_o, f)[:depth]),
        "biasA": _biasA_table(np.asarray(t5_table, f)),
        "biasB": np.stack([_biasB_table(np.asarray(na_rpb[l], f)) for l in range(depth)]),
        "maskB": masks, "pbcA": pa, "pbcP": pp,
        "ident": np.eye(128, dtype=f),
        "iota16": np.ascontiguousarray(np.broadcast_to(np.arange(16, dtype=f), (128, 16))),
        "wq": np.ascontiguousarray(np.asarray(peer_wq, f)[:depth]),
        "keys": np.ascontiguousarray(np.asarray(peer_keys, f)[:depth].reshape(depth, 16, 128, 128)),
        "uv": uv,
    }


def kernel(x, w_in, w_o, attn_sink, na_rpb, t5_table, gnorm_a, gnorm_b,
           ln1_g, ln1_b, ln2_g, ln2_b, peer_wq, peer_keys, peer_u, peer_v):
    x = np.asarray(x, np.float32)
    B, T, _ = x.shape
    key = (T, DEPTH)
    if key not in _CACHE:
        _CACHE[key] = build_fused(T, DEPTH)
    nc, _, masks = _CACHE[key]
    shared = host_inputs(DEPTH, w_in, w_o, attn_sink, na_rpb, t5_table, gnorm_a, gnorm_b,
                         ln1_g, ln1_b, ln2_g, ln2_b, peer_wq, peer_keys, peer_u, peer_v, masks)
    in_maps = [dict(shared, x=np.ascontiguousarray(x[b])) for b in range(B)]
    r = run_bass_kernel_spmd(nc, in_maps, core_ids=list(range(B)))
    return np.stack([np.asarray(r.results[b]["y"], np.float32) for b in range(B)], axis=0)
```

```python
import math
import os
from contextlib import ExitStack

import numpy as np
import concourse.bass as bass
import concourse.mybir as mybir
from concourse.bass_utils import run_bass_kernel_spmd

F32 = mybir.dt.float32
BF16 = mybir.dt.bfloat16
U32 = mybir.dt.uint32
I32 = mybir.dt.int32
ALU = mybir.AluOpType
AF = mybir.ActivationFunctionType
AX = mybir.AxisListType

D = 1024
DEPTH = 4
SEQ = 4096
NCORES = 8
ALPHA = (2 * DEPTH) ** 0.25
LN_EPS = 1e-5
PROJX = 2432
NEGM = -30000.0
RING = 8
ENGS = ['sync', 'scalar', 'vector', 'gpsimd', 'tensor']


class DmaSems:
    def __init__(self, nc, stack, n, tag):
        self.sems = [stack.enter_context(nc.semaphore("d%s_%d" % (tag, i))) for i in range(n)]
        self.cnt = [0] * n


class SemPool:
    def __init__(self, nc, stack, dma, tag):
        self.eng_sem = {e: stack.enter_context(nc.semaphore("e%s_%s" % (tag, e))) for e in ENGS if e != 'sync'}
        self.eng_cnt = {e: 0 for e in self.eng_sem}
        self.dma_sems = dma.sems
        self.dma_cnt = dma.cnt


class _Op:
    __slots__ = ('eng', 'fn', 'deps', 'dma', 'ev', 'waits', 'clock')


class Prog:
    def __init__(self, nc, pool):
        self.nc = nc
        self.pool = pool
        self.ops = []
        self.last_write = {}
        self.readers = {}
        self.slot_idx = {}

    def add(self, eng, fn, reads=(), writes=(), dma=None):
        op = _Op()
        op.eng, op.fn, op.dma = eng, fn, dma
        deps = set()
        for r in reads:
            w = self.last_write.get(r)
            if w is not None:
                deps.add(w)
        for r in writes:
            w = self.last_write.get(r)
            if w is not None:
                deps.add(w)
            for x in self.readers.get(r, ()):
                deps.add(x)
        idx = len(self.ops)
        op.deps = deps
        for r in reads:
            self.readers.setdefault(r, []).append(idx)
        for r in writes:
            self.last_write[r] = idx
            self.readers[r] = []
        self.ops.append(op)
        return idx

    def emit(self):
        nc, pool, ops = self.nc, self.pool, self.ops
        for op in ops:
            if op.dma is None:
                pool.eng_cnt[op.eng] += 1
                op.ev = (('E', op.eng), pool.eng_cnt[op.eng])
            else:
                if op.dma not in self.slot_idx:
                    self.slot_idx[op.dma] = len(self.slot_idx)
                    assert len(self.slot_idx) <= len(pool.dma_sems), "too many dma slots"
                si = self.slot_idx[op.dma]
                pool.dma_cnt[si] += 16
                op.ev = (('D', si), pool.dma_cnt[si])
        clock = {e: {} for e in ENGS}
        for op in ops:
            clk = clock[op.eng]
            waits = {}
            for d in sorted(op.deps, reverse=True):
                A = ops[d]
                if A.eng == 'tensor' and op.eng == 'tensor' and A.dma is None and op.dma is None:
                    continue
                k, v = A.ev
                if clk.get(k, 0) >= v:
                    continue
                waits[k] = max(waits.get(k, 0), v)
                for kk, vv in A.clock.items():
                    if clk.get(kk, 0) < vv:
                        clk[kk] = vv
            op.waits = list(waits.items())
            c = dict(clk)
            c[op.ev[0]] = op.ev[1]
            op.clock = c
        final = {}
        for op in ops:
            final[op.ev[0]] = max(final.get(op.ev[0], 0), op.ev[1])

        def sem_of(k):
            return pool.eng_sem[k[1]] if k[0] == 'E' else pool.dma_sems[k[1]]

        by_eng = {e: [op for op in ops if op.eng == e] for e in ENGS}

        def body(ename):
            def f(eng):
                for op in by_eng[ename]:
                    for k, v in op.waits:
                        eng.wait_ge(sem_of(k), v)
                    ins = op.fn(eng)
                    ins.then_inc(sem_of(op.ev[0]), 16 if op.dma is not None else 1)
                clk = clock[ename]
                for k, v in final.items():
                    if clk.get(k, 0) < v:
                        eng.wait_ge(sem_of(k), v)
            return f

        with nc.Block() as block:
            block.sync(body('sync'))
            block.scalar(body('scalar'))
            block.vector(body('vector'))
            block.gpsimd(body('gpsimd'))
            block.tensor(body('tensor'))
        n = len(ops)
        nw = sum(len(op.waits) for op in ops)
        for op in ops:
            op.clock = None
        return n, nw


def mkap(t, off, dims):
    pst = t[:].ap[0][0]
    return bass.AP(t, off, [[pst, 128]] + [list(d) for d in dims])


def _t5_bucket(rel):
    nb = 16
    max_exact = 8
    ret = (rel > 0).astype(np.int32) * nb
    n = np.abs(rel).astype(np.int32)
    nf = np.maximum(n, 1).astype(np.float32)
    large = max_exact + (np.log(nf / np.float32(max_exact)) / np.float32(math.log(128 / max_exact))
                         * np.float32(nb - max_exact)).astype(np.int32)
    large = np.minimum(large, nb - 1)
    return ret + np.where(n < max_exact, n, large)


def _biasA_table(t5_table):
    kp = np.arange(128)[:, None, None]
    c = np.arange(3)[None, :, None]
    q = np.arange(128)[None, None, :]
    rel = (c - 1) * 128 + kp - q
    valid = np.abs(rel) <= 128
    b = t5_table[_t5_bucket(rel)]
    b = np.where(valid[..., None], b, np.float32(NEGM)).astype(np.float32)
    return np.ascontiguousarray(b.transpose(3, 0, 1, 2)).reshape(8, 128, 384)


def _na_geometry(T):
    rows = T // 64
    kh = min(8, rows)
    nt = T // 128

    def rs(r):
        return min(max(r - kh // 2, 0), rows - kh)

    def cs(c):
        return min(max(c - 8, 0), 64 - 16)

    kl = np.arange(128)
    krl, kc = kl // 64, kl % 64
    masks = []
    mask_ids = {}
    geo = []
    csq = np.array([cs(c) for c in range(64)])
    for t in range(nt):
        lo = min(rs(2 * t), rs(2 * t + 1))
        hi = max(rs(2 * t), rs(2 * t + 1)) + kh - 1
        lst = []
        for kt in range(lo // 2, hi // 2 + 1):
            krow = 2 * kt + krl[:, None]
            qrow = 2 * t + (kl // 64)[None, :]
            rsq = np.array([rs(2 * t), rs(2 * t + 1)])[(kl // 64)][None, :]
            qc = (kl % 64)[None, :]
            valid = (krow >= rsq) & (krow < rsq + kh) & (kc[:, None] >= csq[qc]) & (kc[:, None] < csq[qc] + 16)
            m = np.where(valid, np.float32(0.0), np.float32(NEGM)).astype(np.float32)
            key = m.tobytes()
            if key not in mask_ids:
                mask_ids[key] = len(masks)
                masks.append(m)
            lst.append((kt, kt - t, mask_ids[key]))
        geo.append(lst)
    return geo, np.stack(masks)


def _biasB_table(rpb_l):
    kl = np.arange(128)
    krl, kc = kl // 64, kl % 64
    out = np.zeros((8, 7, 128, 128), np.float32)
    for di, delta in enumerate(range(-3, 4)):
        dr = 2 * delta + krl[:, None] - krl[None, :] + 7
        dc = np.clip(kc[:, None] - kc[None, :], -15, 15) + 15
        ok = (dr >= 0) & (dr <= 14)
        g = rpb_l[:, np.clip(dr, 0, 14), dc]
        out[:, di] = np.where(ok[None], g, np.float32(0.0))
    return out


def emit_attn(nc, pool, T, tag, x_d, y_d, win_d, wo_d, bA_d, bB_d, mB_d, pbc_d, id_d, geo, NM, uv_src=None, uv_dst=None):
    NT = T // 128
    dbg_t = None
    with ExitStack() as st:
        def sb(name, shape, dt):
            return st.enter_context(nc.sbuf_tensor(name + tag, shape, dt))

        def ps(name, shape, dt):
            return st.enter_context(nc.psum_tensor(name + tag, shape, dt))

        identf = sb("identf", [128, 128], F32)
        ident = sb("ident_bf", [128, 128], BF16)
        pbc = sb("pbc_s", [128, 3 * D + 8], F32)
        sinkexp = sb("sinkexp", [128, 8], F32)
        biasA = sb("biasA_bf", [128, 8, 384], BF16)
        biasB = sb("biasB_bf", [128, 8, 7, 128], BF16)
        maskB = sb("maskB_bf", [128, NM, 128], BF16)
        win = sb("win_bf", [128, 8, PROJX], BF16)
        wo = sb("wo_bf", [128, 8, D], BF16)
        stg = [sb("stg%d" % i, [128, PROJX], F32) for i in range(2)]
        kring = sb("kring", [128, 6, RING * 128], BF16)
        vring = sb("vring", [128, RING, 10, 66], BF16)
        qring = sb("qring", [128, 4, 1024], BF16)
        xf = [sb("xf%d" % i, [128, D], F32) for i in range(2)]
        xb = sb("xb", [128, D], BF16)
        xT = sb("xT", [128, 8, 128], BF16)
        eA = [sb("eA%d" % i, [128, 384], BF16) for i in range(2)]
        eB = [sb("eB%d" % i, [128, 640], BF16) for i in range(2)]
        ya = sb("ya", [128, D], F32)
        sq = sb("sq", [128, D], F32)
        y16 = sb("y16", [128, D], BF16)
        yT = sb("yT", [128, 8, 128], BF16)
        xres = sb("xres", [128, D], F32)
        res = sb("res", [128, D], F32)
        outt = [sb("outt%d" % i, [128, D], F32) for i in range(2)]
        dn = sb("dn", [128, 16], F32)
        rd = sb("rd", [128, 16], F32)
        sm = sb("sm", [128, 16], F32)
        cvo = [sb("cvo%d" % i, [128, 2048], BF16) for i in range(2)]

        pT = ps("pT", [128, 1024], BF16)
        pg = [ps("pg%d" % i, [128, 512], F32) for i in range(2)]
        pSA = ps("pSA", [128, 512], F32)
        pSB = [ps("pSB%d" % i, [128, 512], F32) for i in range(2)]
        pP = [ps("pP%d" % i, [128, 512], F32) for i in range(2)]

        P = Prog(nc, pool)
        A = P.add
        cvt_rr = [0]
        if dbg_t is not None:
            dq_d = nc.dram_tensor("dbg_q", [128, 1024], BF16, kind="ExternalOutput").ap()
            dk_d = nc.dram_tensor("dbg_k", [128, 6, RING * 128], BF16, kind="ExternalOutput").ap()
            dv_d = nc.dram_tensor("dbg_v", [128, RING, 10, 66], BF16, kind="ExternalOutput").ap()
            dya_d = nc.dram_tensor("dbg_ya", [128, 1024], F32, kind="ExternalOutput").ap()
            dy16_d = nc.dram_tensor("dbg_y16", [128, 1024], BF16, kind="ExternalOutput").ap()
            dres_d = nc.dram_tensor("dbg_res", [128, 1024], F32, kind="ExternalOutput").ap()
            dxT_d = nc.dram_tensor("dbg_xT", [128, 8, 128], BF16, kind="ExternalOutput").ap()

        def convert(out_ap, in_ap, reads, writes):
            e = ['vector', 'gpsimd', 'scalar'][cvt_rr[0] % 3]
            cvt_rr[0] += 1
            if e == 'scalar':
                A(e, lambda g: g.copy(out=out_ap, in_=in_ap), reads, writes)
            else:
                A(e, lambda g: g.tensor_copy(out=out_ap, in_=in_ap), reads, writes)

        A('sync', lambda e: e.dma_start(out=identf[:], in_=id_d), writes=['identf'], dma='identf')
        A('vector', lambda e: e.tensor_copy(out=ident[:], in_=identf[:]), ['identf'], ['ident'])
        A('sync', lambda e: e.dma_start(out=pbc[:], in_=pbc_d), writes=['pbc'], dma='pbc')
        A('scalar', lambda e: e.activation(out=sinkexp[:], in_=pbc[:, 3 * D:3 * D + 8], func=AF.Exp), ['pbc'], ['sinkexp'])
        A('vector', lambda e: e.memset(vring[:], 1.0), [], [('v', i) for i in range(RING)])
        si = [0]

        def stage_load(dst_view, src_ap):
            s = si[0] % 2
            si[0] += 1
            A('sync', lambda e: e.dma_start(out=dst_view(stg[s]), in_=src_ap), writes=[('stg', s)], dma=('stg', s))
            return s

        for h in range(8):
            s = stage_load(lambda t: t[:, 0:384], bA_d[h])
            convert(biasA[:, h, :], stg[s][:, 0:384], [('stg', s)], ['biasA'])
        m0 = 0
        while m0 < NM:
            m1 = min(NM, m0 + 19)
            n = m1 - m0
            s = stage_load(lambda t, n=n: t[:, 0:n * 128].rearrange("k (m q) -> k m q", q=128),
                           mB_d[m0:m1].rearrange("m k q -> k m q"))
            convert(maskB[:, m0:m1, :], stg[s][:, 0:n * 128].rearrange("k (m q) -> k m q", q=128), [('stg', s)], ['maskB'])
            m0 = m1
        for h in range(8):
            s = stage_load(lambda t: t[:, 0:896].rearrange("k (m q) -> k m q", q=128),
                           bB_d[h].rearrange("m k q -> k m q"))
            convert(biasB[:, h, :, :], stg[s][:, 0:896].rearrange("k (m q) -> k m q", q=128), [('stg', s)], ['biasB'])
        for k in range(8):
            s = stage_load(lambda t: t[:], win_d[k * 128:(k + 1) * 128, :])
            convert(win[:, k, :], stg[s][:], [('stg', s)], ['win'])
        for k in range(8):
            s = stage_load(lambda t: t[:, 0:D], wo_d[k * 128:(k + 1) * 128, :])
            convert(wo[:, k, :], stg[s][:, 0:D], [('stg', s)], ['wo'])

        pgi = [0]

        def next_pg():
            i = pgi[0] % 2
            pgi[0] += 1
            return i

        projected = set()

        def project(t):
            projected.add(t)
            s = t % 2
            ks = t % RING
            qs = t % 4
            A('sync', lambda e: e.dma_start(out=xf[s][:], in_=x_d[t * 128:(t + 1) * 128, :]), writes=[('xf', s)], dma=('xf', s))
            A('vector', lambda e: e.tensor_copy(out=xb[:], in_=xf[s][:]), [('xf', s)], ['xb'])
            for k in range(8):
                A('tensor', lambda e, k=k: e.transpose(out=pT[:, k * 128:(k + 1) * 128], in_=xb[:, k * 128:(k + 1) * 128], identity=ident[:]),
                  ['xb', 'ident'], ['pT'])
            A('scalar', lambda e: e.copy(out=xT[:].rearrange("p a b -> p (a b)"), in_=pT[:]), ['pT'], ['xT'])
            if dbg_t == t:
                A('sync', lambda e: e.dma_start(out=dxT_d, in_=xT[:]), reads=['xT'], dma='dbg')
            groups = [
                ('q', [0, 128, 256, 384], 0),
                ('q', [768, 896, 1024, 1152], 4),
                ('kb', [1280, 1408, 1536, 1664], 1),
                ('ka', [512, 2304], 0),
            ]
            for gi, (kind, cols, j0) in enumerate(groups):
                b = next_pg()
                for j, col in enumerate(cols):
                    for k in range(8):
                        A('tensor', lambda e, b=b, j=j, col=col, k=k: e.matmul(
                            pg[b][:, j * 128:(j + 1) * 128], win[:, k, col:col + 128], xT[:, k, :],
                            start=(k == 0), stop=(k == 7)), ['xT', 'win'], [('pg', b)])
                if kind == 'q':
                    if gi == 0:
                        A('scalar', lambda e, b=b, j0=j0: e.mul(out=qring[:, qs, j0 * 128:(j0 + 4) * 128], in_=pg[b][:, 0:512], mul=0.125),
                          [('pg', b)], [('q', qs)])
                    else:
                        A('vector', lambda e, b=b, j0=j0: e.tensor_scalar(out=qring[:, qs, j0 * 128:(j0 + 4) * 128], in0=pg[b][:, 0:512],
                                                                          scalar1=0.125, scalar2=None, op0=ALU.mult),
                          [('pg', b)], [('q', qs)])
                elif kind == 'kb':
                    A('scalar', lambda e, b=b: e.copy(out=kring[:, 1:5, ks * 128:(ks + 1) * 128],
                                                      in_=pg[b][:, 0:512].rearrange("p (a b) -> p a b", b=128)),
                      [('pg', b)], [('k', ks)])
                else:
                    A('vector', lambda e, b=b: e.tensor_copy(out=kring[:, 0, ks * 128:(ks + 1) * 128], in_=pg[b][:, 0:128]),
                      [('pg', b)], [('k', ks)])
                    A('vector', lambda e, b=b: e.tensor_copy(out=kring[:, 5, ks * 128:(ks + 1) * 128], in_=pg[b][:, 128:256]),
                      [('pg', b)], [('k', ks)])
            for (col, n, h0, nh) in [(640, 128, 0, 2), (1792, 512, 2, 8)]:
                b = next_pg()
                for k in range(8):
                    A('tensor', lambda e, b=b, col=col, n=n, k=k: e.matmul(
                        pg[b][:, 0:n], xT[:, k, :], win[:, k, col:col + n], start=(k == 0), stop=(k == 7)),
                      ['xT', 'win'], [('pg', b)])
                eng = 'scalar' if nh == 8 else 'vector'
                if eng == 'scalar':
                    A('scalar', lambda e, b=b, n=n, h0=h0, nh=nh: e.copy(
                        out=vring[:, ks, h0:h0 + nh, 0:64], in_=pg[b][:, 0:n].rearrange("p (a b) -> p a b", b=64)),
                      [('pg', b)], [('v', ks)])
                else:
                    A('vector', lambda e, b=b, n=n, h0=h0, nh=nh: e.tensor_copy(
                        out=vring[:, ks, h0:h0 + nh, 0:64], in_=pg[b][:, 0:n].rearrange("p (a b) -> p a b", b=64)),
                      [('pg', b)], [('v', ks)])

        ecnt = [0]

        def attend(t):
            qs = t % 4
            def head_a(h):
                g = h // 4
                base = (h % 2) * 64
                kch = 0 if (h % 2) == g else 5
                blocks = [c for c in (0, 1, 2) if 0 <= t + c - 1 < NT]
                for c in blocks:
                    ks = (t + c - 1) % RING
                    A('tensor', lambda e, c=c, ks=ks: e.matmul(
                        pSA[:, c * 128:(c + 1) * 128], kring[base:base + 64, kch, ks * 128:(ks + 1) * 128],
                        qring[base:base + 64, qs, (h // 2) * 128:(h // 2 + 1) * 128], start=True, stop=False),
                      [('k', ks), ('q', qs)], ['pSA'])
                    A('tensor', lambda e, c=c: e.matmul(
                        pSA[:, c * 128:(c + 1) * 128], ident[:], biasA[:, h, c * 128:(c + 1) * 128], start=False, stop=True),
                      ['ident', 'biasA'], ['pSA'])
                c0, c1 = blocks[0], blocks[-1]
                es = ecnt[0] % 2
                ecnt[0] += 1
                A('scalar', lambda e, es=es: e.activation(out=eA[es][:, c0 * 128:(c1 + 1) * 128], in_=pSA[:, c0 * 128:(c1 + 1) * 128], func=AF.Exp),
                  ['pSA'], [('eA', es)])
                r = h % 2
                for c in blocks:
                    ks = (t + c - 1) % RING
                    A('tensor', lambda e, c=c, ks=ks, es=es: e.matmul(
                        pP[r][:, 0:65], eA[es][:, c * 128:(c + 1) * 128], vring[:, ks, g, 0:65],
                        start=(c == c0), stop=(c == c1)), [('eA', es), ('v', ks)], [('pP', r)])
                A('vector', lambda e: e.tensor_scalar(out=dn[:, h:h + 1], in0=pP[r][:, 64:65], scalar1=sinkexp[:, h:h + 1],
                                                      scalar2=None, op0=ALU.add), [('pP', r), 'sinkexp'], [('dn', h)])
                A('vector', lambda e: e.reciprocal(out=rd[:, h:h + 1], in_=dn[:, h:h + 1]), [('dn', h)], [('rd', h)])
                A('vector', lambda e: e.tensor_scalar(out=ya[:, h * 64:(h + 1) * 64], in0=pP[r][:, 0:64], scalar1=rd[:, h:h + 1],
                                                      scalar2=None, op0=ALU.mult), [('pP', r), ('rd', h)], [('ya', 0)])
            for h in range(8):
                head_a(h)
            kts = geo[t]
            nb = len(kts)
            assert all(kt in projected for kt, _, _ in kts) and nb <= 5

            def head_b(h):
                base = (h % 2) * 64
                qch = 4 + h // 2
                kch = 1 + h // 2

                def region(i):
                    return pSB[0][:, i * 128:(i + 1) * 128] if i < 4 else pSB[1][:, (i - 4) * 128:(i - 3) * 128]

                for i, (kt, delta, mid) in enumerate(kts):
                    ks = kt % RING
                    A('tensor', lambda e, i=i, ks=ks: e.matmul(
                        region(i), kring[base:base + 64, kch, ks * 128:(ks + 1) * 128],
                        qring[base:base + 64, qs, qch * 128:(qch + 1) * 128], start=True, stop=False),
                      [('k', ks), ('q', qs)], ['pSB'])
                    A('tensor', lambda e, i=i, delta=delta: e.matmul(
                        region(i), ident[:], biasB[:, h, delta + 3, :], start=False, stop=False), ['ident', 'biasB'], ['pSB'])
                    A('tensor', lambda e, i=i, mid=mid: e.matmul(
                        region(i), ident[:], maskB[:, mid, :], start=False, stop=True), ['ident', 'maskB'], ['pSB'])
                es = ecnt[0] % 2
                ecnt[0] += 1
                n4 = min(nb, 4)
                A('scalar', lambda e, es=es: e.activation(out=eB[es][:, 0:n4 * 128], in_=pSB[0][:, 0:n4 * 128], func=AF.Exp),
                  ['pSB'], [('eB', es)])
                if nb > 4:
                    A('scalar', lambda e, es=es: e.activation(out=eB[es][:, 512:512 + (nb - 4) * 128], in_=pSB[1][:, 0:(nb - 4) * 128], func=AF.Exp),
                      ['pSB'], [('eB', es)])
                r = h % 2
                for i, (kt, delta, mid) in enumerate(kts):
                    ks = kt % RING
                    A('tensor', lambda e, i=i, ks=ks, es=es: e.matmul(
                        pP[r][:, 0:65], eB[es][:, i * 128:(i + 1) * 128], vring[:, ks, 2 + h, 0:65],
                        start=(i == 0), stop=(i == nb - 1)), [('eB', es), ('v', ks)], [('pP', r)])
                A('vector', lambda e: e.reciprocal(out=rd[:, 8 + h:9 + h], in_=pP[r][:, 64:65]), [('pP', r)], [('rd', 8 + h)])
                A('vector', lambda e: e.tensor_scalar(out=ya[:, 512 + h * 64:512 + (h + 1) * 64], in0=pP[r][:, 0:64],
                                                      scalar1=rd[:, 8 + h:9 + h], scalar2=None, op0=ALU.mult),
                  [('pP', r), ('rd', 8 + h)], [('ya', 1)])
            for h in range(8):
                head_b(h)
            if dbg_t == t:
                A('sync', lambda e: e.dma_start(out=dq_d, in_=qring[:, qs, :]), reads=[('q', qs)], dma='dbg')
                A('sync', lambda e: e.dma_start(out=dk_d, in_=kring[:]), reads=[('k', i) for i in range(RING)], dma='dbg')
                A('sync', lambda e: e.dma_start(out=dv_d, in_=vring[:]), reads=[('v', i) for i in range(RING)], dma='dbg')
                A('sync', lambda e: e.dma_start(out=dya_d, in_=ya[:]), reads=[('ya', 0), ('ya', 1)], dma='dbg')
            A('vector', lambda e: e.tensor_tensor(out=sq[:], in0=ya[:], in1=ya[:], op=ALU.mult), [('ya', 0), ('ya', 1)], ['sq'])
            A('vector', lambda e: e.tensor_reduce(out=sm[:, 0:2], in_=sq[:].rearrange("p (a b) -> p a b", b=512), axis=AX.X, op=ALU.add),
              ['sq'], ['sm01'])
            A('vector', lambda e: e.tensor_scalar(out=sm[:, 2:4], in0=sm[:, 0:2], scalar1=1.0 / 512, scalar2=LN_EPS, op0=ALU.mult, op1=ALU.add),
              ['sm01'], ['sm23'])
            A('scalar', lambda e: e.activation(out=sm[:, 4:6], in_=sm[:, 2:4], func=AF.Sqrt), ['sm23'], ['sm45'])
            A('vector', lambda e: e.reciprocal(out=sm[:, 6:8], in_=sm[:, 4:6]), ['sm45'], ['sm67'])
            for j in range(2):
                A('vector', lambda e, j=j: e.scalar_tensor_tensor(out=y16[:, j * 512:(j + 1) * 512], in0=ya[:, j * 512:(j + 1) * 512],
                                                                  scalar=sm[:, 6 + j:7 + j], in1=pbc[:, j * 512:(j + 1) * 512],
                                                                  op0=ALU.mult, op1=ALU.mult),
                  [('ya', j), 'sm67', 'pbc'], ['y16'])
            for k in range(8):
                A('tensor', lambda e, k=k: e.transpose(out=pT[:, k * 128:(k + 1) * 128], in_=y16[:, k * 128:(k + 1) * 128], identity=ident[:]),
                  ['y16', 'ident'], ['pT'])
            A('scalar', lambda e: e.copy(out=yT[:].rearrange("p a b -> p (a b)"), in_=pT[:]), ['pT'], ['yT'])
            for n in range(2):
                for k in range(8):
                    A('tensor', lambda e, n=n, k=k: e.matmul(pg[n][:, 0:512], yT[:, k, :], wo[:, k, n * 512:(n + 1) * 512],
                                                             start=(k == 0), stop=(k == 7)), ['yT', 'wo'], [('pg', n)])
            A('sync', lambda e: e.dma_start(out=xres[:], in_=x_d[t * 128:(t + 1) * 128, :]), writes=['xres'], dma='xres')
            for n in range(2):
                A('vector', lambda e, n=n: e.scalar_tensor_tensor(out=res[:, n * 512:(n + 1) * 512], in0=xres[:, n * 512:(n + 1) * 512],
                                                                  scalar=float(ALPHA), in1=pg[n][:, 0:512], op0=ALU.mult, op1=ALU.add),
                  ['xres', ('pg', n)], ['res'])
            if dbg_t == t:
                A('sync', lambda e: e.dma_start(out=dy16_d, in_=y16[:]), reads=['y16'], dma='dbg')
                A('sync', lambda e: e.dma_start(out=dres_d, in_=res[:]), reads=['res'], dma='dbg')
            layer_norm(A, res, sq, sm, pbc, D, 2 * D, outt[t % 2], ('outt', t % 2))
            A('sync', lambda e: e.dma_start(out=y_d[t * 128:(t + 1) * 128, :], in_=outt[t % 2][:]), reads=[('outt', t % 2)], dma=('outt', t % 2))

        ccnt = [0]

        def convert_rows(n):
            if uv_src is None:
                return
            srcv = uv_src.rearrange("(p r) c -> p r c", p=128)
            dstv = uv_dst.rearrange("(p r) c -> p r c", p=128)
            for _ in range(n):
                c = ccnt[0]
                if c >= 128:
                    return
                ccnt[0] += 1
                s = c % 2
                A('gpsimd', lambda e, c=c, s=s: e.dma_start(out=stg[s][:, 0:2048], in_=srcv[:, c, :]), writes=[('stg', s)], dma=('stg', s))
                A('gpsimd', lambda e, s=s: e.tensor_copy(out=cvo[s][:], in_=stg[s][:, 0:2048]), [('stg', s)], [('cvo', s)])
                A('gpsimd', lambda e, c=c, s=s: e.dma_start(out=dstv[:, c, :], in_=cvo[s][:]), reads=[('cvo', s)], dma=('cvo', s))

        LAG = 3
        per = -(-128 // NT)
        for t in range(NT):
            project(t)
            convert_rows(per)
            if t >= LAG:
                attend(t - LAG)
        for t in range(max(NT - LAG, 0), NT):
            attend(t)
        convert_rows(128)
        stats = P.emit()
    return stats


def layer_norm(A, res, sq, sm, pbc, goff, boff, out_t, out_key):
    A('vector', lambda e: e.tensor_reduce(out=sm[:, 8:9], in_=res[:], axis=AX.X, op=ALU.add), ['res'], ['ln_s'])
    A('vector', lambda e: e.tensor_scalar(out=sm[:, 9:10], in0=sm[:, 8:9], scalar1=1.0 / D, scalar2=None, op0=ALU.mult), ['ln_s'], ['ln_m'])
    A('vector', lambda e: e.tensor_scalar(out=res[:], in0=res[:], scalar1=sm[:, 9:10], scalar2=None, op0=ALU.subtract), ['res', 'ln_m'], ['res'])
    A('vector', lambda e: e.tensor_tensor(out=sq[:], in0=res[:], in1=res[:], op=ALU.mult), ['res'], ['sq'])
    A('vector', lambda e: e.tensor_reduce(out=sm[:, 10:11], in_=sq[:], axis=AX.X, op=ALU.add), ['sq'], ['ln_v'])
    A('vector', lambda e: e.tensor_scalar(out=sm[:, 11:12], in0=sm[:, 10:11], scalar1=1.0 / D, scalar2=LN_EPS, op0=ALU.mult, op1=ALU.add),
      ['ln_v'], ['ln_ve'])
    A('scalar', lambda e: e.activation(out=sm[:, 12:13], in_=sm[:, 11:12], func=AF.Sqrt), ['ln_ve'], ['ln_sd'])
    A('vector', lambda e: e.reciprocal(out=sm[:, 13:14], in_=sm[:, 12:13]), ['ln_sd'], ['ln_r'])
    A('vector', lambda e: e.scalar_tensor_tensor(out=res[:], in0=res[:], scalar=sm[:, 13:14], in1=pbc[:, goff:goff + D], op0=ALU.mult, op1=ALU.mult),
      ['res', 'ln_r', 'pbc'], ['res'])
    A('gpsimd', lambda e: e.tensor_tensor(out=out_t[:], in0=res[:], in1=pbc[:, boff:boff + D], op=ALU.add), ['res', 'pbc'], [out_key])


NS = 12


def emit_peer(nc, pool, T, tag, x_d, y_d, wq_d, keys_d, uv_d, pbc_d, id_d, io_d, e_off):
    NT = T // 128
    with ExitStack() as st:
        def sb(name, shape, dt):
            return st.enter_context(nc.sbuf_tensor(name + tag, shape, dt))

        def ps(name, shape, dt):
            return st.enter_context(nc.psum_tensor(name + tag, shape, dt))

        identf = sb("identf", [128, 128], F32)
        ident = sb("ident_bf", [128, 128], BF16)
        iota = sb("iota", [128, 16], F32)
        pbc = sb("pbc_s", [128, 2 * D], F32)
        wq = sb("wq_bf", [128, 8, 2048], BF16)
        keysb = sb("keys_bf", [128, 16, 128], BF16)
        keysT = sb("keysT", [128, 16, 128], BF16)
        stg = [sb("stg%d" % i, [128, 2048], F32) for i in range(2)]
        xf = [sb("xf%d" % i, [128, D], F32) for i in range(2)]
        xb = sb("xb", [128, D], BF16)
        xT = sb("xT", [128, 8, 128], BF16)
        qT = sb("qT", [128, 16, 128], BF16)
        sc = sb("sc", [128, 2048], F32)
        tmp = sb("tmp", [128, 256], F32)
        stop_ = sb("s_top", [128, 256], F32)
        itop = sb("i_top", [128, 256], U32)
        itopf = sb("i_topf", [128, 256], F32)
        cand = sb("cand", [128, 2048], F32)
        oh = sb("oh", [128, 2048], F32)
        fs = sb("fs", [128, 128], F32)
        fpos = sb("fpos", [128, 128], U32)
        abu = sb("abu", [128, 256], U32)
        abf = sb("abf", [128, 256], F32)
        isel = sb("isel", [128, 256], F32)
        ef = sb("ef", [128, 128], F32)
        eidx = [sb("eidx%d" % i, [128, 128], I32) for i in range(2)]
        gate = [sb("gate%d" % i, [128, 128], F32) for i in range(2)]
        ge = sb("ge", [128, 128], F32)
        gs = sb("gs", [128, 16], F32)
        hraw = sb("hraw", [128, 128], F32)
        aa = sb("aa", [128, 128], F32)
        uv = sb("uvring", [128, NS, 2048], BF16)
        diag = [sb("diag%d" % i, [128, 4, 128], BF16) for i in range(2)]
        junk = sb("junk", [128, D], BF16)
        res = sb("res", [128, D], F32)
        sq = sb("sq", [128, D], F32)
        sm = sb("sm", [128, 16], F32)
        outt = [sb("outt%d" % i, [128, D], F32) for i in range(2)]

        pT = ps("pT", [128, 1024], BF16)
        pg = [ps("pg%d" % i, [128, 512], F32) for i in range(2)]
        pS = [ps("pS%d" % i, [128, 512], F32) for i in range(2)]
        pacc = [ps("pacc%d" % i, [128, 512], F32) for i in range(2)]

        P = Prog(nc, pool)
        A = P.add
        cvt_rr = [0]

        def convert(out_ap, in_ap, reads, writes):
            e = ['vector', 'gpsimd', 'scalar'][cvt_rr[0] % 3]
            cvt_rr[0] += 1
            if e == 'scalar':
                A(e, lambda g: g.copy(out=out_ap, in_=in_ap), reads, writes)
            else:
                A(e, lambda g: g.tensor_copy(out=out_ap, in_=in_ap), reads, writes)

        A('sync', lambda e: e.dma_start(out=identf[:], in_=id_d), writes=['identf'], dma='identf')
        A('vector', lambda e: e.tensor_copy(out=ident[:], in_=identf[:]), ['identf'], ['ident'])
        A('sync', lambda e: e.dma_start(out=iota[:], in_=io_d), writes=['iota'], dma='iota')
        A('sync', lambda e: e.dma_start(out=pbc[:], in_=pbc_d), writes=['pbc'], dma='pbc')
        for k in range(8):
            s = k % 2
            A('sync', lambda e, s=s, k=k: e.dma_start(out=stg[s][:], in_=wq_d[k * 128:(k + 1) * 128, :]), writes=[('stg', s)], dma=('stg', s))
            convert(wq[:, k, :], stg[s][:], [('stg', s)], ['wq'])
        A('sync', lambda e: e.dma_start(out=stg[0][:].rearrange("k (m c) -> k m c", c=128), in_=keys_d.rearrange("m k c -> k m c")),
          writes=[('stg', 0)], dma=('stg', 0))
        A('vector', lambda e: e.tensor_copy(out=keysb[:].rearrange("p a b -> p (a b)"), in_=stg[0][:]), [('stg', 0)], ['keysb'])
        for half in range(2):
            for j in range(8):
                m = half * 8 + j
                A('tensor', lambda e, m=m, j=j: e.transpose(out=pT[:, j * 128:(j + 1) * 128], in_=keysb[:, m, :], identity=ident[:]),
                  ['keysb', 'ident'], ['pT'])
            A('scalar', lambda e, half=half: e.copy(out=keysT[:, half * 8:(half + 1) * 8, :].rearrange("p a b -> p (a b)"), in_=pT[:]),
              ['pT'], ['keysT'])

        def top16(seg, n, vals, idxs, o, use_tmp):
            A('vector', lambda e: e.max(out=vals[:, o:o + 8], in_=seg), ['segsrc'], ['tk_v0'])
            A('vector', lambda e: e.match_replace(out=use_tmp[:, 0:n], in_to_replace=vals[:, o:o + 8], in_values=seg, imm_value=-1e30),
              ['segsrc', 'tk_v0'], ['tk_tmp'])
            A('vector', lambda e: e.max(out=vals[:, o + 8:o + 16], in_=use_tmp[:, 0:n]), ['tk_tmp'], ['tk_v1'])
            A('vector', lambda e: e.max_index(out=idxs[:, o:o + 8], in_max=vals[:, o:o + 8], in_values=seg), ['segsrc', 'tk_v0'], ['tk_i'])
            A('vector', lambda e: e.max_index(out=idxs[:, o + 8:o + 16], in_max=vals[:, o + 8:o + 16], in_values=seg), ['segsrc', 'tk_v1'], ['tk_i'])

        hkc = [0]

        def route(t):
            s = t % 2
            A('sync', lambda e: e.dma_start(out=xf[s][:], in_=x_d[t * 128:(t + 1) * 128, :]), writes=[('xf', s)], dma=('xf', s))
            A('gpsimd', lambda e: e.tensor_copy(out=xb[:], in_=xf[s][:]), [('xf', s)], ['xb'])
            for k in range(8):
                A('tensor', lambda e, k=k: e.transpose(out=pT[:, k * 128:(k + 1) * 128], in_=xb[:, k * 128:(k + 1) * 128], identity=ident[:]),
                  ['xb', 'ident'], ['pT'])
            A('scalar', lambda e: e.copy(out=xT[:].rearrange("p a b -> p (a b)"), in_=pT[:]), ['pT'], ['xT'])
            yield
            for g4 in range(4):
                b = g4 % 2
                for j in range(4):
                    m = g4 * 4 + j
                    for k in range(8):
                        A('tensor', lambda e, b=b, j=j, m=m, k=k: e.matmul(
                            pg[b][:, j * 128:(j + 1) * 128], wq[:, k, m * 128:(m + 1) * 128], xT[:, k, :], start=(k == 0), stop=(k == 7)),
                          ['xT', 'wq'], [('pg', b)])
                A('scalar', lambda e, b=b, g4=g4: e.copy(out=qT[:, g4 * 4:(g4 + 1) * 4, :].rearrange("p a b -> p (a b)"), in_=pg[b][:, 0:512]),
                  [('pg', b)], [('qT', g4)])
                yield
            for g4 in range(4):
                b = g4 % 2
                for j in range(4):
                    m = g4 * 4 + j
                    A('tensor', lambda e, b=b, j=j, m=m: e.matmul(pS[b][:, j * 128:(j + 1) * 128], qT[:, m, :], keysT[:, m, :], start=True, stop=True),
                      [('qT', g4), 'keysT'], [('pS', b)])
                A('vector', lambda e, b=b, g4=g4: e.tensor_copy(out=sc[:, g4 * 512:(g4 + 1) * 512], in_=pS[b][:, 0:512]), [('pS', b)], ['segsrc'])
                yield
            for m in range(16):
                top16(sc[:, m * 128:(m + 1) * 128], 128, stop_, itop, m * 16, tmp)
                yield
            A('vector', lambda e: e.tensor_copy(out=itopf[:], in_=itop[:]), ['tk_i', 'tk_v0', 'tk_v1'], ['itopf'])
            A('vector', lambda e: e.tensor_tensor(out=mkap(cand, 0, [[256, 8], [16, 16], [1, 16]]),
                                                  in0=mkap(stop_, 0, [[32, 8], [1, 16], [0, 16]]),
                                                  in1=mkap(stop_, 16, [[32, 8], [0, 16], [1, 16]]), op=ALU.add),
              ['tk_v0', 'tk_v1', 'itopf'], ['segsrc', 'cand'])
            yield
            for h in range(8):
                top16(cand[:, h * 256:(h + 1) * 256], 256, fs, fpos, h * 16, tmp)
                yield
            A('vector', lambda e: e.tensor_scalar(out=abu[:, 0:128], in0=fpos[:], scalar1=4, scalar2=None, op0=ALU.logical_shift_right),
              ['tk_i', 'tk_v0', 'tk_v1'], ['abu0'])
            A('vector', lambda e: e.tensor_scalar(out=abu[:, 128:256], in0=fpos[:], scalar1=15, scalar2=None, op0=ALU.bitwise_and),
              ['tk_i'], ['abu1'])
            A('vector', lambda e: e.tensor_copy(out=abf[:], in_=abu[:]), ['abu0', 'abu1'], ['abf'])
            yield
            for p in range(2):
                A('vector', lambda e, p=p: e.tensor_tensor(out=mkap(oh, 0, [[256, 8], [16, 16], [1, 16]]),
                                                           in0=mkap(abf, p * 128, [[16, 8], [1, 16], [0, 16]]),
                                                           in1=mkap(iota, 0, [[0, 8], [0, 16], [1, 16]]), op=ALU.is_equal),
                  ['abf', 'iota'], ['oh'])
                A('vector', lambda e, p=p: e.tensor_tensor(out=mkap(oh, 0, [[256, 8], [16, 16], [1, 16]]),
                                                           in0=mkap(oh, 0, [[256, 8], [16, 16], [1, 16]]),
                                                           in1=mkap(itopf, p * 16, [[32, 8], [0, 16], [1, 16]]), op=ALU.mult),
                  ['oh', 'itopf'], ['oh'])
                A('vector', lambda e, p=p: e.tensor_reduce(out=mkap(isel, p * 128, [[16, 8], [1, 16]]),
                                                           in_=mkap(oh, 0, [[256, 8], [16, 16], [1, 16]]), axis=AX.X, op=ALU.add),
                  ['oh'], [('isel', p)])
                yield
            A('vector', lambda e: e.scalar_tensor_tensor(out=ef[:], in0=isel[:, 0:128], scalar=128.0, in1=isel[:, 128:256], op0=ALU.mult, op1=ALU.add),
              [('isel', 0), ('isel', 1)], ['ef'])
            if e_off:
                A('vector', lambda e: e.tensor_scalar(out=ef[:], in0=ef[:], scalar1=float(e_off), scalar2=None, op0=ALU.add), ['ef'], ['ef'])
            A('vector', lambda e: e.tensor_copy(out=eidx[s][:], in_=ef[:]), ['ef'], [('eidx', s)])
            yield
            A('vector', lambda e: e.tensor_tensor(out=mkap(ge, 0, [[16, 8], [1, 16]]), in0=mkap(fs, 0, [[16, 8], [1, 16]]),
                                                  in1=mkap(fs, 0, [[16, 8], [0, 16]]), op=ALU.subtract), ['tk_v0', 'tk_v1', 'abu0'], ['ge'])
            A('scalar', lambda e: e.activation(out=ge[:], in_=ge[:], func=AF.Exp), ['ge'], ['ge'])
            A('vector', lambda e: e.tensor_reduce(out=gs[:, 0:8], in_=mkap(ge, 0, [[16, 8], [1, 16]]), axis=AX.X, op=ALU.add), ['ge'], ['gs0'])
            A('vector', lambda e: e.reciprocal(out=gs[:, 8:16], in_=gs[:, 0:8]), ['gs0'], ['gs1'])
            A('vector', lambda e: e.tensor_tensor(out=mkap(gate[s], 0, [[16, 8], [1, 16]]), in0=mkap(ge, 0, [[16, 8], [1, 16]]),
                                                  in1=mkap(gs, 8, [[1, 8], [0, 16]]), op=ALU.mult), ['ge', 'gs1'], [('gate', s)])

        SKIP_DVE = os.environ.get("PEER_SKIP_DVE") == "1"
        SKIP_DMA = os.environ.get("PEER_SKIP_DMA") == "1"

        def drain(gen, n):
            for _ in range(n):
                try:
                    next(gen)
                except StopIteration:
                    return

        def experts(t, gen):
            s = t % 2
            for g in range(32):
                drain(gen, 2)
                slots = []
                for j in range(4):
                    hk = g * 4 + j
                    sl = hkc[0] % NS
                    hkc[0] += 1
                    slots.append(sl)
                    if not SKIP_DMA:
                        A('gpsimd', lambda e, hk=hk, sl=sl: e.indirect_dma_start(
                            out=uv[:, sl, :], out_offset=None, in_=uv_d,
                            in_offset=bass.IndirectOffsetOnAxis(ap=eidx[s][:, hk:hk + 1], axis=0)),
                          reads=[('eidx', s)], writes=[('uv', sl)], dma=('uv', sl))
                if SKIP_DVE:
                    continue
                for j in range(4):
                    hk = g * 4 + j
                    sl = slots[j]
                    A('vector', lambda e, hk=hk, sl=sl: e.scalar_tensor_tensor(
                        out=junk[:], in0=uv[:, sl, 0:D], scalar=1.0, in1=xf[s][:], op0=ALU.mult, op1=ALU.mult,
                        accum_out=hraw[:, hk:hk + 1]), [('uv', sl), ('xf', s)], [('hraw', g)])
                A('scalar', lambda e, g=g: e.activation(out=aa[:, g * 4:(g + 1) * 4], in_=hraw[:, g * 4:(g + 1) * 4], func=AF.Gelu),
                  [('hraw', g)], [('aa', g)])
                A('vector', lambda e, g=g: e.tensor_tensor(out=aa[:, g * 4:(g + 1) * 4], in0=aa[:, g * 4:(g + 1) * 4],
                                                           in1=gate[s][:, g * 4:(g + 1) * 4], op=ALU.mult),
                  [('aa', g), ('gate', s)], [('aa', g)])
                d = g % 2
                A('vector', lambda e, g=g, d=d: e.tensor_tensor(out=diag[d][:], in0=mkap(identf, 0, [[0, 4], [1, 128]]),
                                                                in1=mkap(aa, g * 4, [[1, 4], [0, 128]]), op=ALU.mult),
                  [('aa', g), 'identf'], [('diag', d)])
                for j in range(4):
                    hk = g * 4 + j
                    sl = slots[j]
                    for n in range(2):
                        A('tensor', lambda e, hk=hk, sl=sl, j=j, n=n, d=d: e.matmul(
                            pacc[n][:, 0:512], diag[d][:, j, :], uv[:, sl, D + n * 512:D + (n + 1) * 512],
                            start=(hk == 0), stop=(hk == 127)), [('diag', d), ('uv', sl)], [('pacc', n)])
            for n in range(2):
                A('vector', lambda e, n=n: e.scalar_tensor_tensor(out=res[:, n * 512:(n + 1) * 512], in0=xf[s][:, n * 512:(n + 1) * 512],
                                                                  scalar=float(ALPHA), in1=pacc[n][:, 0:512], op0=ALU.mult, op1=ALU.add),
                  [('xf', s), ('pacc', n)], ['res'])
            layer_norm(A, res, sq, sm, pbc, 0, D, outt[s], ('outt', s))
            A('sync', lambda e: e.dma_start(out=y_d[t * 128:(t + 1) * 128, :], in_=outt[s][:]), reads=[('outt', s)], dma=('outt', s))

        for _ in route(0):
            pass
        for t in range(NT):
            gen = route(t + 1) if t + 1 < NT else iter(())
            experts(t, gen)
            for _ in gen:
                pass
        stats = P.emit()
    return stats


_CACHE = {}


def build_fused(T, depth):
    geo, masks = _na_geometry(T)
    NM = masks.shape[0]
    nc = bass.Bass("TRN2", target_bir_lowering=False)
    x_d = nc.dram_tensor("x", [T, D], F32, kind="ExternalInput").ap()
    win_d = nc.dram_tensor("win", [depth, D, PROJX], F32, kind="ExternalInput").ap()
    wo_d = nc.dram_tensor("wo", [depth, D, D], F32, kind="ExternalInput").ap()
    bA_d = nc.dram_tensor("biasA", [8, 128, 384], F32, kind="ExternalInput").ap()
    bB_d = nc.dram_tensor("biasB", [depth, 8, 7, 128, 128], F32, kind="ExternalInput").ap()
    mB_d = nc.dram_tensor("maskB", [NM, 128, 128], F32, kind="ExternalInput").ap()
    pa_d = nc.dram_tensor("pbcA", [depth, 128, 3 * D + 8], F32, kind="ExternalInput").ap()
    pp_d = nc.dram_tensor("pbcP", [depth, 128, 2 * D], F32, kind="ExternalInput").ap()
    id_d = nc.dram_tensor("ident", [128, 128], F32, kind="ExternalInput").ap()
    io_d = nc.dram_tensor("iota16", [128, 16], F32, kind="ExternalInput").ap()
    wq_d = nc.dram_tensor("wq", [depth, D, 2048], F32, kind="ExternalInput").ap()
    keys_d = nc.dram_tensor("keys", [depth, 16, 128, 128], F32, kind="ExternalInput").ap()
    uv_d = nc.dram_tensor("uv", [depth * 16384, 2048], F32, kind="ExternalInput").ap()
    y_d = nc.dram_tensor("y", [T, D], F32, kind="ExternalOutput").ap()
    scr_a = nc.dram_tensor("scr_a", [T, D], F32).ap()
    scr_b = nc.dram_tensor("scr_b", [T, D], F32).ap()
    uvb_d = nc.dram_tensor("uv_bf16", [16384, 2048], BF16).ap()
    stats = []
    with ExitStack() as gs:
        dmas = [DmaSems(nc, gs, 24, "x"), DmaSems(nc, gs, 24, "y")]
        cur = x_d
        for l in range(depth):
            dma = dmas[0] if l < 2 else dmas[1]
            pool = SemPool(nc, gs, dma, "a%d" % l)
            stats.append(emit_attn(nc, pool, T, "_a%d" % l, cur, scr_a, win_d[l], wo_d[l], bA_d, bB_d[l], mB_d, pa_d[l], id_d, geo, NM,
                                   uv_src=uv_d[l * 16384:(l + 1) * 16384, :], uv_dst=uvb_d))
            pool = SemPool(nc, gs, dma, "p%d" % l)
            dst = y_d if l == depth - 1 else scr_b
            stats.append(emit_peer(nc, pool, T, "_p%d" % l, scr_a, dst, wq_d[l], keys_d[l], uvb_d, pp_d[l], id_d, io_d, 0))
            cur = scr_b
    return nc, stats, masks


def host_inputs(depth, w_in, w_o, attn_sink, na_rpb, t5_table, gnorm_a, gnorm_b,
                ln1_g, ln1_b, ln2_g, ln2_b, peer_wq, peer_keys, peer_u, peer_v, masks):
    f = np.float32
    w_in = np.asarray(w_in, f)[:depth]
    win_ext = np.ascontiguousarray(np.concatenate([w_in, w_in[:, :, 576:640], w_in[:, :, 512:576]], axis=2))
    pa = np.concatenate([np.asarray(gnorm_a, f)[:depth], np.asarray(gnorm_b, f)[:depth], np.asarray(ln1_g, f)[:depth],
                         np.asarray(ln1_b, f)[:depth], np.asarray(attn_sink, f)[:depth]], axis=1)
    pa = np.ascontiguousarray(np.broadcast_to(pa[:, None, :], (depth, 128, pa.shape[1])))
    pp = np.concatenate([np.asarray(ln2_g, f)[:depth], np.asarray(ln2_b, f)[:depth]], axis=1)
    pp = np.ascontiguousarray(np.broadcast_to(pp[:, None, :], (depth, 128, pp.shape[1])))
    uv = np.ascontiguousarray(np.concatenate([np.asarray(peer_u, f)[:depth], np.asarray(peer_v, f)[:depth]], axis=2)).reshape(depth * 16384, 2048)
    return {
        "win": win_ext, "wo": np.ascontiguousarray(np.asarray(w_o, f)[:depth]),
        "biasA": _biasA_table(np.asarray(t5_table, f)),
        "biasB": np.stack([_biasB_table(np.asarray(na_rpb[l], f)) for l in range(depth)]),
        "maskB": masks, "pbcA": pa, "pbcP": pp,
        "ident": np.eye(128, dtype=f),
        "iota16": np.ascontiguousarray(np.broadcast_to(np.arange(16, dtype=f), (128, 16))),
        "wq": np.ascontiguousarray(np.asarray(peer_wq, f)[:depth]),
        "keys": np.ascontiguousarray(np.asarray(peer_keys, f)[:depth].reshape(depth, 16, 128, 128)),
        "uv": uv,
    }


def kernel(x, w_in, w_o, attn_sink, na_rpb, t5_table, gnorm_a, gnorm_b,
           ln1_g, ln1_b, ln2_g, ln2_b, peer_wq, peer_keys, peer_u, peer_v):
    x = np.asarray(x, np.float32)
    B, T, _ = x.shape
    key = (T, DEPTH)
    if key not in _CACHE:
        _CACHE[key] = build_fused(T, DEPTH)
    nc, _, masks = _CACHE[key]
    shared = host_inputs(DEPTH, w_in, w_o, attn_sink, na_rpb, t5_table, gnorm_a, gnorm_b,
                         ln1_g, ln1_b, ln2_g, ln2_b, peer_wq, peer_keys, peer_u, peer_v, masks)
    in_maps = [dict(shared, x=np.ascontiguousarray(x[b])) for b in range(B)]
    r = run_bass_kernel_spmd(nc, in_maps, core_ids=list(range(B)))
    return np.stack([np.asarray(r.results[b]["y"], np.float32) for b in range(B)], axis=0)
```

```python
import math
import os
from contextlib import ExitStack

import numpy as np
import concourse.bass as bass
import concourse.mybir as mybir
from concourse.bass_utils import run_bass_kernel_spmd

F32 = mybir.dt.float32
BF16 = mybir.dt.bfloat16
U32 = mybir.dt.uint32
I32 = mybir.dt.int32
ALU = mybir.AluOpType
AF = mybir.ActivationFunctionType
AX = mybir.AxisListType

D = 1024
DEPTH = 4
SEQ = 4096
NCORES = 8
ALPHA = (2 * DEPTH) ** 0.25
LN_EPS = 1e-5
PROJX = 2432
NEGM = -30000.0
RING = 8
ENGS = ['sync', 'scalar', 'vector', 'gpsimd', 'tensor']


class DmaSems:
    def __init__(self, nc, stack, n, tag):
        self.sems = [stack.enter_context(nc.semaphore("d%s_%d" % (tag, i))) for i in range(n)]
        self.cnt = [0] * n


class SemPool:
    def __init__(self, nc, stack, dma, tag, sw=None):
        self.eng_sem = {e: stack.enter_context(nc.semaphore("e%s_%s" % (tag, e))) for e in ENGS if e != 'sync'}
        self.eng_cnt = {e: 0 for e in self.eng_sem}
        self.dma_sems = dma.sems
        self.dma_cnt = dma.cnt
        self.sw_sems = sw.sems if sw is not None else []
        self.sw_cnt = sw.cnt if sw is not None else []


class _Op:
    __slots__ = ('eng', 'fn', 'deps', 'dma', 'ev', 'waits', 'clock')


class Prog:
    def __init__(self, nc, pool):
        self.nc = nc
        self.pool = pool
        self.ops = []
        self.last_write = {}
        self.readers = {}
        self.slot_idx = {}
        self.sw_idx = {}

    def add(self, eng, fn, reads=(), writes=(), dma=None):
        op = _Op()
        op.eng, op.fn, op.dma = eng, fn, dma
        deps = set()
        for r in reads:
            w = self.last_write.get(r)
            if w is not None:
                deps.add(w)
        for r in writes:
            w = self.last_write.get(r)
            if w is not None:
                deps.add(w)
            for x in self.readers.get(r, ()):
                deps.add(x)
        idx = len(self.ops)
        op.deps = deps
        for r in reads:
            self.readers.setdefault(r, []).append(idx)
        for r in writes:
            self.last_write[r] = idx
            self.readers[r] = []
        self.ops.append(op)
        return idx

    def emit(self):
        nc, pool, ops = self.nc, self.pool, self.ops
        for op in ops:
            if op.dma is None:
                pool.eng_cnt[op.eng] += 1
                op.ev = (('E', op.eng), pool.eng_cnt[op.eng])
            elif op.eng == 'gpsimd':
                assert op.dma not in self.slot_idx
                if op.dma not in self.sw_idx:
                    self.sw_idx[op.dma] = len(self.sw_idx)
                    assert len(self.sw_idx) <= len(pool.sw_sems), "too many sw dma slots"
                si = self.sw_idx[op.dma]
                pool.sw_cnt[si] += 16
                op.ev = (('W', si), pool.sw_cnt[si])
            else:
                assert op.dma not in self.sw_idx
                if op.dma not in self.slot_idx:
                    self.slot_idx[op.dma] = len(self.slot_idx)
                    assert len(self.slot_idx) <= len(pool.dma_sems), "too many dma slots"
                si = self.slot_idx[op.dma]
                pool.dma_cnt[si] += 16
                op.ev = (('D', si), pool.dma_cnt[si])
        clock = {e: {} for e in ENGS}
        for op in ops:
            clk = clock[op.eng]
            waits = {}
            for d in sorted(op.deps, reverse=True):
                A = ops[d]
                if A.eng == 'tensor' and op.eng == 'tensor' and A.dma is None and op.dma is None:
                    continue
                k, v = A.ev
                if clk.get(k, 0) >= v:
                    continue
                waits[k] = max(waits.get(k, 0), v)
                for kk, vv in A.clock.items():
                    if clk.get(kk, 0) < vv:
                        clk[kk] = vv
            op.waits = list(waits.items())
            c = dict(clk)
            c[op.ev[0]] = op.ev[1]
            op.clock = c
        final = {}
        for op in ops:
            final[op.ev[0]] = max(final.get(op.ev[0], 0), op.ev[1])

        def sem_of(k):
            if k[0] == 'E':
                return pool.eng_sem[k[1]]
            return pool.dma_sems[k[1]] if k[0] == 'D' else pool.sw_sems[k[1]]

        by_eng = {e: [op for op in ops if op.eng == e] for e in ENGS}

        def body(ename):
            def f(eng):
                for op in by_eng[ename]:
                    for k, v in op.waits:
                        eng.wait_ge(sem_of(k), v)
                    ins = op.fn(eng)
                    ins.then_inc(sem_of(op.ev[0]), 16 if op.dma is not None else 1)
                clk = clock[ename]
                for k, v in final.items():
                    if clk.get(k, 0) < v:
                        eng.wait_ge(sem_of(k), v)
            return f

        with nc.Block() as block:
            block.sync(body('sync'))
            block.scalar(body('scalar'))
            block.vector(body('vector'))
            block.gpsimd(body('gpsimd'))
            block.tensor(body('tensor'))
        n = len(ops)
        nw = sum(len(op.waits) for op in ops)
        for op in ops:
            op.clock = None
        return n, nw


def mkap(t, off, dims):
    pst = t[:].ap[0][0]
    return bass.AP(t, off, [[pst, 128]] + [list(d) for d in dims])


def _t5_bucket(rel):
    nb = 16
    max_exact = 8
    ret = (rel > 0).astype(np.int32) * nb
    n = np.abs(rel).astype(np.int32)
    nf = np.maximum(n, 1).astype(np.float32)
    large = max_exact + (np.log(nf / np.float32(max_exact)) / np.float32(math.log(128 / max_exact))
                         * np.float32(nb - max_exact)).astype(np.int32)
    large = np.minimum(large, nb - 1)
    return ret + np.where(n < max_exact, n, large)


def _biasA_table(t5_table):
    kp = np.arange(128)[:, None, None]
    c = np.arange(3)[None, :, None]
    q = np.arange(128)[None, None, :]
    rel = (c - 1) * 128 + kp - q
    valid = np.abs(rel) <= 128
    b = t5_table[_t5_bucket(rel)]
    b = np.where(valid[..., None], b, np.float32(NEGM)).astype(np.float32)
    return np.ascontiguousarray(b.transpose(3, 0, 1, 2)).reshape(8, 128, 384)


def _na_geometry(T):
    rows = T // 64
    kh = min(8, rows)
    nt = T // 128

    def rs(r):
        return min(max(r - kh // 2, 0), rows - kh)

    def cs(c):
        return min(max(c - 8, 0), 64 - 16)

    kl = np.arange(128)
    krl, kc = kl // 64, kl % 64
    masks = []
    mask_ids = {}
    geo = []
    csq = np.array([cs(c) for c in range(64)])
    for t in range(nt):
        lo = min(rs(2 * t), rs(2 * t + 1))
        hi = max(rs(2 * t), rs(2 * t + 1)) + kh - 1
        lst = []
        for kt in range(lo // 2, hi // 2 + 1):
            krow = 2 * kt + krl[:, None]
            qrow = 2 * t + (kl // 64)[None, :]
            rsq = np.array([rs(2 * t), rs(2 * t + 1)])[(kl // 64)][None, :]
            qc = (kl % 64)[None, :]
            valid = (krow >= rsq) & (krow < rsq + kh) & (kc[:, None] >= csq[qc]) & (kc[:, None] < csq[qc] + 16)
            m = np.where(valid, np.float32(0.0), np.float32(NEGM)).astype(np.float32)
            key = m.tobytes()
            if key not in mask_ids:
                mask_ids[key] = len(masks)
                masks.append(m)
            lst.append((kt, kt - t, mask_ids[key]))
        geo.append(lst)
    return geo, np.stack(masks)


def _biasB_table(rpb_l):
    kl = np.arange(128)
    krl, kc = kl // 64, kl % 64
    out = np.zeros((8, 7, 128, 128), np.float32)
    for di, delta in enumerate(range(-3, 4)):
        dr = 2 * delta + krl[:, None] - krl[None, :] + 7
        dc = np.clip(kc[:, None] - kc[None, :], -15, 15) + 15
        ok = (dr >= 0) & (dr <= 14)
        g = rpb_l[:, np.clip(dr, 0, 14), dc]
        out[:, di] = np.where(ok[None], g, np.float32(0.0))
    return out


def emit_attn(nc, pool, T, tag, x_d, y_d, win_d, wo_d, bA_d, bB_d, mB_d, pbc_d, id_d, geo, NM, uv_src=None, uv_dst=None):
    NT = T // 128
    dbg_t = None
    with ExitStack() as st:
        def sb(name, shape, dt):
            return st.enter_context(nc.sbuf_tensor(name + tag, shape, dt))

        def ps(name, shape, dt):
            return st.enter_context(nc.psum_tensor(name + tag, shape, dt))

        identf = sb("identf", [128, 128], F32)
        ident = sb("ident_bf", [128, 128], BF16)
        pbc = sb("pbc_s", [128, 3 * D + 8], F32)
        sinkexp = sb("sinkexp", [128, 8], F32)
        biasA = sb("biasA_bf", [128, 8, 384], BF16)
        biasB = sb("biasB_bf", [128, 8, 7, 128], BF16)
        maskB = sb("maskB_bf", [128, NM, 128], BF16)
        win = sb("win_bf", [128, 8, PROJX], BF16)
        wo = sb("wo_bf", [128, 8, D], BF16)
        stg = [sb("stg%d" % i, [128, PROJX], F32) for i in range(2)]
        kring = sb("kring", [128, 6, RING * 128], BF16)
        vring = sb("vring", [128, RING, 10, 66], BF16)
        qring = sb("qring", [128, 4, 1024], BF16)
        xf = [sb("xf%d" % i, [128, D], F32) for i in range(2)]
        xb = sb("xb", [128, D], BF16)
        xT = sb("xT", [128, 8, 128], BF16)
        eA = [sb("eA%d" % i, [128, 384], BF16) for i in range(2)]
        eB = [sb("eB%d" % i, [128, 640], BF16) for i in range(2)]
        ya = sb("ya", [128, D], F32)
        sq = sb("sq", [128, D], F32)
        y16 = sb("y16", [128, D], BF16)
        yT = sb("yT", [128, 8, 128], BF16)
        xres = sb("xres", [128, D], F32)
        res = sb("res", [128, D], F32)
        outt = [sb("outt%d" % i, [128, D], F32) for i in range(2)]
        dn = sb("dn", [128, 16], F32)
        rd = sb("rd", [128, 16], F32)
        sm = sb("sm", [128, 16], F32)
        cvo = [sb("cvo%d" % i, [128, 2048], BF16) for i in range(2)]

        pT = ps("pT", [128, 1024], BF16)
        pg = [ps("pg%d" % i, [128, 512], F32) for i in range(2)]
        pS = [ps("pS%d" % i, [128, 512], F32) for i in range(3)]
        pP = [ps("pP%d" % i, [128, 512], F32) for i in range(2)]

        P = Prog(nc, pool)
        A = P.add
        cvt_rr = [0]
        if dbg_t is not None:
            dq_d = nc.dram_tensor("dbg_q", [128, 1024], BF16, kind="ExternalOutput").ap()
            dk_d = nc.dram_tensor("dbg_k", [128, 6, RING * 128], BF16, kind="ExternalOutput").ap()
            dv_d = nc.dram_tensor("dbg_v", [128, RING, 10, 66], BF16, kind="ExternalOutput").ap()
            dya_d = nc.dram_tensor("dbg_ya", [128, 1024], F32, kind="ExternalOutput").ap()
            dy16_d = nc.dram_tensor("dbg_y16", [128, 1024], BF16, kind="ExternalOutput").ap()
            dres_d = nc.dram_tensor("dbg_res", [128, 1024], F32, kind="ExternalOutput").ap()
            dxT_d = nc.dram_tensor("dbg_xT", [128, 8, 128], BF16, kind="ExternalOutput").ap()

        def convert(out_ap, in_ap, reads, writes):
            e = ['vector', 'gpsimd', 'scalar'][cvt_rr[0] % 3]
            cvt_rr[0] += 1
            if e == 'scalar':
                A(e, lambda g: g.copy(out=out_ap, in_=in_ap), reads, writes)
            else:
                A(e, lambda g: g.tensor_copy(out=out_ap, in_=in_ap), reads, writes)

        A('sync', lambda e: e.dma_start(out=identf[:], in_=id_d), writes=['identf'], dma='identf')
        A('vector', lambda e: e.tensor_copy(out=ident[:], in_=identf[:]), ['identf'], ['ident'])
        A('sync', lambda e: e.dma_start(out=pbc[:], in_=pbc_d), writes=['pbc'], dma='pbc')
        A('scalar', lambda e: e.activation(out=sinkexp[:], in_=pbc[:, 3 * D:3 * D + 8], func=AF.Exp), ['pbc'], ['sinkexp'])
        A('vector', lambda e: e.memset(vring[:], 1.0), [], [('v', i) for i in range(RING)])
        si = [0]

        def stage_load(dst_view, src_ap):
            s = si[0] % 2
            si[0] += 1
            A('sync', lambda e: e.dma_start(out=dst_view(stg[s]), in_=src_ap), writes=[('stg', s)], dma=('stg', s))
            return s

        for h in range(8):
            s = stage_load(lambda t: t[:, 0:384], bA_d[h])
            convert(biasA[:, h, :], stg[s][:, 0:384], [('stg', s)], ['biasA'])
        m0 = 0
        while m0 < NM:
            m1 = min(NM, m0 + 19)
            n = m1 - m0
            s = stage_load(lambda t, n=n: t[:, 0:n * 128].rearrange("k (m q) -> k m q", q=128),
                           mB_d[m0:m1].rearrange("m k q -> k m q"))
            convert(maskB[:, m0:m1, :], stg[s][:, 0:n * 128].rearrange("k (m q) -> k m q", q=128), [('stg', s)], ['maskB'])
            m0 = m1
        for h in range(8):
            s = stage_load(lambda t: t[:, 0:896].rearrange("k (m q) -> k m q", q=128),
                           bB_d[h].rearrange("m k q -> k m q"))
            convert(biasB[:, h, :, :], stg[s][:, 0:896].rearrange("k (m q) -> k m q", q=128), [('stg', s)], ['biasB'])
        for k in range(8):
            s = stage_load(lambda t: t[:], win_d[k * 128:(k + 1) * 128, :])
            convert(win[:, k, :], stg[s][:], [('stg', s)], ['win'])
        for k in range(8):
            s = stage_load(lambda t: t[:, 0:D], wo_d[k * 128:(k + 1) * 128, :])
            convert(wo[:, k, :], stg[s][:, 0:D], [('stg', s)], ['wo'])

        pgi = [0]

        def next_pg():
            i = pgi[0] % 2
            pgi[0] += 1
            return i

        projected = set()

        def project(t):
            projected.add(t)
            s = t % 2
            ks = t % RING
            qs = t % 4
            A('sync', lambda e: e.dma_start(out=xf[s][:], in_=x_d[t * 128:(t + 1) * 128, :]), writes=[('xf', s)], dma=('xf', s))
            A('vector', lambda e: e.tensor_copy(out=xb[:], in_=xf[s][:]), [('xf', s)], ['xb'])
            for k in range(8):
                A('tensor', lambda e, k=k: e.transpose(out=pT[:, k * 128:(k + 1) * 128], in_=xb[:, k * 128:(k + 1) * 128], identity=ident[:]),
                  ['xb', 'ident'], ['pT'])
            A('scalar', lambda e: e.copy(out=xT[:].rearrange("p a b -> p (a b)"), in_=pT[:]), ['pT'], ['xT'])
            if dbg_t == t:
                A('sync', lambda e: e.dma_start(out=dxT_d, in_=xT[:]), reads=['xT'], dma='dbg')
            groups = [
                ('q', [0, 128, 256, 384], 0),
                ('q', [768, 896, 1024, 1152], 4),
                ('kb', [1280, 1408, 1536, 1664], 1),
                ('ka', [512, 2304], 0),
            ]
            for gi, (kind, cols, j0) in enumerate(groups):
                b = next_pg()
                for j, col in enumerate(cols):
                    for k in range(8):
                        A('tensor', lambda e, b=b, j=j, col=col, k=k: e.matmul(
                            pg[b][:, j * 128:(j + 1) * 128], win[:, k, col:col + 128], xT[:, k, :],
                            start=(k == 0), stop=(k == 7)), ['xT', 'win'], [('pg', b)])
                if kind == 'q':
                    if gi == 0:
                        A('scalar', lambda e, b=b, j0=j0: e.mul(out=qring[:, qs, j0 * 128:(j0 + 4) * 128], in_=pg[b][:, 0:512], mul=0.125),
                          [('pg', b)], [('q', qs)])
                    else:
                        A('vector', lambda e, b=b, j0=j0: e.tensor_scalar(out=qring[:, qs, j0 * 128:(j0 + 4) * 128], in0=pg[b][:, 0:512],
                                                                          scalar1=0.125, scalar2=None, op0=ALU.mult),
                          [('pg', b)], [('q', qs)])
                elif kind == 'kb':
                    A('scalar', lambda e, b=b: e.copy(out=kring[:, 1:5, ks * 128:(ks + 1) * 128],
                                                      in_=pg[b][:, 0:512].rearrange("p (a b) -> p a b", b=128)),
                      [('pg', b)], [('k', ks)])
                else:
                    A('vector', lambda e, b=b: e.tensor_copy(out=kring[:, 0, ks * 128:(ks + 1) * 128], in_=pg[b][:, 0:128]),
                      [('pg', b)], [('k', ks)])
                    A('vector', lambda e, b=b: e.tensor_copy(out=kring[:, 5, ks * 128:(ks + 1) * 128], in_=pg[b][:, 128:256]),
                      [('pg', b)], [('k', ks)])
            for (col, n, h0, nh) in [(640, 128, 0, 2), (1792, 512, 2, 8)]:
                b = next_pg()
                for k in range(8):
                    A('tensor', lambda e, b=b, col=col, n=n, k=k: e.matmul(
                        pg[b][:, 0:n], xT[:, k, :], win[:, k, col:col + n], start=(k == 0), stop=(k == 7)),
                      ['xT', 'win'], [('pg', b)])
                eng = 'scalar' if nh == 8 else 'vector'
                if eng == 'scalar':
                    A('scalar', lambda e, b=b, n=n, h0=h0, nh=nh: e.copy(
                        out=vring[:, ks, h0:h0 + nh, 0:64], in_=pg[b][:, 0:n].rearrange("p (a b) -> p a b", b=64)),
                      [('pg', b)], [('v', ks)])
                else:
                    A('vector', lambda e, b=b, n=n, h0=h0, nh=nh: e.tensor_copy(
                        out=vring[:, ks, h0:h0 + nh, 0:64], in_=pg[b][:, 0:n].rearrange("p (a b) -> p a b", b=64)),
                      [('pg', b)], [('v', ks)])

        def attend(t):
            qs = t % 4
            blocks = [c for c in (0, 1, 2) if 0 <= t + c - 1 < NT]
            c0, c1 = blocks[0], blocks[-1]

            def scores_a(h, i):
                g = h // 4
                base = (h % 2) * 64
                kch = 0 if (h % 2) == g else 5
                for c in blocks:
                    ks = (t + c - 1) % RING
                    A('tensor', lambda e, c=c, ks=ks: e.matmul(
                        pS[i][:, c * 128:(c + 1) * 128], kring[base:base + 64, kch, ks * 128:(ks + 1) * 128],
                        qring[base:base + 64, qs, (h // 2) * 128:(h // 2 + 1) * 128], start=True, stop=False),
                      [('k', ks), ('q', qs)], [('S', i)])
                    A('tensor', lambda e, c=c: e.matmul(
                        pS[i][:, c * 128:(c + 1) * 128], ident[:], biasA[:, h, c * 128:(c + 1) * 128], start=False, stop=True),
                      ['ident', 'biasA'], [('S', i)])

            def rest_a(h, i):
                g = h // 4
                es = i
                A('scalar', lambda e: e.activation(out=eA[es][:, c0 * 128:(c1 + 1) * 128], in_=pS[i][:, c0 * 128:(c1 + 1) * 128], func=AF.Exp),
                  [('S', i)], [('eA', es)])
                r = h % 2
                for c in blocks:
                    ks = (t + c - 1) % RING
                    A('tensor', lambda e, c=c, ks=ks: e.matmul(
                        pP[r][:, 0:65], eA[es][:, c * 128:(c + 1) * 128], vring[:, ks, g, 0:65],
                        start=(c == c0), stop=(c == c1)), [('eA', es), ('v', ks)], [('pP', r)])
                A('vector', lambda e: e.tensor_scalar(out=dn[:, h:h + 1], in0=pP[r][:, 64:65], scalar1=sinkexp[:, h:h + 1],
                                                      scalar2=None, op0=ALU.add), [('pP', r), 'sinkexp'], [('dn', h)])
                A('vector', lambda e: e.reciprocal(out=rd[:, h:h + 1], in_=dn[:, h:h + 1]), [('dn', h)], [('rd', h)])
                A('vector', lambda e: e.tensor_scalar(out=ya[:, h * 64:(h + 1) * 64], in0=pP[r][:, 0:64], scalar1=rd[:, h:h + 1],
                                                      scalar2=None, op0=ALU.mult), [('pP', r), ('rd', h)], [('ya', 0)])

            scores_a(0, 0)
            for h in range(8):
                if h + 1 < 8:
                    scores_a(h + 1, (h + 1) % 2)
                rest_a(h, h % 2)
            kts = geo[t]
            nb = len(kts)
            assert all(kt in projected for kt, _, _ in kts) and nb <= 5

            def scores_b(h, i):
                base = (h % 2) * 64
                qch = 4 + h // 2
                kch = 1 + h // 2
                for bi, (kt, delta, mid) in enumerate(kts):
                    ks = kt % RING
                    if bi < 4:
                        reg = pS[i][:, bi * 128:(bi + 1) * 128]
                        wk = [('S', i)]
                    else:
                        reg = pS[2][:, i * 128:(i + 1) * 128]
                        wk = ['S2']
                    A('tensor', lambda e, reg=reg, ks=ks: e.matmul(
                        reg, kring[base:base + 64, kch, ks * 128:(ks + 1) * 128],
                        qring[base:base + 64, qs, qch * 128:(qch + 1) * 128], start=True, stop=False),
                      [('k', ks), ('q', qs)], wk)
                    A('tensor', lambda e, reg=reg, delta=delta: e.matmul(
                        reg, ident[:], biasB[:, h, delta + 3, :], start=False, stop=False), ['ident', 'biasB'], wk)
                    A('tensor', lambda e, reg=reg, mid=mid: e.matmul(
                        reg, ident[:], maskB[:, mid, :], start=False, stop=True), ['ident', 'maskB'], wk)

            def rest_b(h, i):
                es = i
                n4 = min(nb, 4)
                if nb > 4:
                    A('scalar', lambda e: e.activation(out=eB[es][:, 512:640], in_=pS[2][:, i * 128:(i + 1) * 128], func=AF.Exp),
                      ['S2'], [('eB', es)])
                A('scalar', lambda e: e.activation(out=eB[es][:, 0:n4 * 128], in_=pS[i][:, 0:n4 * 128], func=AF.Exp),
                  [('S', i)], [('eB', es)])
                r = h % 2
                for bi, (kt, delta, mid) in enumerate(kts):
                    ks = kt % RING
                    A('tensor', lambda e, bi=bi, ks=ks: e.matmul(
                        pP[r][:, 0:65], eB[es][:, bi * 128:(bi + 1) * 128], vring[:, ks, 2 + h, 0:65],
                        start=(bi == 0), stop=(bi == nb - 1)), [('eB', es), ('v', ks)], [('pP', r)])
                A('vector', lambda e: e.reciprocal(out=rd[:, 8 + h:9 + h], in_=pP[r][:, 64:65]), [('pP', r)], [('rd', 8 + h)])
                A('vector', lambda e: e.tensor_scalar(out=ya[:, 512 + h * 64:512 + (h + 1) * 64], in0=pP[r][:, 0:64],
                                                      scalar1=rd[:, 8 + h:9 + h], scalar2=None, op0=ALU.mult),
                  [('pP', r), ('rd', 8 + h)], [('ya', 1)])

            scores_b(0, 0)
            for h in range(8):
                if h + 1 < 8:
                    scores_b(h + 1, (h + 1) % 2)
                rest_b(h, h % 2)
            if dbg_t == t:
                A('sync', lambda e: e.dma_start(out=dq_d, in_=qring[:, qs, :]), reads=[('q', qs)], dma='dbg')
                A('sync', lambda e: e.dma_start(out=dk_d, in_=kring[:]), reads=[('k', i) for i in range(RING)], dma='dbg')
                A('sync', lambda e: e.dma_start(out=dv_d, in_=vring[:]), reads=[('v', i) for i in range(RING)], dma='dbg')
                A('sync', lambda e: e.dma_start(out=dya_d, in_=ya[:]), reads=[('ya', 0), ('ya', 1)], dma='dbg')
            A('vector', lambda e: e.tensor_tensor(out=sq[:], in0=ya[:], in1=ya[:], op=ALU.mult), [('ya', 0), ('ya', 1)], ['sq'])
            A('vector', lambda e: e.tensor_reduce(out=sm[:, 0:2], in_=sq[:].rearrange("p (a b) -> p a b", b=512), axis=AX.X, op=ALU.add),
              ['sq'], ['sm01'])
            A('vector', lambda e: e.tensor_scalar(out=sm[:, 2:4], in0=sm[:, 0:2], scalar1=1.0 / 512, scalar2=LN_EPS, op0=ALU.mult, op1=ALU.add),
              ['sm01'], ['sm23'])
            A('scalar', lambda e: e.activation(out=sm[:, 4:6], in_=sm[:, 2:4], func=AF.Sqrt), ['sm23'], ['sm45'])
            A('vector', lambda e: e.reciprocal(out=sm[:, 6:8], in_=sm[:, 4:6]), ['sm45'], ['sm67'])
            for j in range(2):
                A('vector', lambda e, j=j: e.scalar_tensor_tensor(out=y16[:, j * 512:(j + 1) * 512], in0=ya[:, j * 512:(j + 1) * 512],
                                                                  scalar=sm[:, 6 + j:7 + j], in1=pbc[:, j * 512:(j + 1) * 512],
                                                                  op0=ALU.mult, op1=ALU.mult),
                  [('ya', j), 'sm67', 'pbc'], ['y16'])
            for k in range(8):
                A('tensor', lambda e, k=k: e.transpose(out=pT[:, k * 128:(k + 1) * 128], in_=y16[:, k * 128:(k + 1) * 128], identity=ident[:]),
                  ['y16', 'ident'], ['pT'])
            A('scalar', lambda e: e.copy(out=yT[:].rearrange("p a b -> p (a b)"), in_=pT[:]), ['pT'], ['yT'])
            for n in range(2):
                for k in range(8):
                    A('tensor', lambda e, n=n, k=k: e.matmul(pg[n][:, 0:512], yT[:, k, :], wo[:, k, n * 512:(n + 1) * 512],
                                                             start=(k == 0), stop=(k == 7)), ['yT', 'wo'], [('pg', n)])
            A('sync', lambda e: e.dma_start(out=xres[:], in_=x_d[t * 128:(t + 1) * 128, :]), writes=['xres'], dma='xres')
            for n in range(2):
                A('vector', lambda e, n=n: e.scalar_tensor_tensor(out=res[:, n * 512:(n + 1) * 512], in0=xres[:, n * 512:(n + 1) * 512],
                                                                  scalar=float(ALPHA), in1=pg[n][:, 0:512], op0=ALU.mult, op1=ALU.add),
                  ['xres', ('pg', n)], ['res'])
            if dbg_t == t:
                A('sync', lambda e: e.dma_start(out=dy16_d, in_=y16[:]), reads=['y16'], dma='dbg')
                A('sync', lambda e: e.dma_start(out=dres_d, in_=res[:]), reads=['res'], dma='dbg')
            layer_norm(A, res, sq, sm, pbc, D, 2 * D, outt[t % 2], ('outt', t % 2))
            A('sync', lambda e: e.dma_start(out=y_d[t * 128:(t + 1) * 128, :], in_=outt[t % 2][:]), reads=[('outt', t % 2)], dma=('outt', t % 2))

        ccnt = [0]

        def convert_rows(n):
            if uv_src is None:
                return
            srcv = uv_src.rearrange("(p r) c -> p r c", p=128)
            dstv = uv_dst.rearrange("(p r) c -> p r c", p=128)
            for _ in range(n):
                c = ccnt[0]
                if c >= 128:
                    return
                ccnt[0] += 1
                s = c % 2
                A('gpsimd', lambda e, c=c, s=s: e.dma_start(out=stg[s][:, 0:2048], in_=srcv[:, c, :]), writes=[('stg', s)], dma=('cst', s))
                A('gpsimd', lambda e, s=s: e.tensor_copy(out=cvo[s][:], in_=stg[s][:, 0:2048]), [('stg', s)], [('cvo', s)])
                A('gpsimd', lambda e, c=c, s=s: e.dma_start(out=dstv[:, c, :], in_=cvo[s][:]), reads=[('cvo', s)], dma=('cvo', s))

        LAG = 3
        per = -(-128 // NT)
        for t in range(NT):
            project(t)
            convert_rows(per)
            if t >= LAG:
                attend(t - LAG)
        for t in range(max(NT - LAG, 0), NT):
            attend(t)
        convert_rows(128)
        stats = P.emit()
    return stats


def layer_norm(A, res, sq, sm, pbc, goff, boff, out_t, out_key):
    A('vector', lambda e: e.tensor_reduce(out=sm[:, 8:9], in_=res[:], axis=AX.X, op=ALU.add), ['res'], ['ln_s'])
    A('vector', lambda e: e.tensor_scalar(out=sm[:, 9:10], in0=sm[:, 8:9], scalar1=1.0 / D, scalar2=None, op0=ALU.mult), ['ln_s'], ['ln_m'])
    A('vector', lambda e: e.tensor_scalar(out=res[:], in0=res[:], scalar1=sm[:, 9:10], scalar2=None, op0=ALU.subtract), ['res', 'ln_m'], ['res'])
    A('vector', lambda e: e.tensor_tensor(out=sq[:], in0=res[:], in1=res[:], op=ALU.mult), ['res'], ['sq'])
    A('vector', lambda e: e.tensor_reduce(out=sm[:, 10:11], in_=sq[:], axis=AX.X, op=ALU.add), ['sq'], ['ln_v'])
    A('vector', lambda e: e.tensor_scalar(out=sm[:, 11:12], in0=sm[:, 10:11], scalar1=1.0 / D, scalar2=LN_EPS, op0=ALU.mult, op1=ALU.add),
      ['ln_v'], ['ln_ve'])
    A('scalar', lambda e: e.activation(out=sm[:, 12:13], in_=sm[:, 11:12], func=AF.Sqrt), ['ln_ve'], ['ln_sd'])
    A('vector', lambda e: e.reciprocal(out=sm[:, 13:14], in_=sm[:, 12:13]), ['ln_sd'], ['ln_r'])
    A('vector', lambda e: e.scalar_tensor_tensor(out=res[:], in0=res[:], scalar=sm[:, 13:14], in1=pbc[:, goff:goff + D], op0=ALU.mult, op1=ALU.mult),
      ['res', 'ln_r', 'pbc'], ['res'])
    A('gpsimd', lambda e: e.tensor_tensor(out=out_t[:], in0=res[:], in1=pbc[:, boff:boff + D], op=ALU.add), ['res', 'pbc'], [out_key])


NS = 12


def emit_peer(nc, pool, T, tag, x_d, y_d, wq_d, keys_d, uv_d, pbc_d, id_d, io_d, e_off):
    NT = T // 128
    with ExitStack() as st:
        def sb(name, shape, dt):
            return st.enter_context(nc.sbuf_tensor(name + tag, shape, dt))

        def ps(name, shape, dt):
            return st.enter_context(nc.psum_tensor(name + tag, shape, dt))

        identf = sb("identf", [128, 128], F32)
        ident = sb("ident_bf", [128, 128], BF16)
        iota = sb("iota", [128, 16], F32)
        pbc = sb("pbc_s", [128, 2 * D], F32)
        wq = sb("wq_bf", [128, 8, 2048], BF16)
        keysb = sb("keys_bf", [128, 16, 128], BF16)
        keysT = sb("keysT", [128, 16, 128], BF16)
        stg = [sb("stg%d" % i, [128, 2048], F32) for i in range(2)]
        xf = [sb("xf%d" % i, [128, D], F32) for i in range(2)]
        xb = sb("xb", [128, D], BF16)
        xT = sb("xT", [128, 8, 128], BF16)
        qT = sb("qT", [128, 16, 128], BF16)
        sc = sb("sc", [128, 2048], F32)
        tmp = sb("tmp", [128, 256], F32)
        stop_ = sb("s_top", [128, 256], F32)
        itop = sb("i_top", [128, 256], U32)
        itopf = sb("i_topf", [128, 256], F32)
        cand = sb("cand", [128, 2048], F32)
        oh = sb("oh", [128, 2048], F32)
        fs = sb("fs", [128, 128], F32)
        fpos = sb("fpos", [128, 128], U32)
        abu = sb("abu", [128, 256], U32)
        abf = sb("abf", [128, 256], F32)
        isel = sb("isel", [128, 256], F32)
        ef = sb("ef", [128, 128], F32)
        eidx = [sb("eidx%d" % i, [128, 128], I32) for i in range(2)]
        gate = [sb("gate%d" % i, [128, 128], F32) for i in range(2)]
        ge = sb("ge", [128, 128], F32)
        gs = sb("gs", [128, 16], F32)
        hraw = sb("hraw", [128, 128], F32)
        aa = sb("aa", [128, 128], F32)
        uv = sb("uvring", [128, NS, 2048], BF16)
        diag = [sb("diag%d" % i, [128, 4, 128], BF16) for i in range(2)]
        junk = sb("junk", [128, D], BF16)
        res = sb("res", [128, D], F32)
        sq = sb("sq", [128, D], F32)
        sm = sb("sm", [128, 16], F32)
        outt = [sb("outt%d" % i, [128, D], F32) for i in range(2)]

        pT = ps("pT", [128, 1024], BF16)
        pg = [ps("pg%d" % i, [128, 512], F32) for i in range(2)]
        pS = [ps("pS%d" % i, [128, 512], F32) for i in range(2)]
        pacc = [ps("pacc%d" % i, [128, 512], F32) for i in range(2)]

        P = Prog(nc, pool)
        A = P.add
        cvt_rr = [0]

        def convert(out_ap, in_ap, reads, writes):
            e = ['vector', 'gpsimd', 'scalar'][cvt_rr[0] % 3]
            cvt_rr[0] += 1
            if e == 'scalar':
                A(e, lambda g: g.copy(out=out_ap, in_=in_ap), reads, writes)
            else:
                A(e, lambda g: g.tensor_copy(out=out_ap, in_=in_ap), reads, writes)

        A('sync', lambda e: e.dma_start(out=identf[:], in_=id_d), writes=['identf'], dma='identf')
        A('vector', lambda e: e.tensor_copy(out=ident[:], in_=identf[:]), ['identf'], ['ident'])
        A('sync', lambda e: e.dma_start(out=iota[:], in_=io_d), writes=['iota'], dma='iota')
        A('sync', lambda e: e.dma_start(out=pbc[:], in_=pbc_d), writes=['pbc'], dma='pbc')
        for k in range(8):
            s = k % 2
            A('sync', lambda e, s=s, k=k: e.dma_start(out=stg[s][:], in_=wq_d[k * 128:(k + 1) * 128, :]), writes=[('stg', s)], dma=('stg', s))
            convert(wq[:, k, :], stg[s][:], [('stg', s)], ['wq'])
        A('sync', lambda e: e.dma_start(out=stg[0][:].rearrange("k (m c) -> k m c", c=128), in_=keys_d.rearrange("m k c -> k m c")),
          writes=[('stg', 0)], dma=('stg', 0))
        A('vector', lambda e: e.tensor_copy(out=keysb[:].rearrange("p a b -> p (a b)"), in_=stg[0][:]), [('stg', 0)], ['keysb'])
        for half in range(2):
            for j in range(8):
                m = half * 8 + j
                A('tensor', lambda e, m=m, j=j: e.transpose(out=pT[:, j * 128:(j + 1) * 128], in_=keysb[:, m, :], identity=ident[:]),
                  ['keysb', 'ident'], ['pT'])
            A('scalar', lambda e, half=half: e.copy(out=keysT[:, half * 8:(half + 1) * 8, :].rearrange("p a b -> p (a b)"), in_=pT[:]),
              ['pT'], ['keysT'])

        def top16(seg, n, vals, idxs, o, use_tmp):
            A('vector', lambda e: e.max(out=vals[:, o:o + 8], in_=seg), ['segsrc'], ['tk_v0'])
            A('vector', lambda e: e.match_replace(out=use_tmp[:, 0:n], in_to_replace=vals[:, o:o + 8], in_values=seg, imm_value=-1e30),
              ['segsrc', 'tk_v0'], ['tk_tmp'])
            A('vector', lambda e: e.max(out=vals[:, o + 8:o + 16], in_=use_tmp[:, 0:n]), ['tk_tmp'], ['tk_v1'])
            A('vector', lambda e: e.max_index(out=idxs[:, o:o + 8], in_max=vals[:, o:o + 8], in_values=seg), ['segsrc', 'tk_v0'], ['tk_i'])
            A('vector', lambda e: e.max_index(out=idxs[:, o + 8:o + 16], in_max=vals[:, o + 8:o + 16], in_values=seg), ['segsrc', 'tk_v1'], ['tk_i'])

        hkc = [0]

        def route(t):
            s = t % 2
            A('sync', lambda e: e.dma_start(out=xf[s][:], in_=x_d[t * 128:(t + 1) * 128, :]), writes=[('xf', s)], dma=('xf', s))
            A('gpsimd', lambda e: e.tensor_copy(out=xb[:], in_=xf[s][:]), [('xf', s)], ['xb'])
            for k in range(8):
                A('tensor', lambda e, k=k: e.transpose(out=pT[:, k * 128:(k + 1) * 128], in_=xb[:, k * 128:(k + 1) * 128], identity=ident[:]),
                  ['xb', 'ident'], ['pT'])
            A('scalar', lambda e: e.copy(out=xT[:].rearrange("p a b -> p (a b)"), in_=pT[:]), ['pT'], ['xT'])
            yield
            for g4 in range(4):
                b = g4 % 2
                for j in range(4):
                    m = g4 * 4 + j
                    for k in range(8):
                        A('tensor', lambda e, b=b, j=j, m=m, k=k: e.matmul(
                            pg[b][:, j * 128:(j + 1) * 128], wq[:, k, m * 128:(m + 1) * 128], xT[:, k, :], start=(k == 0), stop=(k == 7)),
                          ['xT', 'wq'], [('pg', b)])
                A('scalar', lambda e, b=b, g4=g4: e.copy(out=qT[:, g4 * 4:(g4 + 1) * 4, :].rearrange("p a b -> p (a b)"), in_=pg[b][:, 0:512]),
                  [('pg', b)], [('qT', g4)])
                yield
            for g4 in range(4):
                b = g4 % 2
                for j in range(4):
                    m = g4 * 4 + j
                    A('tensor', lambda e, b=b, j=j, m=m: e.matmul(pS[b][:, j * 128:(j + 1) * 128], qT[:, m, :], keysT[:, m, :], start=True, stop=True),
                      [('qT', g4), 'keysT'], [('pS', b)])
                A('scalar', lambda e, b=b, g4=g4: e.copy(out=sc[:, g4 * 512:(g4 + 1) * 512], in_=pS[b][:, 0:512]), [('pS', b)], ['segsrc'])
                yield
            for m in range(16):
                top16(sc[:, m * 128:(m + 1) * 128], 128, stop_, itop, m * 16, tmp)
                yield
            A('vector', lambda e: e.tensor_copy(out=itopf[:], in_=itop[:]), ['tk_i', 'tk_v0', 'tk_v1'], ['itopf'])
            A('vector', lambda e: e.tensor_tensor(out=mkap(cand, 0, [[256, 8], [16, 16], [1, 16]]),
                                                  in0=mkap(stop_, 0, [[32, 8], [1, 16], [0, 16]]),
                                                  in1=mkap(stop_, 16, [[32, 8], [0, 16], [1, 16]]), op=ALU.add),
              ['tk_v0', 'tk_v1', 'itopf'], ['segsrc', 'cand'])
            yield
            for h in range(8):
                top16(cand[:, h * 256:(h + 1) * 256], 256, fs, fpos, h * 16, tmp)
                yield
            A('vector', lambda e: e.tensor_scalar(out=abu[:, 0:128], in0=fpos[:], scalar1=4, scalar2=None, op0=ALU.logical_shift_right),
              ['tk_i', 'tk_v0', 'tk_v1'], ['abu0'])
            A('vector', lambda e: e.tensor_scalar(out=abu[:, 128:256], in0=fpos[:], scalar1=15, scalar2=None, op0=ALU.bitwise_and),
              ['tk_i'], ['abu1'])
            A('vector', lambda e: e.tensor_copy(out=abf[:], in_=abu[:]), ['abu0', 'abu1'], ['abf'])
            yield
            for p in range(2):
                A('vector', lambda e, p=p: e.tensor_tensor(out=mkap(oh, 0, [[256, 8], [16, 16], [1, 16]]),
                                                           in0=mkap(abf, p * 128, [[16, 8], [1, 16], [0, 16]]),
                                                           in1=mkap(iota, 0, [[0, 8], [0, 16], [1, 16]]), op=ALU.is_equal),
                  ['abf', 'iota'], ['oh'])
                A('vector', lambda e, p=p: e.tensor_tensor(out=mkap(oh, 0, [[256, 8], [16, 16], [1, 16]]),
                                                           in0=mkap(oh, 0, [[256, 8], [16, 16], [1, 16]]),
                                                           in1=mkap(itopf, p * 16, [[32, 8], [0, 16], [1, 16]]), op=ALU.mult),
                  ['oh', 'itopf'], ['oh'])
                A('vector', lambda e, p=p: e.tensor_reduce(out=mkap(isel, p * 128, [[16, 8], [1, 16]]),
                                                           in_=mkap(oh, 0, [[256, 8], [16, 16], [1, 16]]), axis=AX.X, op=ALU.add),
                  ['oh'], [('isel', p)])
                yield
            A('vector', lambda e: e.scalar_tensor_tensor(out=ef[:], in0=isel[:, 0:128], scalar=128.0, in1=isel[:, 128:256], op0=ALU.mult, op1=ALU.add),
              [('isel', 0), ('isel', 1)], ['ef'])
            if e_off:
                A('vector', lambda e: e.tensor_scalar(out=ef[:], in0=ef[:], scalar1=float(e_off), scalar2=None, op0=ALU.add), ['ef'], ['ef'])
            A('vector', lambda e: e.tensor_copy(out=eidx[s][:], in_=ef[:]), ['ef'], [('eidx', s)])
            yield
            A('vector', lambda e: e.tensor_tensor(out=mkap(ge, 0, [[16, 8], [1, 16]]), in0=mkap(fs, 0, [[16, 8], [1, 16]]),
                                                  in1=mkap(fs, 0, [[16, 8], [0, 16]]), op=ALU.subtract), ['tk_v0', 'tk_v1', 'abu0'], ['ge'])
            A('scalar', lambda e: e.activation(out=ge[:], in_=ge[:], func=AF.Exp), ['ge'], ['ge'])
            A('vector', lambda e: e.tensor_reduce(out=gs[:, 0:8], in_=mkap(ge, 0, [[16, 8], [1, 16]]), axis=AX.X, op=ALU.add), ['ge'], ['gs0'])
            A('vector', lambda e: e.reciprocal(out=gs[:, 8:16], in_=gs[:, 0:8]), ['gs0'], ['gs1'])
            A('vector', lambda e: e.tensor_tensor(out=mkap(gate[s], 0, [[16, 8], [1, 16]]), in0=mkap(ge, 0, [[16, 8], [1, 16]]),
                                                  in1=mkap(gs, 8, [[1, 8], [0, 16]]), op=ALU.mult), ['ge', 'gs1'], [('gate', s)])

        SKIP_DVE = os.environ.get("PEER_SKIP_DVE") == "1"
        SKIP_DMA = os.environ.get("PEER_SKIP_DMA") == "1"

        def drain(gen, n):
            for _ in range(n):
                try:
                    next(gen)
                except StopIteration:
                    return

        def experts(t, gen):
            s = t % 2
            for g in range(32):
                drain(gen, 2)
                slots = []
                for j in range(4):
                    hk = g * 4 + j
                    sl = hkc[0] % NS
                    hkc[0] += 1
                    slots.append(sl)
                    if not SKIP_DMA:
                        A('gpsimd', lambda e, hk=hk, sl=sl: e.indirect_dma_start(
                            out=uv[:, sl, :], out_offset=None, in_=uv_d,
                            in_offset=bass.IndirectOffsetOnAxis(ap=eidx[s][:, hk:hk + 1], axis=0)),
                          reads=[('eidx', s)], writes=[('uv', sl)], dma=('uv', sl))
                if SKIP_DVE:
                    continue
                for j in range(4):
                    hk = g * 4 + j
                    sl = slots[j]
                    A('vector', lambda e, hk=hk, sl=sl: e.scalar_tensor_tensor(
                        out=junk[:], in0=uv[:, sl, 0:D], scalar=1.0, in1=xf[s][:], op0=ALU.mult, op1=ALU.mult,
                        accum_out=hraw[:, hk:hk + 1]), [('uv', sl), ('xf', s)], [('hraw', g)])
                A('scalar', lambda e, g=g: e.activation(out=aa[:, g * 4:(g + 1) * 4], in_=hraw[:, g * 4:(g + 1) * 4], func=AF.Gelu),
                  [('hraw', g)], [('aa', g)])
                A('vector', lambda e, g=g: e.tensor_tensor(out=aa[:, g * 4:(g + 1) * 4], in0=aa[:, g * 4:(g + 1) * 4],
                                                           in1=gate[s][:, g * 4:(g + 1) * 4], op=ALU.mult),
                  [('aa', g), ('gate', s)], [('aa', g)])
                d = g % 2
                for j in range(4):
                    A('scalar', lambda e, g=g, d=d, j=j: e.activation(out=diag[d][:, j, :], in_=identf[:], func=AF.Copy,
                                                                      scale=aa[:, g * 4 + j:g * 4 + j + 1]),
                      [('aa', g), 'identf'], [('diag', d)])
                for j in range(4):
                    hk = g * 4 + j
                    sl = slots[j]
                    for n in range(2):
                        A('tensor', lambda e, hk=hk, sl=sl, j=j, n=n, d=d: e.matmul(
                            pacc[n][:, 0:512], diag[d][:, j, :], uv[:, sl, D + n * 512:D + (n + 1) * 512],
                            start=(hk == 0), stop=(hk == 127)), [('diag', d), ('uv', sl)], [('pacc', n)])
            for n in range(2):
                A('vector', lambda e, n=n: e.scalar_tensor_tensor(out=res[:, n * 512:(n + 1) * 512], in0=xf[s][:, n * 512:(n + 1) * 512],
                                                                  scalar=float(ALPHA), in1=pacc[n][:, 0:512], op0=ALU.mult, op1=ALU.add),
                  [('xf', s), ('pacc', n)], ['res'])
            layer_norm(A, res, sq, sm, pbc, 0, D, outt[s], ('outt', s))
            A('sync', lambda e: e.dma_start(out=y_d[t * 128:(t + 1) * 128, :], in_=outt[s][:]), reads=[('outt', s)], dma=('outt', s))

        for _ in route(0):
            pass
        for t in range(NT):
            gen = route(t + 1) if t + 1 < NT else iter(())
            experts(t, gen)
            for _ in gen:
                pass
        stats = P.emit()
    return stats


_CACHE = {}


def build_fused(T, depth):
    geo, masks = _na_geometry(T)
    NM = masks.shape[0]
    nc = bass.Bass("TRN2", target_bir_lowering=False)
    x_d = nc.dram_tensor("x", [T, D], F32, kind="ExternalInput").ap()
    win_d = nc.dram_tensor("win", [depth, D, PROJX], F32, kind="ExternalInput").ap()
    wo_d = nc.dram_tensor("wo", [depth, D, D], F32, kind="ExternalInput").ap()
    bA_d = nc.dram_tensor("biasA", [8, 128, 384], F32, kind="ExternalInput").ap()
    bB_d = nc.dram_tensor("biasB", [depth, 8, 7, 128, 128], F32, kind="ExternalInput").ap()
    mB_d = nc.dram_tensor("maskB", [NM, 128, 128], F32, kind="ExternalInput").ap()
    pa_d = nc.dram_tensor("pbcA", [depth, 128, 3 * D + 8], F32, kind="ExternalInput").ap()
    pp_d = nc.dram_tensor("pbcP", [depth, 128, 2 * D], F32, kind="ExternalInput").ap()
    id_d = nc.dram_tensor("ident", [128, 128], F32, kind="ExternalInput").ap()
    io_d = nc.dram_tensor("iota16", [128, 16], F32, kind="ExternalInput").ap()
    wq_d = nc.dram_tensor("wq", [depth, D, 2048], F32, kind="ExternalInput").ap()
    keys_d = nc.dram_tensor("keys", [depth, 16, 128, 128], F32, kind="ExternalInput").ap()
    uv_d = nc.dram_tensor("uv", [depth * 16384, 2048], F32, kind="ExternalInput").ap()
    y_d = nc.dram_tensor("y", [T, D], F32, kind="ExternalOutput").ap()
    scr_a = nc.dram_tensor("scr_a", [T, D], F32).ap()
    scr_b = nc.dram_tensor("scr_b", [T, D], F32).ap()
    uvb_d = nc.dram_tensor("uv_bf16", [16384, 2048], BF16).ap()
    stats = []
    with ExitStack() as gs:
        dmas = [DmaSems(nc, gs, 12, "x"), DmaSems(nc, gs, 12, "y")]
        swd = DmaSems(nc, gs, 16, "w")
        cur = x_d
        for l in range(depth):
            dma = dmas[0] if l < 2 else dmas[1]
            pool = SemPool(nc, gs, dma, "a%d" % l, sw=swd)
            stats.append(emit_attn(nc, pool, T, "_a%d" % l, cur, scr_a, win_d[l], wo_d[l], bA_d, bB_d[l], mB_d, pa_d[l], id_d, geo, NM,
                                   uv_src=uv_d[l * 16384:(l + 1) * 16384, :], uv_dst=uvb_d))
            pool = SemPool(nc, gs, dma, "p%d" % l, sw=swd)
            dst = y_d if l == depth - 1 else scr_b
            stats.append(emit_peer(nc, pool, T, "_p%d" % l, scr_a, dst, wq_d[l], keys_d[l], uvb_d, pp_d[l], id_d, io_d, 0))
            cur = scr_b
    return nc, stats, masks


def host_inputs(depth, w_in, w_o, attn_sink, na_rpb, t5_table, gnorm_a, gnorm_b,
                ln1_g, ln1_b, ln2_g, ln2_b, peer_wq, peer_keys, peer_u, peer_v, masks):
    f = np.float32
    w_in = np.asarray(w_in, f)[:depth]
    win_ext = np.ascontiguousarray(np.concatenate([w_in, w_in[:, :, 576:640], w_in[:, :, 512:576]], axis=2))
    pa = np.concatenate([np.asarray(gnorm_a, f)[:depth], np.asarray(gnorm_b, f)[:depth], np.asarray(ln1_g, f)[:depth],
                         np.asarray(ln1_b, f)[:depth], np.asarray(attn_sink, f)[:depth]], axis=1)
    pa = np.ascontiguousarray(np.broadcast_to(pa[:, None, :], (depth, 128, pa.shape[1])))
    pp = np.concatenate([np.asarray(ln2_g, f)[:depth], np.asarray(ln2_b, f)[:depth]], axis=1)
    pp = np.ascontiguousarray(np.broadcast_to(pp[:, None, :], (depth, 128, pp.shape[1])))
    uv = np.ascontiguousarray(np.concatenate([np.asarray(peer_u, f)[:depth], np.asarray(peer_v, f)[:depth]], axis=2)).reshape(depth * 16384, 2048)
    return {
        "win": win_ext, "wo": np.ascontiguousarray(np.asarray(w_o, f)[:depth]),
        "biasA": _biasA_table(np.asarray(t5_table, f)),
        "biasB": np.stack([_biasB_table(np.asarray(na_rpb[l], f)) for l in range(depth)]),
        "maskB": masks, "pbcA": pa, "pbcP": pp,
        "ident": np.eye(128, dtype=f),
        "iota16": np.ascontiguousarray(np.broadcast_to(np.arange(16, dtype=f), (128, 16))),
        "wq": np.ascontiguousarray(np.asarray(peer_wq, f)[:depth]),
        "keys": np.ascontiguousarray(np.asarray(peer_keys, f)[:depth].reshape(depth, 16, 128, 128)),
        "uv": uv,
    }


def kernel(x, w_in, w_o, attn_sink, na_rpb, t5_table, gnorm_a, gnorm_b,
           ln1_g, ln1_b, ln2_g, ln2_b, peer_wq, peer_keys, peer_u, peer_v):
    x = np.asarray(x, np.float32)
    B, T, _ = x.shape
    key = (T, DEPTH)
    if key not in _CACHE:
        _CACHE[key] = build_fused(T, DEPTH)
    nc, _, masks = _CACHE[key]
    shared = host_inputs(DEPTH, w_in, w_o, attn_sink, na_rpb, t5_table, gnorm_a, gnorm_b,
                         ln1_g, ln1_b, ln2_g, ln2_b, peer_wq, peer_keys, peer_u, peer_v, masks)
    in_maps = [dict(shared, x=np.ascontiguousarray(x[b])) for b in range(B)]
    r = run_bass_kernel_spmd(nc, in_maps, core_ids=list(range(B)))
    return np.stack([np.asarray(r.results[b]["y"], np.float32) for b in range(B)], axis=0)
```

```python
import math
import os
from contextlib import ExitStack

import numpy as np
import concourse.bass as bass
import concourse.mybir as mybir
from concourse.bass_utils import run_bass_kernel_spmd

F32 = mybir.dt.float32
BF16 = mybir.dt.bfloat16
U32 = mybir.dt.uint32
I32 = mybir.dt.int32
ALU = mybir.AluOpType
AF = mybir.ActivationFunctionType
AX = mybir.AxisListType

D = 1024
DEPTH = 4
SEQ = 4096
NCORES = 8
ALPHA = (2 * DEPTH) ** 0.25
LN_EPS = 1e-5
PROJX = 2432
NEGM = -30000.0
RING = 8
ENGS = ['sync', 'scalar', 'vector', 'gpsimd', 'tensor']


class DmaSems:
    def __init__(self, nc, stack, n, tag):
        self.sems = [stack.enter_context(nc.semaphore("d%s_%d" % (tag, i))) for i in range(n)]
        self.cnt = [0] * n


class SemPool:
    def __init__(self, nc, stack, dma, tag, sw=None):
        self.eng_sem = {e: stack.enter_context(nc.semaphore("e%s_%s" % (tag, e))) for e in ENGS if e != 'sync'}
        self.eng_cnt = {e: 0 for e in self.eng_sem}
        self.dma_sems = dma.sems
        self.dma_cnt = dma.cnt
        self.sw_sems = sw.sems if sw is not None else []
        self.sw_cnt = sw.cnt if sw is not None else []


class _Op:
    __slots__ = ('eng', 'fn', 'deps', 'dma', 'ev', 'waits', 'clock')


class Prog:
    def __init__(self, nc, pool):
        self.nc = nc
        self.pool = pool
        self.ops = []
        self.last_write = {}
        self.readers = {}
        self.slot_idx = {}
        self.sw_idx = {}

    def add(self, eng, fn, reads=(), writes=(), dma=None):
        op = _Op()
        op.eng, op.fn, op.dma = eng, fn, dma
        deps = set()
        for r in reads:
            w = self.last_write.get(r)
            if w is not None:
                deps.add(w)
        for r in writes:
            w = self.last_write.get(r)
            if w is not None:
                deps.add(w)
            for x in self.readers.get(r, ()):
                deps.add(x)
        idx = len(self.ops)
        op.deps = deps
        for r in reads:
            self.readers.setdefault(r, []).append(idx)
        for r in writes:
            self.last_write[r] = idx
            self.readers[r] = []
        self.ops.append(op)
        return idx

    def emit(self):
        nc, pool, ops = self.nc, self.pool, self.ops
        for op in ops:
            if op.dma is None:
                pool.eng_cnt[op.eng] += 1
                op.ev = (('E', op.eng), pool.eng_cnt[op.eng])
            elif op.eng == 'gpsimd':
                assert op.dma not in self.slot_idx
                if op.dma not in self.sw_idx:
                    self.sw_idx[op.dma] = len(self.sw_idx)
                    assert len(self.sw_idx) <= len(pool.sw_sems), "too many sw dma slots"
                si = self.sw_idx[op.dma]
                pool.sw_cnt[si] += 16
                op.ev = (('W', si), pool.sw_cnt[si])
            else:
                assert op.dma not in self.sw_idx
                if op.dma not in self.slot_idx:
                    self.slot_idx[op.dma] = len(self.slot_idx)
                    assert len(self.slot_idx) <= len(pool.dma_sems), "too many dma slots"
                si = self.slot_idx[op.dma]
                pool.dma_cnt[si] += 16
                op.ev = (('D', si), pool.dma_cnt[si])
        clock = {e: {} for e in ENGS}
        for op in ops:
            clk = clock[op.eng]
            waits = {}
            for d in sorted(op.deps, reverse=True):
                A = ops[d]
                if A.eng == 'tensor' and op.eng == 'tensor' and A.dma is None and op.dma is None:
                    continue
                k, v = A.ev
                if clk.get(k, 0) >= v:
                    continue
                waits[k] = max(waits.get(k, 0), v)
                for kk, vv in A.clock.items():
                    if clk.get(kk, 0) < vv:
                        clk[kk] = vv
            op.waits = list(waits.items())
            c = dict(clk)
            c[op.ev[0]] = op.ev[1]
            op.clock = c
        final = {}
        for op in ops:
            final[op.ev[0]] = max(final.get(op.ev[0], 0), op.ev[1])

        def sem_of(k):
            if k[0] == 'E':
                return pool.eng_sem[k[1]]
            return pool.dma_sems[k[1]] if k[0] == 'D' else pool.sw_sems[k[1]]

        by_eng = {e: [op for op in ops if op.eng == e] for e in ENGS}

        def body(ename):
            def f(eng):
                for op in by_eng[ename]:
                    for k, v in op.waits:
                        eng.wait_ge(sem_of(k), v)
                    ins = op.fn(eng)
                    ins.then_inc(sem_of(op.ev[0]), 16 if op.dma is not None else 1)
                clk = clock[ename]
                for k, v in final.items():
                    if clk.get(k, 0) < v:
                        eng.wait_ge(sem_of(k), v)
            return f

        with nc.Block() as block:
            block.sync(body('sync'))
            block.scalar(body('scalar'))
            block.vector(body('vector'))
            block.gpsimd(body('gpsimd'))
            block.tensor(body('tensor'))
        n = len(ops)
        nw = sum(len(op.waits) for op in ops)
        for op in ops:
            op.clock = None
        return n, nw


def mkap(t, off, dims):
    pst = t[:].ap[0][0]
    return bass.AP(t, off, [[pst, 128]] + [list(d) for d in dims])


def _t5_bucket(rel):
    nb = 16
    max_exact = 8
    ret = (rel > 0).astype(np.int32) * nb
    n = np.abs(rel).astype(np.int32)
    nf = np.maximum(n, 1).astype(np.float32)
    large = max_exact + (np.log(nf / np.float32(max_exact)) / np.float32(math.log(128 / max_exact))
                         * np.float32(nb - max_exact)).astype(np.int32)
    large = np.minimum(large, nb - 1)
    return ret + np.where(n < max_exact, n, large)


def _biasA_table(t5_table):
    kp = np.arange(128)[:, None, None]
    c = np.arange(3)[None, :, None]
    q = np.arange(128)[None, None, :]
    rel = (c - 1) * 128 + kp - q
    valid = np.abs(rel) <= 128
    b = t5_table[_t5_bucket(rel)]
    b = np.where(valid[..., None], b, np.float32(NEGM)).astype(np.float32)
    return np.ascontiguousarray(b.transpose(3, 0, 1, 2)).reshape(8, 128, 384)


def _na_geometry(T):
    rows = T // 64
    kh = min(8, rows)
    nt = T // 128

    def rs(r):
        return min(max(r - kh // 2, 0), rows - kh)

    def cs(c):
        return min(max(c - 8, 0), 64 - 16)

    kl = np.arange(128)
    krl, kc = kl // 64, kl % 64
    masks = []
    mask_ids = {}
    geo = []
    csq = np.array([cs(c) for c in range(64)])
    for t in range(nt):
        lo = min(rs(2 * t), rs(2 * t + 1))
        hi = max(rs(2 * t), rs(2 * t + 1)) + kh - 1
        lst = []
        for kt in range(lo // 2, hi // 2 + 1):
            krow = 2 * kt + krl[:, None]
            qrow = 2 * t + (kl // 64)[None, :]
            rsq = np.array([rs(2 * t), rs(2 * t + 1)])[(kl // 64)][None, :]
            qc = (kl % 64)[None, :]
            valid = (krow >= rsq) & (krow < rsq + kh) & (kc[:, None] >= csq[qc]) & (kc[:, None] < csq[qc] + 16)
            m = np.where(valid, np.float32(0.0), np.float32(NEGM)).astype(np.float32)
            key = m.tobytes()
            if key not in mask_ids:
                mask_ids[key] = len(masks)
                masks.append(m)
            lst.append((kt, kt - t, mask_ids[key]))
        geo.append(lst)
    return geo, np.stack(masks)


def _biasB_table(rpb_l):
    kl = np.arange(128)
    krl, kc = kl // 64, kl % 64
    out = np.zeros((8, 7, 128, 128), np.float32)
    for di, delta in enumerate(range(-3, 4)):
        dr = 2 * delta + krl[:, None] - krl[None, :] + 7
        dc = np.clip(kc[:, None] - kc[None, :], -15, 15) + 15
        ok = (dr >= 0) & (dr <= 14)
        g = rpb_l[:, np.clip(dr, 0, 14), dc]
        out[:, di] = np.where(ok[None], g, np.float32(0.0))
    return out


def emit_attn(nc, pool, T, tag, x_d, y_d, win_d, wo_d, bA_d, bB_d, mB_d, pbc_d, id_d, geo, NM, uv_src=None, uv_dst=None):
    NT = T // 128
    dbg_t = None
    with ExitStack() as st:
        def sb(name, shape, dt):
            return st.enter_context(nc.sbuf_tensor(name + tag, shape, dt))

        def ps(name, shape, dt):
            return st.enter_context(nc.psum_tensor(name + tag, shape, dt))

        identf = sb("identf", [128, 128], F32)
        ident = sb("ident_bf", [128, 128], BF16)
        pbc = sb("pbc_s", [128, 3 * D + 8], F32)
        sinkexp = sb("sinkexp", [128, 8], F32)
        biasA = sb("biasA_bf", [128, 8, 384], BF16)
        biasB = sb("biasB_bf", [128, 8, 7, 128], BF16)
        maskB = sb("maskB_bf", [128, NM, 128], BF16)
        win = sb("win_bf", [128, 8, PROJX], BF16)
        wo = sb("wo_bf", [128, 8, D], BF16)
        stg = [sb("stg%d" % i, [128, PROJX], F32) for i in range(2)]
        kring = sb("kring", [128, 6, RING * 128], BF16)
        vring = sb("vring", [128, RING, 10, 66], BF16)
        qring = sb("qring", [128, 4, 1024], BF16)
        xf = [sb("xf%d" % i, [128, D], F32) for i in range(2)]
        xb = sb("xb", [128, D], BF16)
        xT = sb("xT", [128, 8, 128], BF16)
        eA = [sb("eA%d" % i, [128, 384], BF16) for i in range(2)]
        eB = [sb("eB%d" % i, [128, 640], BF16) for i in range(2)]
        ya = sb("ya", [128, D], F32)
        sq = sb("sq", [128, D], F32)
        y16 = sb("y16", [128, D], BF16)
        yT = sb("yT", [128, 8, 128], BF16)
        xres = sb("xres", [128, D], F32)
        res = sb("res", [128, D], F32)
        outt = [sb("outt%d" % i, [128, D], F32) for i in range(2)]
        dn = sb("dn", [128, 16], F32)
        rd = sb("rd", [128, 16], F32)
        sm = sb("sm", [128, 16], F32)
        cvo = [sb("cvo%d" % i, [128, 2048], BF16) for i in range(2)]

        pT = ps("pT", [128, 1024], BF16)
        pg = [ps("pg%d" % i, [128, 512], F32) for i in range(2)]
        pS = [ps("pS%d" % i, [128, 512], F32) for i in range(3)]
        pP = [ps("pP%d" % i, [128, 512], F32) for i in range(2)]

        P = Prog(nc, pool)
        A = P.add
        cvt_rr = [0]
        if dbg_t is not None:
            dq_d = nc.dram_tensor("dbg_q", [128, 1024], BF16, kind="ExternalOutput").ap()
            dk_d = nc.dram_tensor("dbg_k", [128, 6, RING * 128], BF16, kind="ExternalOutput").ap()
            dv_d = nc.dram_tensor("dbg_v", [128, RING, 10, 66], BF16, kind="ExternalOutput").ap()
            dya_d = nc.dram_tensor("dbg_ya", [128, 1024], F32, kind="ExternalOutput").ap()
            dy16_d = nc.dram_tensor("dbg_y16", [128, 1024], BF16, kind="ExternalOutput").ap()
            dres_d = nc.dram_tensor("dbg_res", [128, 1024], F32, kind="ExternalOutput").ap()
            dxT_d = nc.dram_tensor("dbg_xT", [128, 8, 128], BF16, kind="ExternalOutput").ap()

        def convert(out_ap, in_ap, reads, writes):
            e = ['vector', 'gpsimd', 'scalar'][cvt_rr[0] % 3]
            cvt_rr[0] += 1
            if e == 'scalar':
                A(e, lambda g: g.copy(out=out_ap, in_=in_ap), reads, writes)
            else:
                A(e, lambda g: g.tensor_copy(out=out_ap, in_=in_ap), reads, writes)

        A('sync', lambda e: e.dma_start(out=identf[:], in_=id_d), writes=['identf'], dma='identf')
        A('vector', lambda e: e.tensor_copy(out=ident[:], in_=identf[:]), ['identf'], ['ident'])
        A('sync', lambda e: e.dma_start(out=pbc[:], in_=pbc_d), writes=['pbc'], dma='pbc')
        A('scalar', lambda e: e.activation(out=sinkexp[:], in_=pbc[:, 3 * D:3 * D + 8], func=AF.Exp), ['pbc'], ['sinkexp'])
        A('vector', lambda e: e.memset(vring[:], 1.0), [], [('v', i) for i in range(RING)])
        si = [0]

        def stage_load(dst_view, src_ap):
            s = si[0] % 2
            si[0] += 1
            A('sync', lambda e: e.dma_start(out=dst_view(stg[s]), in_=src_ap), writes=[('stg', s)], dma=('stg', s))
            return s

        for h in range(8):
            s = stage_load(lambda t: t[:, 0:384], bA_d[h])
            convert(biasA[:, h, :], stg[s][:, 0:384], [('stg', s)], ['biasA'])
        m0 = 0
        while m0 < NM:
            m1 = min(NM, m0 + 19)
            n = m1 - m0
            s = stage_load(lambda t, n=n: t[:, 0:n * 128].rearrange("k (m q) -> k m q", q=128),
                           mB_d[m0:m1].rearrange("m k q -> k m q"))
            convert(maskB[:, m0:m1, :], stg[s][:, 0:n * 128].rearrange("k (m q) -> k m q", q=128), [('stg', s)], ['maskB'])
            m0 = m1
        for h in range(8):
            s = stage_load(lambda t: t[:, 0:896].rearrange("k (m q) -> k m q", q=128),
                           bB_d[h].rearrange("m k q -> k m q"))
            convert(biasB[:, h, :, :], stg[s][:, 0:896].rearrange("k (m q) -> k m q", q=128), [('stg', s)], ['biasB'])
        for k in range(8):
            s = stage_load(lambda t: t[:], win_d[k * 128:(k + 1) * 128, :])
            convert(win[:, k, :], stg[s][:], [('stg', s)], ['win'])
        for k in range(8):
            s = stage_load(lambda t: t[:, 0:D], wo_d[k * 128:(k + 1) * 128, :])
            convert(wo[:, k, :], stg[s][:, 0:D], [('stg', s)], ['wo'])

        pgi = [0]

        def next_pg():
            i = pgi[0] % 2
            pgi[0] += 1
            return i

        projected = set()

        def project(t):
            projected.add(t)
            s = t % 2
            ks = t % RING
            qs = t % 4
            A('sync', lambda e: e.dma_start(out=xf[s][:], in_=x_d[t * 128:(t + 1) * 128, :]), writes=[('xf', s)], dma=('xf', s))
            A('vector', lambda e: e.tensor_copy(out=xb[:], in_=xf[s][:]), [('xf', s)], ['xb'])
            for k in range(8):
                A('tensor', lambda e, k=k: e.transpose(out=pT[:, k * 128:(k + 1) * 128], in_=xb[:, k * 128:(k + 1) * 128], identity=ident[:]),
                  ['xb', 'ident'], ['pT'])
            A('scalar', lambda e: e.copy(out=xT[:].rearrange("p a b -> p (a b)"), in_=pT[:]), ['pT'], ['xT'])
            if dbg_t == t:
                A('sync', lambda e: e.dma_start(out=dxT_d, in_=xT[:]), reads=['xT'], dma='dbg')
            groups = [
                ('q', [0, 128, 256, 384], 0),
                ('q', [768, 896, 1024, 1152], 4),
                ('kb', [1280, 1408, 1536, 1664], 1),
                ('ka', [512, 2304], 0),
            ]
            for gi, (kind, cols, j0) in enumerate(groups):
                b = next_pg()
                for j, col in enumerate(cols):
                    for k in range(8):
                        A('tensor', lambda e, b=b, j=j, col=col, k=k: e.matmul(
                            pg[b][:, j * 128:(j + 1) * 128], win[:, k, col:col + 128], xT[:, k, :],
                            start=(k == 0), stop=(k == 7)), ['xT', 'win'], [('pg', b)])
                if kind == 'q':
                    if gi == 0:
                        A('scalar', lambda e, b=b, j0=j0: e.mul(out=qring[:, qs, j0 * 128:(j0 + 4) * 128], in_=pg[b][:, 0:512], mul=0.125),
                          [('pg', b)], [('q', qs)])
                    else:
                        A('vector', lambda e, b=b, j0=j0: e.tensor_scalar(out=qring[:, qs, j0 * 128:(j0 + 4) * 128], in0=pg[b][:, 0:512],
                                                                          scalar1=0.125, scalar2=None, op0=ALU.mult),
                          [('pg', b)], [('q', qs)])
                elif kind == 'kb':
                    A('scalar', lambda e, b=b: e.copy(out=kring[:, 1:5, ks * 128:(ks + 1) * 128],
                                                      in_=pg[b][:, 0:512].rearrange("p (a b) -> p a b", b=128)),
                      [('pg', b)], [('k', ks)])
                else:
                    A('vector', lambda e, b=b: e.tensor_copy(out=kring[:, 0, ks * 128:(ks + 1) * 128], in_=pg[b][:, 0:128]),
                      [('pg', b)], [('k', ks)])
                    A('vector', lambda e, b=b: e.tensor_copy(out=kring[:, 5, ks * 128:(ks + 1) * 128], in_=pg[b][:, 128:256]),
                      [('pg', b)], [('k', ks)])
            for (col, n, h0, nh) in [(640, 128, 0, 2), (1792, 512, 2, 8)]:
                b = next_pg()
                for k in range(8):
                    A('tensor', lambda e, b=b, col=col, n=n, k=k: e.matmul(
                        pg[b][:, 0:n], xT[:, k, :], win[:, k, col:col + n], start=(k == 0), stop=(k == 7)),
                      ['xT', 'win'], [('pg', b)])
                eng = 'scalar' if nh == 8 else 'vector'
                if eng == 'scalar':
                    A('scalar', lambda e, b=b, n=n, h0=h0, nh=nh: e.copy(
                        out=vring[:, ks, h0:h0 + nh, 0:64], in_=pg[b][:, 0:n].rearrange("p (a b) -> p a b", b=64)),
                      [('pg', b)], [('v', ks)])
                else:
                    A('vector', lambda e, b=b, n=n, h0=h0, nh=nh: e.tensor_copy(
                        out=vring[:, ks, h0:h0 + nh, 0:64], in_=pg[b][:, 0:n].rearrange("p (a b) -> p a b", b=64)),
                      [('pg', b)], [('v', ks)])

        def attend(t):
            qs = t % 4
            blocks = [c for c in (0, 1, 2) if 0 <= t + c - 1 < NT]
            c0, c1 = blocks[0], blocks[-1]

            def scores_a(h, i):
                g = h // 4
                base = (h % 2) * 64
                kch = 0 if (h % 2) == g else 5
                for c in blocks:
                    ks = (t + c - 1) % RING
                    A('tensor', lambda e, c=c, ks=ks: e.matmul(
                        pS[i][:, c * 128:(c + 1) * 128], kring[base:base + 64, kch, ks * 128:(ks + 1) * 128],
                        qring[base:base + 64, qs, (h // 2) * 128:(h // 2 + 1) * 128], start=True, stop=False),
                      [('k', ks), ('q', qs)], [('S', i)])
                    A('tensor', lambda e, c=c: e.matmul(
                        pS[i][:, c * 128:(c + 1) * 128], ident[:], biasA[:, h, c * 128:(c + 1) * 128], start=False, stop=True),
                      ['ident', 'biasA'], [('S', i)])

            def rest_a(h, i):
                g = h // 4
                es = i
                A('scalar', lambda e: e.activation(out=eA[es][:, c0 * 128:(c1 + 1) * 128], in_=pS[i][:, c0 * 128:(c1 + 1) * 128], func=AF.Exp),
                  [('S', i)], [('eA', es)])
                r = h % 2
                for c in blocks:
                    ks = (t + c - 1) % RING
                    A('tensor', lambda e, c=c, ks=ks: e.matmul(
                        pP[r][:, 0:65], eA[es][:, c * 128:(c + 1) * 128], vring[:, ks, g, 0:65],
                        start=(c == c0), stop=(c == c1)), [('eA', es), ('v', ks)], [('pP', r)])
                A('vector', lambda e: e.tensor_scalar(out=dn[:, h:h + 1], in0=pP[r][:, 64:65], scalar1=sinkexp[:, h:h + 1],
                                                      scalar2=None, op0=ALU.add), [('pP', r), 'sinkexp'], [('dn', h)])
                A('vector', lambda e: e.reciprocal(out=rd[:, h:h + 1], in_=dn[:, h:h + 1]), [('dn', h)], [('rd', h)])
                A('vector', lambda e: e.tensor_scalar(out=ya[:, h * 64:(h + 1) * 64], in0=pP[r][:, 0:64], scalar1=rd[:, h:h + 1],
                                                      scalar2=None, op0=ALU.mult), [('pP', r), ('rd', h)], [('ya', 0)])

            scores_a(0, 0)
            for h in range(8):
                if h + 1 < 8:
                    scores_a(h + 1, (h + 1) % 2)
                rest_a(h, h % 2)
            kts = geo[t]
            nb = len(kts)
            assert all(kt in projected for kt, _, _ in kts) and nb <= 5

            def scores_b(h, i):
                base = (h % 2) * 64
                qch = 4 + h // 2
                kch = 1 + h // 2
                for bi, (kt, delta, mid) in enumerate(kts):
                    ks = kt % RING
                    if bi < 4:
                        reg = pS[i][:, bi * 128:(bi + 1) * 128]
                        wk = [('S', i)]
                    else:
                        reg = pS[2][:, i * 128:(i + 1) * 128]
                        wk = ['S2']
                    A('tensor', lambda e, reg=reg, ks=ks: e.matmul(
                        reg, kring[base:base + 64, kch, ks * 128:(ks + 1) * 128],
                        qring[base:base + 64, qs, qch * 128:(qch + 1) * 128], start=True, stop=False),
                      [('k', ks), ('q', qs)], wk)
                    A('tensor', lambda e, reg=reg, delta=delta: e.matmul(
                        reg, ident[:], biasB[:, h, delta + 3, :], start=False, stop=False), ['ident', 'biasB'], wk)
                    A('tensor', lambda e, reg=reg, mid=mid: e.matmul(
                        reg, ident[:], maskB[:, mid, :], start=False, stop=True), ['ident', 'maskB'], wk)

            def rest_b(h, i):
                es = i
                n4 = min(nb, 4)
                if nb > 4:
                    A('scalar', lambda e: e.activation(out=eB[es][:, 512:640], in_=pS[2][:, i * 128:(i + 1) * 128], func=AF.Exp),
                      ['S2'], [('eB', es)])
                A('scalar', lambda e: e.activation(out=eB[es][:, 0:n4 * 128], in_=pS[i][:, 0:n4 * 128], func=AF.Exp),
                  [('S', i)], [('eB', es)])
                r = h % 2
                for bi, (kt, delta, mid) in enumerate(kts):
                    ks = kt % RING
                    A('tensor', lambda e, bi=bi, ks=ks: e.matmul(
                        pP[r][:, 0:65], eB[es][:, bi * 128:(bi + 1) * 128], vring[:, ks, 2 + h, 0:65],
                        start=(bi == 0), stop=(bi == nb - 1)), [('eB', es), ('v', ks)], [('pP', r)])
                A('vector', lambda e: e.reciprocal(out=rd[:, 8 + h:9 + h], in_=pP[r][:, 64:65]), [('pP', r)], [('rd', 8 + h)])
                A('vector', lambda e: e.tensor_scalar(out=ya[:, 512 + h * 64:512 + (h + 1) * 64], in0=pP[r][:, 0:64],
                                                      scalar1=rd[:, 8 + h:9 + h], scalar2=None, op0=ALU.mult),
                  [('pP', r), ('rd', 8 + h)], [('ya', 1)])

            scores_b(0, 0)
            for h in range(8):
                if h + 1 < 8:
                    scores_b(h + 1, (h + 1) % 2)
                rest_b(h, h % 2)
            if dbg_t == t:
                A('sync', lambda e: e.dma_start(out=dq_d, in_=qring[:, qs, :]), reads=[('q', qs)], dma='dbg')
                A('sync', lambda e: e.dma_start(out=dk_d, in_=kring[:]), reads=[('k', i) for i in range(RING)], dma='dbg')
                A('sync', lambda e: e.dma_start(out=dv_d, in_=vring[:]), reads=[('v', i) for i in range(RING)], dma='dbg')
                A('sync', lambda e: e.dma_start(out=dya_d, in_=ya[:]), reads=[('ya', 0), ('ya', 1)], dma='dbg')
            A('vector', lambda e: e.tensor_tensor(out=sq[:], in0=ya[:], in1=ya[:], op=ALU.mult), [('ya', 0), ('ya', 1)], ['sq'])
            A('vector', lambda e: e.tensor_reduce(out=sm[:, 0:2], in_=sq[:].rearrange("p (a b) -> p a b", b=512), axis=AX.X, op=ALU.add),
              ['sq'], ['sm01'])
            A('vector', lambda e: e.tensor_scalar(out=sm[:, 2:4], in0=sm[:, 0:2], scalar1=1.0 / 512, scalar2=LN_EPS, op0=ALU.mult, op1=ALU.add),
              ['sm01'], ['sm23'])
            A('scalar', lambda e: e.activation(out=sm[:, 4:6], in_=sm[:, 2:4], func=AF.Sqrt), ['sm23'], ['sm45'])
            A('vector', lambda e: e.reciprocal(out=sm[:, 6:8], in_=sm[:, 4:6]), ['sm45'], ['sm67'])
            for j in range(2):
                A('vector', lambda e, j=j: e.scalar_tensor_tensor(out=y16[:, j * 512:(j + 1) * 512], in0=ya[:, j * 512:(j + 1) * 512],
                                                                  scalar=sm[:, 6 + j:7 + j], in1=pbc[:, j * 512:(j + 1) * 512],
                                                                  op0=ALU.mult, op1=ALU.mult),
                  [('ya', j), 'sm67', 'pbc'], ['y16'])
            for k in range(8):
                A('tensor', lambda e, k=k: e.transpose(out=pT[:, k * 128:(k + 1) * 128], in_=y16[:, k * 128:(k + 1) * 128], identity=ident[:]),
                  ['y16', 'ident'], ['pT'])
            A('scalar', lambda e: e.copy(out=yT[:].rearrange("p a b -> p (a b)"), in_=pT[:]), ['pT'], ['yT'])
            for n in range(2):
                for k in range(8):
                    A('tensor', lambda e, n=n, k=k: e.matmul(pg[n][:, 0:512], yT[:, k, :], wo[:, k, n * 512:(n + 1) * 512],
                                                             start=(k == 0), stop=(k == 7)), ['yT', 'wo'], [('pg', n)])
            A('sync', lambda e: e.dma_start(out=xres[:], in_=x_d[t * 128:(t + 1) * 128, :]), writes=['xres'], dma='xres')
            for n in range(2):
                A('vector', lambda e, n=n: e.scalar_tensor_tensor(out=res[:, n * 512:(n + 1) * 512], in0=xres[:, n * 512:(n + 1) * 512],
                                                                  scalar=float(ALPHA), in1=pg[n][:, 0:512], op0=ALU.mult, op1=ALU.add),
                  ['xres', ('pg', n)], ['res'])
            if dbg_t == t:
                A('sync', lambda e: e.dma_start(out=dy16_d, in_=y16[:]), reads=['y16'], dma='dbg')
                A('sync', lambda e: e.dma_start(out=dres_d, in_=res[:]), reads=['res'], dma='dbg')
            layer_norm(A, res, sq, sm, pbc, D, 2 * D, outt[t % 2], ('outt', t % 2))
            A('sync', lambda e: e.dma_start(out=y_d[t * 128:(t + 1) * 128, :], in_=outt[t % 2][:]), reads=[('outt', t % 2)], dma=('outt', t % 2))

        ccnt = [0]

        def convert_rows(n):
            if uv_src is None:
                return
            srcv = uv_src.rearrange("(p r) c -> p r c", p=128)
            dstv = uv_dst.rearrange("(p r) c -> p r c", p=128)
            for _ in range(n):
                c = ccnt[0]
                if c >= 128:
                    return
                ccnt[0] += 1
                s = c % 2
                A('gpsimd', lambda e, c=c, s=s: e.dma_start(out=stg[s][:, 0:2048], in_=srcv[:, c, :]), writes=[('stg', s)], dma=('cst', s))
                A('gpsimd', lambda e, s=s: e.tensor_copy(out=cvo[s][:], in_=stg[s][:, 0:2048]), [('stg', s)], [('cvo', s)])
                A('gpsimd', lambda e, c=c, s=s: e.dma_start(out=dstv[:, c, :], in_=cvo[s][:]), reads=[('cvo', s)], dma=('cvo', s))

        LAG = 3
        per = -(-128 // NT)
        for t in range(NT):
            project(t)
            convert_rows(per)
            if t >= LAG:
                attend(t - LAG)
        for t in range(max(NT - LAG, 0), NT):
            attend(t)
        convert_rows(128)
        stats = P.emit()
    return stats


def layer_norm(A, res, sq, sm, pbc, goff, boff, out_t, out_key):
    A('vector', lambda e: e.tensor_reduce(out=sm[:, 8:9], in_=res[:], axis=AX.X, op=ALU.add), ['res'], ['ln_s'])
    A('vector', lambda e: e.tensor_scalar(out=sm[:, 9:10], in0=sm[:, 8:9], scalar1=1.0 / D, scalar2=None, op0=ALU.mult), ['ln_s'], ['ln_m'])
    A('vector', lambda e: e.tensor_scalar(out=res[:], in0=res[:], scalar1=sm[:, 9:10], scalar2=None, op0=ALU.subtract), ['res', 'ln_m'], ['res'])
    A('vector', lambda e: e.tensor_tensor(out=sq[:], in0=res[:], in1=res[:], op=ALU.mult), ['res'], ['sq'])
    A('vector', lambda e: e.tensor_reduce(out=sm[:, 10:11], in_=sq[:], axis=AX.X, op=ALU.add), ['sq'], ['ln_v'])
    A('vector', lambda e: e.tensor_scalar(out=sm[:, 11:12], in0=sm[:, 10:11], scalar1=1.0 / D, scalar2=LN_EPS, op0=ALU.mult, op1=ALU.add),
      ['ln_v'], ['ln_ve'])
    A('scalar', lambda e: e.activation(out=sm[:, 12:13], in_=sm[:, 11:12], func=AF.Sqrt), ['ln_ve'], ['ln_sd'])
    A('vector', lambda e: e.reciprocal(out=sm[:, 13:14], in_=sm[:, 12:13]), ['ln_sd'], ['ln_r'])
    A('vector', lambda e: e.scalar_tensor_tensor(out=res[:], in0=res[:], scalar=sm[:, 13:14], in1=pbc[:, goff:goff + D], op0=ALU.mult, op1=ALU.mult),
      ['res', 'ln_r', 'pbc'], ['res'])
    A('gpsimd', lambda e: e.tensor_tensor(out=out_t[:], in0=res[:], in1=pbc[:, boff:boff + D], op=ALU.add), ['res', 'pbc'], [out_key])


NS = 12


def emit_peer(nc, pool, T, tag, x_d, y_d, wq_d, keys_d, uv_d, pbc_d, id_d, io_d, e_off):
    NT = T // 128
    with ExitStack() as st:
        def sb(name, shape, dt):
            return st.enter_context(nc.sbuf_tensor(name + tag, shape, dt))

        def ps(name, shape, dt):
            return st.enter_context(nc.psum_tensor(name + tag, shape, dt))

        identf = sb("identf", [128, 128], F32)
        ident = sb("ident_bf", [128, 128], BF16)
        iota = sb("iota", [128, 16], F32)
        pbc = sb("pbc_s", [128, 2 * D], F32)
        wq = sb("wq_bf", [128, 8, 2048], BF16)
        keysb = sb("keys_bf", [128, 16, 128], BF16)
        keysT = sb("keysT", [128, 16, 128], BF16)
        stg = [sb("stg%d" % i, [128, 2048], F32) for i in range(2)]
        xf = [sb("xf%d" % i, [128, D], F32) for i in range(2)]
        xb = sb("xb", [128, D], BF16)
        xT = sb("xT", [128, 8, 128], BF16)
        qT = sb("qT", [128, 16, 128], BF16)
        sc = sb("sc", [128, 2048], F32)
        tmp = sb("tmp", [128, 256], F32)
        stop_ = sb("s_top", [128, 256], F32)
        itop = sb("i_top", [128, 256], U32)
        itopf = sb("i_topf", [128, 256], F32)
        cand = sb("cand", [128, 2048], F32)
        oh = sb("oh", [128, 2048], F32)
        fs = sb("fs", [128, 128], F32)
        fpos = sb("fpos", [128, 128], U32)
        abu = sb("abu", [128, 256], U32)
        abf = sb("abf", [128, 256], F32)
        isel = sb("isel", [128, 256], F32)
        ef = sb("ef", [128, 128], F32)
        eidx = [sb("eidx%d" % i, [128, 128], I32) for i in range(2)]
        gate = [sb("gate%d" % i, [128, 128], F32) for i in range(2)]
        ge = sb("ge", [128, 128], F32)
        gs = sb("gs", [128, 16], F32)
        hraw = sb("hraw", [128, 128], F32)
        aa = sb("aa", [128, 128], F32)
        uv = sb("uvring", [128, NS, 2048], BF16)
        diag = [sb("diag%d" % i, [128, 4, 128], BF16) for i in range(2)]
        junk = sb("junk", [128, D], BF16)
        res = sb("res", [128, D], F32)
        sq = sb("sq", [128, D], F32)
        sm = sb("sm", [128, 16], F32)
        outt = [sb("outt%d" % i, [128, D], F32) for i in range(2)]

        pT = ps("pT", [128, 1024], BF16)
        pg = [ps("pg%d" % i, [128, 512], F32) for i in range(2)]
        pS = [ps("pS%d" % i, [128, 512], F32) for i in range(2)]
        pacc = [ps("pacc%d" % i, [128, 512], F32) for i in range(2)]

        P = Prog(nc, pool)
        A = P.add
        cvt_rr = [0]

        def convert(out_ap, in_ap, reads, writes):
            e = ['vector', 'gpsimd', 'scalar'][cvt_rr[0] % 3]
            cvt_rr[0] += 1
            if e == 'scalar':
                A(e, lambda g: g.copy(out=out_ap, in_=in_ap), reads, writes)
            else:
                A(e, lambda g: g.tensor_copy(out=out_ap, in_=in_ap), reads, writes)

        A('sync', lambda e: e.dma_start(out=identf[:], in_=id_d), writes=['identf'], dma='identf')
        A('vector', lambda e: e.tensor_copy(out=ident[:], in_=identf[:]), ['identf'], ['ident'])
        A('sync', lambda e: e.dma_start(out=iota[:], in_=io_d), writes=['iota'], dma='iota')
        A('sync', lambda e: e.dma_start(out=pbc[:], in_=pbc_d), writes=['pbc'], dma='pbc')
        for k in range(8):
            s = k % 2
            A('sync', lambda e, s=s, k=k: e.dma_start(out=stg[s][:], in_=wq_d[k * 128:(k + 1) * 128, :]), writes=[('stg', s)], dma=('stg', s))
            convert(wq[:, k, :], stg[s][:], [('stg', s)], ['wq'])
        A('sync', lambda e: e.dma_start(out=stg[0][:].rearrange("k (m c) -> k m c", c=128), in_=keys_d.rearrange("m k c -> k m c")),
          writes=[('stg', 0)], dma=('stg', 0))
        A('vector', lambda e: e.tensor_copy(out=keysb[:].rearrange("p a b -> p (a b)"), in_=stg[0][:]), [('stg', 0)], ['keysb'])
        for half in range(2):
            for j in range(8):
                m = half * 8 + j
                A('tensor', lambda e, m=m, j=j: e.transpose(out=pT[:, j * 128:(j + 1) * 128], in_=keysb[:, m, :], identity=ident[:]),
                  ['keysb', 'ident'], ['pT'])
            A('scalar', lambda e, half=half: e.copy(out=keysT[:, half * 8:(half + 1) * 8, :].rearrange("p a b -> p (a b)"), in_=pT[:]),
              ['pT'], ['keysT'])

        def top16(seg, n, vals, idxs, o, use_tmp):
            A('vector', lambda e: e.max(out=vals[:, o:o + 8], in_=seg), ['segsrc'], ['tk_v0'])
            A('vector', lambda e: e.match_replace(out=use_tmp[:, 0:n], in_to_replace=vals[:, o:o + 8], in_values=seg, imm_value=-1e30),
              ['segsrc', 'tk_v0'], ['tk_tmp'])
            A('vector', lambda e: e.max(out=vals[:, o + 8:o + 16], in_=use_tmp[:, 0:n]), ['tk_tmp'], ['tk_v1'])
            A('vector', lambda e: e.max_index(out=idxs[:, o:o + 8], in_max=vals[:, o:o + 8], in_values=seg), ['segsrc', 'tk_v0'], ['tk_i'])
            A('vector', lambda e: e.max_index(out=idxs[:, o + 8:o + 16], in_max=vals[:, o + 8:o + 16], in_values=seg), ['segsrc', 'tk_v1'], ['tk_i'])

        hkc = [0]

        def route(t):
            s = t % 2
            A('sync', lambda e: e.dma_start(out=xf[s][:], in_=x_d[t * 128:(t + 1) * 128, :]), writes=[('xf', s)], dma=('xf', s))
            A('gpsimd', lambda e: e.tensor_copy(out=xb[:], in_=xf[s][:]), [('xf', s)], ['xb'])
            for k in range(8):
                A('tensor', lambda e, k=k: e.transpose(out=pT[:, k * 128:(k + 1) * 128], in_=xb[:, k * 128:(k + 1) * 128], identity=ident[:]),
                  ['xb', 'ident'], ['pT'])
            A('scalar', lambda e: e.copy(out=xT[:].rearrange("p a b -> p (a b)"), in_=pT[:]), ['pT'], ['xT'])
            yield
            for g4 in range(4):
                b = g4 % 2
                for j in range(4):
                    m = g4 * 4 + j
                    for k in range(8):
                        A('tensor', lambda e, b=b, j=j, m=m, k=k: e.matmul(
                            pg[b][:, j * 128:(j + 1) * 128], wq[:, k, m * 128:(m + 1) * 128], xT[:, k, :], start=(k == 0), stop=(k == 7)),
                          ['xT', 'wq'], [('pg', b)])
                A('scalar', lambda e, b=b, g4=g4: e.copy(out=qT[:, g4 * 4:(g4 + 1) * 4, :].rearrange("p a b -> p (a b)"), in_=pg[b][:, 0:512]),
                  [('pg', b)], [('qT', g4)])
                yield
            for g4 in range(4):
                b = g4 % 2
                for j in range(4):
                    m = g4 * 4 + j
                    A('tensor', lambda e, b=b, j=j, m=m: e.matmul(pS[b][:, j * 128:(j + 1) * 128], qT[:, m, :], keysT[:, m, :], start=True, stop=True),
                      [('qT', g4), 'keysT'], [('pS', b)])
                A('scalar', lambda e, b=b, g4=g4: e.copy(out=sc[:, g4 * 512:(g4 + 1) * 512], in_=pS[b][:, 0:512]), [('pS', b)], ['segsrc'])
                yield
            for m in range(16):
                top16(sc[:, m * 128:(m + 1) * 128], 128, stop_, itop, m * 16, tmp)
                yield
            A('vector', lambda e: e.tensor_copy(out=itopf[:], in_=itop[:]), ['tk_i', 'tk_v0', 'tk_v1'], ['itopf'])
            A('vector', lambda e: e.tensor_tensor(out=mkap(cand, 0, [[256, 8], [16, 16], [1, 16]]),
                                                  in0=mkap(stop_, 0, [[32, 8], [1, 16], [0, 16]]),
                                                  in1=mkap(stop_, 16, [[32, 8], [0, 16], [1, 16]]), op=ALU.add),
              ['tk_v0', 'tk_v1', 'itopf'], ['segsrc', 'cand'])
            yield
            for h in range(8):
                top16(cand[:, h * 256:(h + 1) * 256], 256, fs, fpos, h * 16, tmp)
                yield
            A('vector', lambda e: e.tensor_scalar(out=abu[:, 0:128], in0=fpos[:], scalar1=4, scalar2=None, op0=ALU.logical_shift_right),
              ['tk_i', 'tk_v0', 'tk_v1'], ['abu0'])
            A('vector', lambda e: e.tensor_scalar(out=abu[:, 128:256], in0=fpos[:], scalar1=15, scalar2=None, op0=ALU.bitwise_and),
              ['tk_i'], ['abu1'])
            A('vector', lambda e: e.tensor_copy(out=abf[:], in_=abu[:]), ['abu0', 'abu1'], ['abf'])
            yield
            for p in range(2):
                A('vector', lambda e, p=p: e.tensor_tensor(out=mkap(oh, 0, [[256, 8], [16, 16], [1, 16]]),
                                                           in0=mkap(abf, p * 128, [[16, 8], [1, 16], [0, 16]]),
                                                           in1=mkap(iota, 0, [[0, 8], [0, 16], [1, 16]]), op=ALU.is_equal),
                  ['abf', 'iota'], ['oh'])
                A('vector', lambda e, p=p: e.tensor_tensor(out=mkap(oh, 0, [[256, 8], [16, 16], [1, 16]]),
                                                           in0=mkap(oh, 0, [[256, 8], [16, 16], [1, 16]]),
                                                           in1=mkap(itopf, p * 16, [[32, 8], [0, 16], [1, 16]]), op=ALU.mult),
                  ['oh', 'itopf'], ['oh'])
                A('vector', lambda e, p=p: e.tensor_reduce(out=mkap(isel, p * 128, [[16, 8], [1, 16]]),
                                                           in_=mkap(oh, 0, [[256, 8], [16, 16], [1, 16]]), axis=AX.X, op=ALU.add),
                  ['oh'], [('isel', p)])
                yield
            A('vector', lambda e: e.scalar_tensor_tensor(out=ef[:], in0=isel[:, 0:128], scalar=128.0, in1=isel[:, 128:256], op0=ALU.mult, op1=ALU.add),
              [('isel', 0), ('isel', 1)], ['ef'])
            if e_off:
                A('vector', lambda e: e.tensor_scalar(out=ef[:], in0=ef[:], scalar1=float(e_off), scalar2=None, op0=ALU.add), ['ef'], ['ef'])
            A('vector', lambda e: e.tensor_copy(out=eidx[s][:], in_=ef[:]), ['ef'], [('eidx', s)])
            yield
            A('vector', lambda e: e.tensor_tensor(out=mkap(ge, 0, [[16, 8], [1, 16]]), in0=mkap(fs, 0, [[16, 8], [1, 16]]),
                                                  in1=mkap(fs, 0, [[16, 8], [0, 16]]), op=ALU.subtract), ['tk_v0', 'tk_v1', 'abu0'], ['ge'])
            A('scalar', lambda e: e.activation(out=ge[:], in_=ge[:], func=AF.Exp), ['ge'], ['ge'])
            A('vector', lambda e: e.tensor_reduce(out=gs[:, 0:8], in_=mkap(ge, 0, [[16, 8], [1, 16]]), axis=AX.X, op=ALU.add), ['ge'], ['gs0'])
            A('vector', lambda e: e.reciprocal(out=gs[:, 8:16], in_=gs[:, 0:8]), ['gs0'], ['gs1'])
            A('vector', lambda e: e.tensor_tensor(out=mkap(gate[s], 0, [[16, 8], [1, 16]]), in0=mkap(ge, 0, [[16, 8], [1, 16]]),
                                                  in1=mkap(gs, 8, [[1, 8], [0, 16]]), op=ALU.mult), ['ge', 'gs1'], [('gate', s)])

        SKIP_DVE = os.environ.get("PEER_SKIP_DVE") == "1"
        SKIP_DMA = os.environ.get("PEER_SKIP_DMA") == "1"

        def drain(gen, n):
            for _ in range(n):
                try:
                    next(gen)
                except StopIteration:
                    return

        def experts(t, gen):
            s = t % 2
            for g in range(32):
                drain(gen, 2)
                slots = []
                for j in range(4):
                    hk = g * 4 + j
                    sl = hkc[0] % NS
                    hkc[0] += 1
                    slots.append(sl)
                    if not SKIP_DMA:
                        A('gpsimd', lambda e, hk=hk, sl=sl: e.indirect_dma_start(
                            out=uv[:, sl, :], out_offset=None, in_=uv_d,
                            in_offset=bass.IndirectOffsetOnAxis(ap=eidx[s][:, hk:hk + 1], axis=0)),
                          reads=[('eidx', s)], writes=[('uv', sl)], dma=('uv', sl))
                if SKIP_DVE:
                    continue
                for j in range(4):
                    hk = g * 4 + j
                    sl = slots[j]
                    A('vector', lambda e, hk=hk, sl=sl: e.scalar_tensor_tensor(
                        out=junk[:], in0=uv[:, sl, 0:D], scalar=1.0, in1=xf[s][:], op0=ALU.mult, op1=ALU.mult,
                        accum_out=hraw[:, hk:hk + 1]), [('uv', sl), ('xf', s)], [('hraw', hk)])
                A('scalar', lambda e, g=g: e.activation(out=aa[:, g * 4:(g + 1) * 4], in_=hraw[:, g * 4:(g + 1) * 4], func=AF.Gelu),
                  [('hraw', g * 4 + j) for j in range(4)], [('aa', g)])
                A('vector', lambda e, g=g: e.tensor_tensor(out=aa[:, g * 4:(g + 1) * 4], in0=aa[:, g * 4:(g + 1) * 4],
                                                           in1=gate[s][:, g * 4:(g + 1) * 4], op=ALU.mult),
                  [('aa', g), ('gate', s)], [('aa', g)])
                d = g % 2
                for j in range(4):
                    A('scalar', lambda e, g=g, d=d, j=j: e.activation(out=diag[d][:, j, :], in_=identf[:], func=AF.Copy,
                                                                      scale=aa[:, g * 4 + j:g * 4 + j + 1]),
                      [('aa', g), 'identf'], [('diag', d, j)])
                for j in range(4):
                    hk = g * 4 + j
                    sl = slots[j]
                    for n in range(2):
                        A('tensor', lambda e, hk=hk, sl=sl, j=j, n=n, d=d: e.matmul(
                            pacc[n][:, 0:512], diag[d][:, j, :], uv[:, sl, D + n * 512:D + (n + 1) * 512],
                            start=(hk == 0), stop=(hk == 127)), [('diag', d, j), ('uv', sl)], [('pacc', n)])
            for n in range(2):
                A('vector', lambda e, n=n: e.scalar_tensor_tensor(out=res[:, n * 512:(n + 1) * 512], in0=xf[s][:, n * 512:(n + 1) * 512],
                                                                  scalar=float(ALPHA), in1=pacc[n][:, 0:512], op0=ALU.mult, op1=ALU.add),
                  [('xf', s), ('pacc', n)], ['res'])
            layer_norm(A, res, sq, sm, pbc, 0, D, outt[s], ('outt', s))
            A('sync', lambda e: e.dma_start(out=y_d[t * 128:(t + 1) * 128, :], in_=outt[s][:]), reads=[('outt', s)], dma=('outt', s))

        for _ in route(0):
            pass
        for t in range(NT):
            gen = route(t + 1) if t + 1 < NT else iter(())
            experts(t, gen)
            for _ in gen:
                pass
        stats = P.emit()
    return stats


_CACHE = {}


def build_fused(T, depth):
    geo, masks = _na_geometry(T)
    NM = masks.shape[0]
    nc = bass.Bass("TRN2", target_bir_lowering=False)
    x_d = nc.dram_tensor("x", [T, D], F32, kind="ExternalInput").ap()
    win_d = nc.dram_tensor("win", [depth, D, PROJX], F32, kind="ExternalInput").ap()
    wo_d = nc.dram_tensor("wo", [depth, D, D], F32, kind="ExternalInput").ap()
    bA_d = nc.dram_tensor("biasA", [8, 128, 384], F32, kind="ExternalInput").ap()
    bB_d = nc.dram_tensor("biasB", [depth, 8, 7, 128, 128], F32, kind="ExternalInput").ap()
    mB_d = nc.dram_tensor("maskB", [NM, 128, 128], F32, kind="ExternalInput").ap()
    pa_d = nc.dram_tensor("pbcA", [depth, 128, 3 * D + 8], F32, kind="ExternalInput").ap()
    pp_d = nc.dram_tensor("pbcP", [depth, 128, 2 * D], F32, kind="ExternalInput").ap()
    id_d = nc.dram_tensor("ident", [128, 128], F32, kind="ExternalInput").ap()
    io_d = nc.dram_tensor("iota16", [128, 16], F32, kind="ExternalInput").ap()
    wq_d = nc.dram_tensor("wq", [depth, D, 2048], F32, kind="ExternalInput").ap()
    keys_d = nc.dram_tensor("keys", [depth, 16, 128, 128], F32, kind="ExternalInput").ap()
    uv_d = nc.dram_tensor("uv", [depth * 16384, 2048], F32, kind="ExternalInput").ap()
    y_d = nc.dram_tensor("y", [T, D], F32, kind="ExternalOutput").ap()
    scr_a = nc.dram_tensor("scr_a", [T, D], F32).ap()
    scr_b = nc.dram_tensor("scr_b", [T, D], F32).ap()
    uvb_d = nc.dram_tensor("uv_bf16", [16384, 2048], BF16).ap()
    stats = []
    with ExitStack() as gs:
        dmas = [DmaSems(nc, gs, 12, "x"), DmaSems(nc, gs, 12, "y")]
        swd = DmaSems(nc, gs, 16, "w")
        cur = x_d
        for l in range(depth):
            dma = dmas[0] if l < 2 else dmas[1]
            pool = SemPool(nc, gs, dma, "a%d" % l, sw=swd)
            stats.append(emit_attn(nc, pool, T, "_a%d" % l, cur, scr_a, win_d[l], wo_d[l], bA_d, bB_d[l], mB_d, pa_d[l], id_d, geo, NM,
                                   uv_src=uv_d[l * 16384:(l + 1) * 16384, :], uv_dst=uvb_d))
            pool = SemPool(nc, gs, dma, "p%d" % l, sw=swd)
            dst = y_d if l == depth - 1 else scr_b
            stats.append(emit_peer(nc, pool, T, "_p%d" % l, scr_a, dst, wq_d[l], keys_d[l], uvb_d, pp_d[l], id_d, io_d, 0))
            cur = scr_b
    return nc, stats, masks


def host_inputs(depth, w_in, w_o, attn_sink, na_rpb, t5_table, gnorm_a, gnorm_b,
                ln1_g, ln1_b, ln2_g, ln2_b, peer_wq, peer_keys, peer_u, peer_v, masks):
    f = np.float32
    w_in = np.asarray(w_in, f)[:depth]
    win_ext = np.ascontiguousarray(np.concatenate([w_in, w_in[:, :, 576:640], w_in[:, :, 512:576]], axis=2))
    pa = np.concatenate([np.asarray(gnorm_a, f)[:depth], np.asarray(gnorm_b, f)[:depth], np.asarray(ln1_g, f)[:depth],
                         np.asarray(ln1_b, f)[:depth], np.asarray(attn_sink, f)[:depth]], axis=1)
    pa = np.ascontiguousarray(np.broadcast_to(pa[:, None, :], (depth, 128, pa.shape[1])))
    pp = np.concatenate([np.asarray(ln2_g, f)[:depth], np.asarray(ln2_b, f)[:depth]], axis=1)
    pp = np.ascontiguousarray(np.broadcast_to(pp[:, None, :], (depth, 128, pp.shape[1])))
    uv = np.ascontiguousarray(np.concatenate([np.asarray(peer_u, f)[:depth], np.asarray(peer_v, f)[:depth]], axis=2)).reshape(depth * 16384, 2048)
    return {
        "win": win_ext, "wo": np.ascontiguousarray(np.asarray(w_o, f)[:depth]),
        "biasA": _biasA_table(np.asarray(t5_table, f)),
        "biasB": np.stack([_biasB_table(np.asarray(na_rpb[l], f)) for l in range(depth)]),
        "maskB": masks, "pbcA": pa, "pbcP": pp,
        "ident": np.eye(128, dtype=f),
        "iota16": np.ascontiguousarray(np.broadcast_to(np.arange(16, dtype=f), (128, 16))),
        "wq": np.ascontiguousarray(np.asarray(peer_wq, f)[:depth]),
        "keys": np.ascontiguousarray(np.asarray(peer_keys, f)[:depth].reshape(depth, 16, 128, 128)),
        "uv": uv,
    }


def kernel(x, w_in, w_o, attn_sink, na_rpb, t5_table, gnorm_a, gnorm_b,
           ln1_g, ln1_b, ln2_g, ln2_b, peer_wq, peer_keys, peer_u, peer_v):
    x = np.asarray(x, np.float32)
    B, T, _ = x.shape
    key = (T, DEPTH)
    if key not in _CACHE:
        _CACHE[key] = build_fused(T, DEPTH)
    nc, _, masks = _CACHE[key]
    shared = host_inputs(DEPTH, w_in, w_o, attn_sink, na_rpb, t5_table, gnorm_a, gnorm_b,
                         ln1_g, ln1_b, ln2_g, ln2_b, peer_wq, peer_keys, peer_u, peer_v, masks)
    in_maps = [dict(shared, x=np.ascontiguousarray(x[b])) for b in range(B)]
    r = run_bass_kernel_spmd(nc, in_maps, core_ids=list(range(B)))
    return np.stack([np.asarray(r.results[b]["y"], np.float32) for b in range(B)], axis=0)
```
